# Optimizing a Trainium2 kernel written in Bass

```python
import jax, jax.numpy as jnp
from jax import lax
import numpy as np

D_MODEL = 1024
BATCH = 8
SEQ = 2048
DEPTH = 4

CHUNK = 64
QBLOCK = 128
ROPE_THETA = 10000.0
EPS = 1e-6
N_MIXERS = 3
N_A = (DEPTH + 2) // 3
N_B = (DEPTH + 1) // 3
N_C = DEPTH // 3

A_WIDTH = D_MODEL
CONV_WIDTH = 3

B_HEADS = 16
B_HEAD_DIM = 64
B_WIDTH = B_HEADS * B_HEAD_DIM
IDX_HEADS = 8
IDX_DIM = 64
IDX_ROPE_DIM = 32
TOPK_MAX = 256
B_IN_COLS = 4 * B_WIDTH + IDX_HEADS * IDX_DIM + IDX_DIM + IDX_HEADS

C_HEADS = 16
C_Q_LORA = 384
C_KV_LORA = 256
C_NOPE = 64
C_ROPE = 32
C_V = 64
C_QK = C_NOPE + C_ROPE
C_WIDTH = C_HEADS * C_V
C_IN_COLS = C_Q_LORA + C_KV_LORA + C_ROPE + C_WIDTH

kernel_name = "hybrid_conv_dsa_mla_stream_encoder"


def rmsnorm(x, g):
    xf = x.astype(jnp.float32)
    y = xf * lax.rsqrt(jnp.mean(xf * xf, axis=-1, keepdims=True) + EPS)
    return (y * g.astype(jnp.float32)).astype(x.dtype)


def rope(x, pos):
    d = x.shape[-1]
    half = d // 2
    inv = ROPE_THETA ** (-jnp.arange(half, dtype=jnp.float32) / half)
    ang = pos.astype(jnp.float32)[..., None] * inv
    cos = jnp.cos(ang)[:, :, None, :]
    sin = jnp.sin(ang)[:, :, None, :]
    xf = x.astype(jnp.float32)
    x1, x2 = xf[..., :half], xf[..., half:]
    out = jnp.concatenate([x1 * cos - x2 * sin, x2 * cos + x1 * sin], axis=-1)
    return out.astype(x.dtype)


def chunk_visible(q_pos, k_pos):
    return (k_pos // CHUNK)[None, :] <= (q_pos // CHUNK)[:, None]


def short_conv_mixer(xn, w_in, conv_w, conv_b, w_out):
    bg, cg, hv, z = jnp.split(xn @ w_in, 4, axis=-1)
    u = cg * hv
    y = lax.conv_general_dilated(
        u, conv_w[:, None, :].astype(u.dtype), window_strides=(1,),
        padding=[(CONV_WIDTH - 1, 0)], dimension_numbers=("NWC", "WIO", "NWC"),
        feature_group_count=A_WIDTH) + conv_b
    return (bg * y * jax.nn.silu(z)) @ w_out


def dsa_mixer(xn, positions, w_in, q_norm, k_norm, w_out):
    bsz, seq, _ = xn.shape
    cuts = np.cumsum([B_WIDTH, B_WIDTH, B_WIDTH, B_WIDTH, IDX_HEADS * IDX_DIM, IDX_DIM]).tolist()
    q, k, v, z, qi, ki, wi = jnp.split(xn @ w_in, cuts, axis=-1)
    q = rope(rmsnorm(q.reshape(bsz, seq, B_HEADS, B_HEAD_DIM), q_norm), positions)
    k = rope(rmsnorm(k.reshape(bsz, seq, B_HEADS, B_HEAD_DIM), k_norm), positions)
    v = v.reshape(bsz, seq, B_HEADS, B_HEAD_DIM)
    qi = qi.reshape(bsz, seq, IDX_HEADS, IDX_DIM)
    qi = jnp.concatenate([rope(qi[..., :IDX_ROPE_DIM], positions), qi[..., IDX_ROPE_DIM:]], axis=-1)
    ki = ki[:, :, None, :]
    ki = jnp.concatenate([rope(ki[..., :IDX_ROPE_DIM], positions), ki[..., IDX_ROPE_DIM:]], axis=-1)[:, :, 0, :]
    wi = wi * (IDX_HEADS ** -0.5 * IDX_DIM ** -0.5)

    topk = min(TOPK_MAX, seq // 4)
    nb = seq // QBLOCK
    scale = B_HEAD_DIM ** -0.5

    def blocks(t):
        return t.reshape((bsz * nb, QBLOCK) + t.shape[2:])

    bidx = jnp.repeat(jnp.arange(bsz, dtype=jnp.int32), nb)
    qstart = jnp.tile(jnp.arange(nb, dtype=jnp.int32) * QBLOCK, bsz)
    k_pos = jnp.arange(seq, dtype=jnp.int32)

    def attend(args):
        q_b, qi_b, w_b, b, q0 = args
        k_b, v_b, ki_b = k[b], v[b], ki[b]
        rel = jax.nn.relu(jnp.einsum("qhd,sd->qhs", qi_b, ki_b).astype(jnp.float32))
        score = jnp.einsum("qh,qhs->qs", w_b.astype(jnp.float32), rel)
        q_pos = q0 + jnp.arange(QBLOCK, dtype=jnp.int32)
        score = jnp.where(chunk_visible(q_pos, k_pos), score, -jnp.inf)
        top_val, top_idx = lax.top_k(score, topk)
        valid = jnp.isfinite(top_val)
        k_sel = k_b[top_idx]
        v_sel = v_b[top_idx]
        logits = jnp.einsum("qhd,qkhd->qhk", q_b, k_sel).astype(jnp.float32) * scale
        logits = jnp.where(valid[:, None, :], logits, -jnp.inf)
        p = jax.nn.softmax(logits, axis=-1).astype(v_sel.dtype)
        return jnp.einsum("qhk,qkhd->qhd", p, v_sel)

    o = lax.map(attend, (blocks(q), blocks(qi), blocks(wi), bidx, qstart))
    o = o.reshape(bsz, seq, B_WIDTH)
    return (o * jax.nn.silu(z)) @ w_out


def dense_chunk_attention(q, k, v, scale):
    bsz, seq, heads, dqk = q.shape
    nb = seq // QBLOCK
    qb = q.reshape(bsz, nb, QBLOCK, heads, dqk).transpose(1, 0, 2, 3, 4)
    starts = jnp.arange(nb, dtype=jnp.int32) * QBLOCK
    k_pos = jnp.arange(seq, dtype=jnp.int32)

    def attend(args):
        q_b, q0 = args
        logits = jnp.einsum("bqhd,bshd->bhqs", q_b, k).astype(jnp.float32) * scale
        mask = chunk_visible(q0 + jnp.arange(QBLOCK, dtype=jnp.int32), k_pos)
        logits = jnp.where(mask[None, None], logits, -jnp.inf)
        p = jax.nn.softmax(logits, axis=-1).astype(v.dtype)
        return jnp.einsum("bhqs,bshd->bqhd", p, v)

    o = lax.map(attend, (qb, starts))
    return o.transpose(1, 0, 2, 3, 4).reshape(bsz, seq, heads, v.shape[-1])


def mla_mixer(xn, positions, w_in, q_lat_norm, kv_lat_norm, w_uq, w_ukv, q_norm, k_norm, w_out):
    bsz, seq, _ = xn.shape
    cuts = np.cumsum([C_Q_LORA, C_KV_LORA, C_ROPE]).tolist()
    cq, ckv, kr, z = jnp.split(xn @ w_in, cuts, axis=-1)
    q = (rmsnorm(cq, q_lat_norm) @ w_uq).reshape(bsz, seq, C_HEADS, C_QK)
    kv = (rmsnorm(ckv, kv_lat_norm) @ w_ukv).reshape(bsz, seq, C_HEADS, C_NOPE + C_V)
    k_nope, v = kv[..., :C_NOPE], kv[..., C_NOPE:]
    k = jnp.concatenate([k_nope, jnp.broadcast_to(kr[:, :, None, :], (bsz, seq, C_HEADS, C_ROPE))], axis=-1)
    q = rmsnorm(q, q_norm)
    k = rmsnorm(k, k_norm)
    q = jnp.concatenate([q[..., :C_NOPE], rope(q[..., C_NOPE:], positions)], axis=-1)
    k = jnp.concatenate([k[..., :C_NOPE], rope(k[..., C_NOPE:], positions)], axis=-1)
    o = dense_chunk_attention(q, k, v, C_QK ** -0.5).reshape(bsz, seq, C_WIDTH)
    return (o * jax.nn.silu(z)) @ w_out


def setup_inputs(seed: int = 0) -> dict:
    key = jax.random.key(seed)
    ks = iter(jax.random.split(key, 32))
    f32 = jnp.float32

    def dense(shape, fan_in):
        return jax.random.normal(next(ks), shape, f32) * fan_in ** -0.5

    def gain(shape):
        return 1.0 + 0.02 * jax.random.normal(next(ks), shape, f32)

    x = jax.random.normal(next(ks), (BATCH, SEQ, D_MODEL), f32)
    offset = jax.random.randint(next(ks), (BATCH,), 0, 64, dtype=jnp.int32) * CHUNK
    positions = offset[:, None] + jnp.arange(SEQ, dtype=jnp.int32)[None, :]
    return {
        "x": x,
        "positions": positions,
        "a_norm": gain((N_A, D_MODEL)),
        "a_w_in": dense((N_A, D_MODEL, 4 * A_WIDTH), D_MODEL),
        "a_conv_w": dense((N_A, CONV_WIDTH, A_WIDTH), CONV_WIDTH),
        "a_conv_b": 0.02 * jax.random.normal(next(ks), (N_A, A_WIDTH), f32),
        "a_w_out": dense((N_A, A_WIDTH, D_MODEL), A_WIDTH),
        "b_norm": gain((N_B, D_MODEL)),
        "b_w_in": dense((N_B, D_MODEL, B_IN_COLS), D_MODEL),
        "b_q_norm": gain((N_B, B_HEAD_DIM)),
        "b_k_norm": gain((N_B, B_HEAD_DIM)),
        "b_w_out": dense((N_B, B_WIDTH, D_MODEL), B_WIDTH),
        "c_norm": gain((N_C, D_MODEL)),
        "c_w_in": dense((N_C, D_MODEL, C_IN_COLS), D_MODEL),
        "c_q_lat_norm": gain((N_C, C_Q_LORA)),
        "c_kv_lat_norm": gain((N_C, C_KV_LORA)),
        "c_w_uq": dense((N_C, C_Q_LORA, C_HEADS * C_QK), C_Q_LORA),
        "c_w_ukv": dense((N_C, C_KV_LORA, C_HEADS * (C_NOPE + C_V)), C_KV_LORA),
        "c_q_norm": gain((N_C, C_QK)),
        "c_k_norm": gain((N_C, C_QK)),
        "c_w_out": dense((N_C, C_WIDTH, D_MODEL), C_WIDTH),
    }


def reference(x, positions, a_norm, a_w_in, a_conv_w, a_conv_b, a_w_out,
              b_norm, b_w_in, b_q_norm, b_k_norm, b_w_out,
              c_norm, c_w_in, c_q_lat_norm, c_kv_lat_norm, c_w_uq, c_w_ukv, c_q_norm, c_k_norm, c_w_out):
    for i in range(DEPTH):
        kind, j = i % N_MIXERS, i // N_MIXERS
        if kind == 0:
            y = short_conv_mixer(rmsnorm(x, a_norm[j]), a_w_in[j], a_conv_w[j], a_conv_b[j], a_w_out[j])
        elif kind == 1:
            y = dsa_mixer(rmsnorm(x, b_norm[j]), positions, b_w_in[j], b_q_norm[j], b_k_norm[j], b_w_out[j])
        else:
            y = mla_mixer(rmsnorm(x, c_norm[j]), positions, c_w_in[j], c_q_lat_norm[j], c_kv_lat_norm[j],
                          c_w_uq[j], c_w_ukv[j], c_q_norm[j], c_k_norm[j], c_w_out[j])
        x = x + y
    return x
```

```python
import math
from contextlib import ExitStack
import numpy as np
import concourse.bass as bass
import concourse.mybir as mybir
from concourse.bass_utils import run_bass_kernel_spmd

F32 = mybir.dt.float32
BF16 = mybir.dt.bfloat16
I32 = mybir.dt.int32
ALU = mybir.AluOpType
AF = mybir.ActivationFunctionType
AX = mybir.AxisListType

SEQ = 2048
D = 1024
EPS = 1e-6
NEG = -1.0e30
TOPK = 256
NBIS = 16
PIPE_DEPTH = 3
HEAT = 0
C_FINE = False
C_DEPTH = 3
C_RATIO = 10 ** 9
STOP_AT = 3


class Buf:
    __slots__ = ("name", "w", "r", "dsem", "dcnt")

    def __init__(self, name):
        self.name = name
        self.w = None
        self.r = []
        self.dsem = None
        self.dcnt = 0


class Sched:
    def __init__(self, nc):
        self.nc = nc
        self.eng = {"pe": nc.tensor, "act": nc.scalar, "dve": nc.vector,
                    "pool": nc.gpsimd, "sp": nc.sync}
        self.sem = {}
        self.cnt = {}
        for e in self.eng:
            self.sem[e] = nc.alloc_semaphore("c_" + e)
            self.cnt[e] = 0
        self.seen = {e: {} for e in self.eng}
        self.dsems = []
        self.free_dsems = []

    def _semof(self, key):
        return self.sem[key] if isinstance(key, str) else key

    def _deps(self, q, reads, writes, is_dma=False):
        need = {}

        def add(d, same_ok):
            if d is None:
                return
            key, val = d
            if same_ok and key == q and not is_dma and q != "pool":
                return
            if key == "pe" and q == "pe" and not is_dma:
                return
            if self.seen[q].get(key, 0) >= val:
                return
            if need.get(key, 0) < val:
                need[key] = val
        for b in reads:
            add(b.w, False)
        for b in writes:
            add(b.w, True)
            for d in b.r:
                add(d, True)
        return need

    def _emit(self, q, need, fn):
        items = list(need.items())
        eng = self.eng[q]
        for key, val in items[:-1]:
            eng.wait_ge(self._semof(key), val)
            self.seen[q][key] = val
        inst = fn()
        if items:
            key, val = items[-1]
            inst._wait_ge(self._semof(key), val)
            self.seen[q][key] = val
        return inst

    def op(self, q, fn, reads=(), writes=(), inc=True):
        need = self._deps(q, reads, writes)
        inst = self._emit(q, need, fn)
        if inc:
            self.cnt[q] += 1
            inst.then_inc(self.sem[q], 1)
            d = (q, self.cnt[q])
        else:
            d = (q, self.cnt[q] + 1)
        for b in reads:
            if len(b.r) > 24:
                b.r = self._compact(b.r)
            b.r.append(d)
        for b in writes:
            b.w = d
            b.r = []
        return inst

    @staticmethod
    def _compact(lst):
        best = {}
        for k, v in lst:
            if best.get(k, 0) < v:
                best[k] = v
        return list(best.items())

    def dma(self, q, out, in_, reads=(), writes=(), own=None):
        if own is None:
            own = writes[0] if writes else reads[0]
        if own.dsem is None:
            self.dsems.append(own)
            own.dsem = self.nc.alloc_semaphore(f"d{len(self.dsems)}_" + own.name)
        need = self._deps(q, reads, writes, is_dma=True)
        inst = self._emit(q, need, lambda: self.eng[q].dma_start(out=out, in_=in_))
        own.dcnt += 16
        inst.then_inc(own.dsem, 16)
        d = (own.dsem, own.dcnt)
        for b in reads:
            b.r.append(d)
        for b in writes:
            b.w = d
            b.r = []
        return inst

    def finish(self, bufs):
        need = {}
        for b in bufs:
            for d in ([b.w] if b.w else []) + list(b.r):
                key, val = d
                if need.get(key, 0) < val:
                    need[key] = val
        for key, val in need.items():
            self.eng["sp"].wait_ge(self._semof(key), val)


class Ring:
    def __init__(self, nc, es, name, n, shape, dtype, psum=False):
        self.items = []
        for i in range(n):
            nm = f"{name}{i}"
            if psum:
                t = es.enter_context(nc.psum_tensor(nm, shape, dtype))
            else:
                t = es.enter_context(nc.sbuf_tensor(nm, shape, dtype))
            self.items.append((t, Buf(nm)))
        self.i = 0

    def get(self):
        it = self.items[self.i % len(self.items)]
        self.i += 1
        return it


CA = [0, 40]
CB = 80
CC = 90
CK = 105
NCOLS = 108 + NBIS + 1
M_ID, M_ONE, M_B64, M_R64, M_RIDX, M_RMLA = 0, 1, 2, 3, 4, 5
NMAT = 6


def tsl(tb):
    return slice(tb * 512, (tb + 1) * 512)


def build(layers):
    nc = bass.Bass("TRN2", target_bir_lowering=False)
    S = Sched(nc)

    def din(name, shape, dt=F32):
        return nc.dram_tensor(name, shape, dt, kind="ExternalInput")

    xT_d = din("xT", [D, SEQ]).ap()
    pos_d = din("pos", [1, SEQ], I32)
    cols_d = din("cols", [128, NCOLS]).ap()
    cmat_d = din("cmat", [128, NMAT * 128]).ap()
    a_w_in = din("a_w_in", [2, 1024, 4096]).ap()
    a_w_out = din("a_w_out", [2, 1024, 1024]).ap()
    b_w_in = din("b_w_in", [1, 1024, 4680]).ap()
    b_w_out = din("b_w_out", [1, 1024, 1024]).ap()
    c_w_in = din("c_w_in", [1, 1024, 1696]).ap()
    c_w_uq = din("c_w_uq", [1, 384, 1536]).ap()
    c_w_ukv = din("c_w_ukv", [1, 256, 2048]).ap()
    c_w_out = din("c_w_out", [1, 1024, 1024]).ap()
    yT_d = nc.dram_tensor("yT", [D, SEQ], F32, kind="ExternalOutput").ap()

    ges = ExitStack()

    def sb(name, shape, dt, es=None):
        return (es or ges).enter_context(nc.sbuf_tensor(name, shape, dt))

    xT = sb("xT_sb", [128, 8, SEQ], F32)
    BxT = [[Buf(f"xT{k}_{t}") for t in range(4)] for k in range(8)]
    xnT = sb("xnT", [128, 8, SEQ], BF16)
    BxnT = [Buf(f"xnT{t}") for t in range(4)]
    cols = sb("cols_sb", [128, NCOLS], F32)
    Bcols = Buf("cols")
    cmat = sb("cmat_sb", [128, NMAT, 128], BF16)
    Bcmat = Buf("cmat")
    PS = Ring(nc, ges, "psg", 4, [128, 512], F32, psum=True)
    PO = Ring(nc, ges, "pso", 4, [128, 512], F32, psum=True)

    PA = Ring.__new__(Ring)
    PA.items = PS.items + PO.items
    PA.i = 0
    _all8 = list(PA.items)
    PA.items = _all8[0:7]
    PO.items = _all8[4:7]

    def subring(idx):
        r = Ring.__new__(Ring)
        r.items = [_all8[i] for i in idx]
        r.i = 0
        return r
    PROJ = [PA]
    T1ENG = ["dve"]
    FINE = [False]
    HEATBANK = _all8[7][0]
    heat_l = cmat[:, M_ONE, :]
    heat_r = cmat[:, 0:4, :].rearrange("p a b -> p (a b)")

    def col(i):
        return cols[:, i:i + 1]

    def mat(i, p=128, m=128):
        return cmat[0:p, i, 0:m]

    ALLDMA = []
    _orig_dma = S.dma

    def dma_track(q, out, in_, reads=(), writes=(), own=None):
        o = own if own is not None else (writes[0] if writes else reads[0])
        if o not in ALLDMA:
            ALLDMA.append(o)
        return _orig_dma(q, out, in_, reads=reads, writes=writes, own=own)
    S.dma = dma_track

    for kc in range(8):
        S.dma("sp", xT[:, kc, :], xT_d[kc * 128:(kc + 1) * 128, :], writes=BxT[kc])
    S.dma("sp", cols[:], cols_d[:, :], writes=[Bcols])
    S.dma("pool", cmat[:].rearrange("p a b -> p (a b)"), cmat_d[:, :], writes=[Bcmat])

    def mm(out, lhsT, rhs, start, stop, reads, writes, inc=None):
        if inc is None:
            inc = stop
        S.op("pe", lambda: nc.tensor.matmul(out, lhsT, rhs, start=start, stop=stop),
             reads=reads, writes=writes, inc=inc)

    def act(out, in_, func, reads, writes, **kw):
        S.op("act", lambda: nc.scalar.activation(out=out, in_=in_, func=func, **kw),
             reads=reads, writes=writes)

    def tt(q, out, in0, in1, op, reads, writes):
        e = S.eng[q]
        S.op(q, lambda: e.tensor_tensor(out=out, in0=in0, in1=in1, op=op), reads=reads, writes=writes)

    def ts(q, out, in0, s1, s2, op0, op1, reads, writes, **kw):
        e = S.eng[q]
        if s2 is None:
            S.op(q, lambda: e.tensor_scalar(out=out, in0=in0, scalar1=s1, scalar2=None, op0=op0, **kw),
                 reads=reads, writes=writes)
        else:
            S.op(q, lambda: e.tensor_scalar(out=out, in0=in0, scalar1=s1, scalar2=s2, op0=op0, op1=op1, **kw),
                 reads=reads, writes=writes)

    def stt(q, out, in0, scalar, in1, op0, op1, reads, writes):
        e = S.eng[q]
        S.op(q, lambda: e.scalar_tensor_tensor(out=out, in0=in0, scalar=scalar, in1=in1, op0=op0, op1=op1),
             reads=reads, writes=writes)

    def rstd_from_ssq(ssq_ps, Bssq, P, n, N, f32r):
        lnv, Bl = f32r.get()
        act(lnv[0:P, 0:N], ssq_ps, AF.Ln, [Bssq], [Bl], scale=1.0 / n, bias=EPS)
        rs, Br = f32r.get()
        act(rs[0:P, 0:N], lnv[0:P, 0:N], AF.Exp, [Bl], [Br], scale=-0.5)
        return rs, Br

    def emit_norm(gc, f32r, sqr):
        for tb in range(4):
            pb, Bpb = PS.get()
            for kc in range(8):
                sq, Bsq = sqr.get()
                act(sq[:], xT[:, kc, tsl(tb)], AF.Square, [BxT[kc][tb]], [Bsq])
                mm(pb[:], mat(M_ONE), sq[:], kc == 0, kc == 7, [Bsq, Bcmat], [Bpb], inc=True)
            rs, Br = rstd_from_ssq(pb[:], Bpb, 128, 1024.0, 512, f32r)
            for kc in range(8):
                stt("dve", xnT[:, kc, tsl(tb)], xT[:, kc, tsl(tb)], col(gc + kc), rs[:], ALU.mult, ALU.mult,
                    [BxT[kc][tb], Br, Bcols], [BxnT[tb]])

    def wout_partial(wo_ap, Bwo, g_ap, Bg, first_last=None):
        for dc in range(8):
            for tb in range(4):
                pb, Bpb = PS.get()
                mm(pb[:], wo_ap[:, dc * 128:(dc + 1) * 128], g_ap[:, tsl(tb)], True, True, [Bwo, Bg], [Bpb])
                tt("dve", xT[:, dc, tsl(tb)], xT[:, dc, tsl(tb)], pb[:], ALU.add,
                   [BxT[dc][tb], Bpb], [BxT[dc][tb]])

    def wout_g(wo_ap, Bwo, g_ap, Bg, ring):
        items = [(dc, tb) for dc in range(8) for tb in range(4)]
        look = len(ring.items) - 1

        def issue(k):
            dc, tb = items[k]
            pb, Bpb = ring.get()
            mm(pb[:], wo_ap[:, dc * 128:(dc + 1) * 128], g_ap[:, tsl(tb)], True, True, [Bwo, Bg], [Bpb])
            return pb, Bpb
        pend = [issue(k) for k in range(look)]
        for k, (dc, tb) in enumerate(items):
            pb, Bpb = pend.pop(0)
            if k + look < len(items):
                pend.append(issue(k + look))
            tt("dve", xT[:, dc, tsl(tb)], xT[:, dc, tsl(tb)], pb[:], ALU.add,
               [BxT[dc][tb], Bpb], [BxT[dc][tb]])
            yield

    def layer_A(j):
        cb = CA[j]
        with ExitStack() as es:
            f32r = Ring(nc, es, f"a{j}f32_", 6, [128, 512], F32)
            sqr = Ring(nc, es, f"a{j}sq_", 3, [128, 512], BF16)
            emit_norm(cb, f32r, sqr)
            gT = sb(f"a{j}_gT", [128, 8, SEQ], BF16, es)
            BgT = [[Buf(f"gT{c}_{t}") for t in range(4)] for c in range(8)]
            wout = sb(f"a{j}_wout", [128, 8, 1024], BF16, es)
            Bwout = Buf("a_wout")
            S.dma("pool", wout[:], a_w_out[j].rearrange("(cc p) d -> p cc d", p=128), writes=[Bwout])
            war = [(sb(f"a{j}_w{i}", [128, 8, 4, 128], BF16, es), [Buf(f"a{j}_w{i}_{g}") for g in range(4)]) for i in range(2)]
            stg = Ring(nc, es, f"a{j}_stg", 2, [128, 8, 128], F32)
            ur = [(sb(f"a{j}_u{i}", [128, 2 + SEQ], F32, es), [Buf(f"a_u{i}_{t}") for t in range(5)]) for i in range(2)]
            wv = a_w_in[j].rearrange("(kc p) (g c i) -> p kc g c i", p=128, g=4, c=8)

            def issue_loads(c):
                wa, Bwa = war[c % 2]
                for g in (0, 1):
                    S.dma("pool", wa[:, :, g, :], wv[:, :, g, c, :], writes=[Bwa[g]])
                stgs = []
                for g in (2, 3):
                    st_, Bst_ = stg.get()
                    S.dma("sp", st_[:], wv[:, :, g, c, :], writes=[Bst_])
                    stgs.append((g, st_, Bst_))
                return stgs

            def issue_casts(c, stgs):
                wa, Bwa = war[c % 2]
                for g, st_, Bst_ in stgs:
                    act(wa[:, :, g, :], st_[:], AF.Copy, [Bst_], [Bwa[g]])

            issue_casts(0, issue_loads(0))
            for c in range(8):
                wa, Bwa = war[c % 2]
                nxt = issue_loads(c + 1) if c + 1 < 8 else None
                u, Bu = ur[c % 2]
                S.op("pool", lambda: nc.gpsimd.memset(u[:, 0:2], 0.0), writes=[Bu[4]])
                for tb in range(4):
                    banks = [_all8[(tb % 2) * 4 + g_] for g_ in range(4)]
                    for g in range(4):
                        for kc in range(8):
                            mm(banks[g][0][:], wa[:, kc, g, :], xnT[:, kc, tsl(tb)], kc == 0, kc == 7,
                               [Bwa[g], BxnT[tb]], [banks[g][1]])
                    (bg, Bbg), (cg, Bcg), (hv, Bhv), (z, Bz) = banks
                    hvs, Bhvs = f32r.get()
                    act(hvs[:], hv[:], AF.Copy, [Bhv], [Bhvs])
                    us = slice(2 + tb * 512, 2 + (tb + 1) * 512)
                    tt("dve", u[:, us], cg[:], hvs[:], ALU.mult, [Bcg, Bhvs], [Bu[tb]])
                    prev = Bu[tb - 1] if tb > 0 else Bu[4]
                    y, By = f32r.get()
                    ts("pool", y[:], u[:, us], col(cb + 24 + c), col(cb + 32 + c), ALU.mult, ALU.add,
                       [Bu[tb], Bcols], [By])
                    stt("dve", y[:], u[:, 1 + tb * 512:1 + (tb + 1) * 512], col(cb + 16 + c), y[:], ALU.mult, ALU.add,
                        [Bu[tb], prev, Bcols, By], [By])
                    stt("dve", y[:], u[:, tb * 512:(tb + 1) * 512], col(cb + 8 + c), y[:], ALU.mult, ALU.add,
                        [Bu[tb], prev, Bcols, By], [By])
                    szs, Bszs = f32r.get()
                    act(szs[:], z[:], AF.Silu, [Bz], [Bszs])
                    tt("dve", szs[:], bg[:], szs[:], ALU.mult, [Bbg, Bszs], [Bszs])
                    tt("pool", gT[:, c, tsl(tb)], szs[:], y[:], ALU.mult, [Bszs, By], [BgT[c][tb]])
                    if tb == 1 and nxt is not None:
                        issue_casts(c + 1, nxt)
            for dc in range(8):
                for tb in range(4):
                    pb, Bpb = PS.get()
                    for cc in range(8):
                        mm(pb[:], wout[:, cc, dc * 128:(dc + 1) * 128], gT[:, cc, tsl(tb)], cc == 0, cc == 7,
                           [Bwout, BgT[cc][tb]], [Bpb])
                    tt("dve", xT[:, dc, tsl(tb)], xT[:, dc, tsl(tb)], pb[:], ALU.add,
                       [BxT[dc][tb], Bpb], [BxT[dc][tb]])
            barrier()

    def barrier():
        for q in ("pe", "act", "dve", "pool"):
            pass
        for q in S.eng:
            for e in S.eng:
                if e == q:
                    continue
                v = S.cnt[e]
                if v > 0 and S.seen[q].get(e, 0) < v:
                    S.eng[q].wait_ge(S.sem[e], v)
                    S.seen[q][e] = v
        for b in ALLDMA:
            if b.dsem is not None and b.dcnt > 0:
                for q in S.eng:
                    if S.seen[q].get(b.dsem, 0) < b.dcnt:
                        S.eng[q].wait_ge(b.dsem, b.dcnt)
                        S.seen[q][b.dsem] = b.dcnt


    def make_tables(es_tab, es_tmp, invc, name):
        tabs = []
        for nm in ("C", "S"):
            tabs.append((sb(name + "_" + nm, [128, SEQ], F32, es_tab), Buf(name + nm)))
        posi = sb(name + "_posi", [128, 512], I32, es_tmp)
        Bposi = Buf(name + "posi")
        a2 = sb(name + "_a2", [128, 512], F32, es_tmp)
        Ba2 = Buf(name + "a2")
        t1 = sb(name + "_t1", [128, 512], F32, es_tmp)
        Bt1 = Buf(name + "t1")
        ki = sb(name + "_ki", [128, 512], I32, es_tmp)
        Bki = Buf(name + "ki")
        for tb in range(4):
            S.dma("sp", posi[:], bass.AP(pos_d, tb * 512, [[0, 128], [1, 512]]), writes=[Bposi])
            S.op("dve", lambda: nc.vector.tensor_copy(out=a2[:], in_=posi[:]), reads=[Bposi], writes=[Ba2])
            ts("dve", a2[:], a2[:], col(invc), 1.0 / (2 * math.pi), ALU.mult, ALU.mult, [Ba2, Bcols], [Ba2])
            for (tab, Btab), c0 in zip(tabs, (0.25, 0.0)):
                ts("dve", t1[:], a2[:], c0, None, ALU.add, None, [Ba2], [Bt1])
                S.op("dve", lambda: nc.vector.tensor_copy(out=ki[:], in_=t1[:]), reads=[Bt1], writes=[Bki])
                S.op("dve", lambda: nc.vector.tensor_copy(out=tab[:, tsl(tb)], in_=ki[:]), reads=[Bki], writes=[Btab])
                tt("dve", t1[:], t1[:], tab[:, tsl(tb)], ALU.subtract, [Bt1, Btab], [Bt1])
                stt("dve", t1[:], t1[:], 0.5, t1[:], ALU.is_gt, ALU.subtract, [Bt1], [Bt1])
                act(tab[:, tsl(tb)], t1[:], AF.Sin, [Bt1], [Btab], scale=-2.0 * math.pi)
        barrier()
        return tabs

    def norm_rope_g(srcfn, P, blk, n, gcol, rot, Ct, St, tb, out, Bout, f32r, bfr, eng2="pool", out_hi=None, Bout_hi=None):
        (C, BC), (Sn, BS) = Ct, St
        fine = FINE[0]
        src, Bsrc = srcfn()
        if fine:
            yield
        if n is not None:
            sq, Bsq = bfr.get()
            act(sq[0:P, :], src, AF.Square, [Bsrc], [Bsq])
            yield
            pb, Bpb = PROJ[0].get()
            mm(pb[0:P, :], blk, sq[0:P, :], True, True, [Bsq, Bcmat], [Bpb])
            if fine:
                yield
            rs, Br = rstd_from_ssq(pb[0:P, :], Bpb, P, float(n), 512, f32r)
            yield
            xn, Bxn = bfr.get()
            stt("dve", xn[0:P, :], src, gcol, rs[0:P, :], ALU.mult, ALU.mult, [Bsrc, Br, Bcols], [Bxn])
        else:
            yield
            xn, Bxn = bfr.get()
            act(xn[0:P, :], src, AF.Copy, [Bsrc], [Bxn])
        if fine:
            yield
        rp, Brp = PROJ[0].get()
        mm(rp[0:P, :], rot, xn[0:P, :], True, True, [Bxn, Bcmat], [Brp])
        yield
        t1, Bt1 = f32r.get()
        tt(T1ENG[0], t1[0:P, :], xn[0:P, :], C[0:P, tsl(tb)], ALU.mult, [Bxn, BC], [Bt1])
        t2, Bt2 = f32r.get()
        tt("dve", t2[0:P, :], rp[0:P, :], Sn[0:P, tsl(tb)], ALU.mult, [Brp, BS], [Bt2])
        if fine:
            yield
        if out_hi is None:
            tt(eng2, out, t1[0:P, :], t2[0:P, :], ALU.add, [Bt1, Bt2], [Bout])
        else:
            tt(eng2, out, t1[0:64, :], t2[0:64, :], ALU.add, [Bt1, Bt2], [Bout])
            tt(eng2, out_hi, t1[64:128, :], t2[64:128, :], ALU.add, [Bt1, Bt2], [Bout_hi])

    def run_pipe_g(gens, depth):
        active = []
        it = iter(gens)
        more = True
        while True:
            for g in list(active):
                try:
                    next(g)
                except StopIteration:
                    active.remove(g)
            if more and len(active) < depth:
                try:
                    g = next(it)
                    active.append(g)
                    try:
                        next(g)
                    except StopIteration:
                        active.remove(g)
                except StopIteration:
                    more = False
            if not active and not more:
                break
            yield

    def run_pipe(gens, depth):
        for _ in run_pipe_g(gens, depth):
            pass

    def co_run(main, side, ratio):
        k = 0
        main_done = side is None and False
        while True:
            try:
                next(main)
            except StopIteration:
                break
            k += 1
            if side is not None and ratio < 0:
                for _ in range(-ratio):
                    try:
                        next(side)
                    except StopIteration:
                        side = None
                        break
            elif side is not None and k % ratio == 0:
                try:
                    next(side)
                except StopIteration:
                    side = None
        if side is not None:
            for _ in side:
                pass

    def attention_g(heads, scale, mask_fn, gT, BgT, sz, Bsz, f32r, ptr, st_ring=None, o_ring=None, heat=0, recip_dve=False):
        st_ring = st_ring or PS
        o_ring = o_ring or PO
        for j in range(4):
            for hd in heads:
                kT, Bk = hd["k"]
                qT, Bq = hd["q"]
                osl, lsl = hd["osl"], hd["lsl"]
                O, BO = o_ring.get()
                last = 4 * j + 3

                def issue_st(i):
                    off = max(0, 128 * (i - 4 * j))
                    N = 512 - off
                    q0 = 512 * j + off
                    st, Bst = st_ring.get()
                    mm(st[:, 0:N], kT[:, i * 128:(i + 1) * 128], qT[:, q0:q0 + N], True, True, [Bk, Bq], [Bst])
                    return st, Bst, off, N, q0
                LOOK = len(st_ring.items) - 1
                pend = [issue_st(i) for i in range(min(LOOK, last + 1))]
                for i in range(last + 1):
                    st, Bst, off, N, q0 = pend.pop(0)
                    if i + LOOK <= last:
                        pend.append(issue_st(i + LOOK))
                    pt, Bpt = ptr.get()
                    act(pt[:, 0:N], st[:, 0:N], AF.Exp, [Bst], [Bpt], scale=scale)
                    if mask_fn is not None:
                        m_ap, Bm = mask_fn(i, q0, N)
                        tt("dve", pt[:, 0:N], pt[:, 0:N], m_ap, ALU.mult, [Bpt, Bm], [Bpt])
                    elif i >= 4 * j:
                        S.op("pool", lambda: nc.gpsimd.memset(pt[64:128, 0:64], 0.0), writes=[Bpt])
                    mm(O[:, off:512], hd["v"](i), pt[:, 0:N], i == 0, i == last, [hd["Bv"], Bpt], [BO], inc=True)
                    for _ in range(heat):
                        nc.tensor.matmul(HEATBANK[:, 0:512], heat_l, heat_r, start=True, stop=True)
                    yield
                rl, Brl = f32r.get()
                if recip_dve and j < 3:
                    S.op("dve", lambda: nc.vector.reciprocal(out=rl[lsl, :], in_=O[lsl, :]), reads=[BO], writes=[Brl])
                else:
                    act(rl[lsl, :], O[lsl, :], AF.Ln, [BO], [Brl])
                    act(rl[lsl, :], rl[lsl, :], AF.Exp, [Brl], [Brl], scale=-1.0)
                tmp, Btmp = f32r.get()
                tt("dve", tmp[osl, :], O[osl, :], rl[lsl, :], ALU.mult, [BO, Brl], [Btmp])
                tt("pool", gT[osl, tsl(j)], tmp[osl, :], sz[osl, tsl(j)], ALU.mult, [Btmp, Bsz], [BgT])

    def attention(heads, scale, mask_fn, gT, BgT, sz, Bsz, f32r, ptr, heat=0):
        for _ in attention_g(heads, scale, mask_fn, gT, BgT, sz, Bsz, f32r, ptr, heat=heat):
            pass

    def load_w(q, dst, src, Bw):
        S.dma(q, dst, src, writes=[Bw])

    def layer_C():
        W = c_w_in[0]
        with ExitStack() as es:
            f32r = Ring(nc, es, "cf32_", 6, [128, 512], F32)
            bfr = Ring(nc, es, "cbf_", 6, [128, 512], BF16)
            emit_norm(CC, f32r, bfr)
            with ExitStack() as es_tmp:
                Ct, St = make_tables(es, es_tmp, CK + 2, "ct")
            cqn = sb("c_cqn", [128, 3, SEQ], BF16, es)
            Bcqn = [Buf(f"cqn{t}") for t in range(4)]
            ckvn = sb("c_ckvn", [128, 2, SEQ], BF16, es)
            Bckvn = [Buf(f"ckvn{t}") for t in range(4)]
            krT = sb("c_krT", [128, SEQ], F32, es)
            BkrT = [Buf(f"krT{t}") for t in range(4)]
            with ExitStack() as es1:
                wcq = sb("c_wcq", [128, 8, 384], BF16, es1)
                Bwcq = Buf("wcq")
                wckv = sb("c_wckv", [128, 8, 256], BF16, es1)
                Bwckv = Buf("wckv")
                wkr = sb("c_wkr", [128, 8, 96], BF16, es1)
                Bwkr = Buf("wkr")
                Wv = W.rearrange("(kc p) n -> p kc n", p=128)
                load_w("pool", wcq[:], Wv[:, :, 0:384], Bwcq)
                load_w("pool", wckv[:], Wv[:, :, 384:640], Bwckv)
                S.op("pool", lambda: nc.gpsimd.memset(wkr[:], 0.0), writes=[Bwkr])
                load_w("pool", wkr[:, :, 64:96], Wv[:, :, 640:672], Bwkr)
                for tb in range(4):
                    for (wt, Bwt, nch, dst, Bdst, gc, nn) in ((wcq, Bwcq, 3, cqn, Bcqn, CC + 8, 384.0),
                                                              (wckv, Bwckv, 2, ckvn, Bckvn, CC + 11, 256.0)):
                        banks = [PS.get() for _ in range(nch)]
                        for ch in range(nch):
                            for kc in range(8):
                                mm(banks[ch][0][:], wt[:, kc, ch * 128:(ch + 1) * 128], xnT[:, kc, tsl(tb)], kc == 0, kc == 7,
                                   [Bwt, BxnT[tb]], [banks[ch][1]])
                        ssq, Bssq = PO.get()
                        for ch in range(nch):
                            sq, Bsq = bfr.get()
                            act(sq[:], banks[ch][0][:], AF.Square, [banks[ch][1]], [Bsq])
                            mm(ssq[:], mat(M_ONE), sq[:], ch == 0, ch == nch - 1, [Bsq, Bcmat], [Bssq], inc=True)
                        rs, Br = rstd_from_ssq(ssq[:], Bssq, 128, nn, 512, f32r)
                        for ch in range(nch):
                            stt("dve", dst[:, ch, tsl(tb)], banks[ch][0][:], col(gc + ch), rs[:], ALU.mult, ALU.mult,
                                [banks[ch][1], Br, Bcols], [Bdst[tb]])
                    pb, Bpb = PS.get()
                    for kc in range(8):
                        mm(pb[0:96, :], wkr[:, kc, :], xnT[:, kc, tsl(tb)], kc == 0, kc == 7, [Bwkr, BxnT[tb]], [Bpb])
                    act(krT[64:96, tsl(tb)], pb[64:96, :], AF.Copy, [Bpb], [BkrT[tb]])
                barrier()
            wzr = Ring(nc, es, "c_wz", 2, [128, 8, 128], BF16)
            wor = Ring(nc, es, "c_wo", 2, [128, 1024], BF16)
            wuqr = Ring(nc, es, "c_wuq", 2, [128, 3, 96], BF16)
            wukvr = Ring(nc, es, "c_wukv", 2, [128, 2, 128], BF16)
            qr = Ring(nc, es, "c_q", 2, [128, SEQ], BF16)
            kr_ = Ring(nc, es, "c_k", 2, [128, SEQ], BF16)
            vr = Ring(nc, es, "c_v", 2, [128, 16, 128], BF16)
            szr = Ring(nc, es, "c_sz", 1, [128, SEQ], BF16)
            gr = Ring(nc, es, "c_g", 1, [128, SEQ], BF16)
            ptr = Ring(nc, es, "c_pt", 4, [128, 512], BF16)
            for sl_, (vt, Bv) in enumerate(vr.items):
                lo = 64 if sl_ == 0 else 0
                S.op("pool", lambda: nc.gpsimd.memset(vt[:, :, lo:lo + 64], 1.0), writes=[Bv])
            Wz = W.rearrange("(kc p) n -> p kc n", p=128)
            Wuq = c_w_uq[0].rearrange("(kc p) n -> p kc n", p=128)
            Wukv = c_w_ukv[0].rearrange("(kc p) n -> p kc n", p=128)
            Wo = c_w_out[0]
            scale = 96.0 ** -0.5
            C_ST = subring([0, 1, 2, 3])
            C_O = subring([4, 5])
            C_PJ = subring([4, 5, 6, 7])
            hds = {}

            def head_proj_g(h):
                hh = h % 2
                wuq, Bwuq = wuqr.get()
                load_w("pool", wuq[:], Wuq[:, :, h * 96:(h + 1) * 96], Bwuq)
                wukv, Bwukv = wukvr.get()
                load_w("pool", wukv[:], Wukv[:, :, h * 128:(h + 1) * 128], Bwukv)
                qT, Bq = qr.get()
                kT, Bk = kr_.get()
                vt, Bv = vr.items[hh]
                vlo = 0 if hh == 0 else 64
                hds[h] = dict(k=(kT[0:96, :], Bk), q=(qT[0:96, :], Bq), v=(lambda i, vt=vt: vt[:, i, :]), Bv=Bv,
                              osl=slice(vlo, vlo + 64), lsl=slice(64 - vlo, 128 - vlo))
                gens = []
                for tb in range(4):
                    def srcq(tb=tb):
                        pq, Bpq = C_PJ.get()
                        for kc in range(3):
                            mm(pq[0:96, :], wuq[:, kc, :], cqn[:, kc, tsl(tb)], kc == 0, kc == 2, [Bwuq, Bcqn[tb]], [Bpq])
                        return pq[0:96, :], Bpq
                    gens.append(norm_rope_g(srcq, 96, mat(M_ONE, 96, 96), 96, cols[0:96, CC + 13:CC + 14], mat(M_RMLA, 96, 96),
                                            Ct, St, tb, qT[0:96, tsl(tb)], Bq, f32r, bfr))

                    def srck(tb=tb):
                        pk, Bpk = C_PJ.get()
                        for kc in range(2):
                            mm(pk[0:64, :], wukv[:, kc, 0:64], ckvn[:, kc, tsl(tb)], kc == 0, kc == 1, [Bwukv, Bckvn[tb]], [Bpk])
                        act(pk[64:96, :], krT[64:96, tsl(tb)], AF.Copy, [BkrT[tb]], [Bpk])
                        return pk[0:96, :], Bpk
                    gens.append(norm_rope_g(srck, 96, mat(M_ONE, 96, 96), 96, cols[0:96, CC + 14:CC + 15], mat(M_RMLA, 96, 96),
                                            Ct, St, tb, kT[0:96, tsl(tb)], Bk, f32r, bfr))
                yield from run_pipe_g(gens, C_DEPTH)
                for g in range(2):
                    pv, Bpv = C_PJ.get()
                    for t8 in range(8):
                        tile_ = g * 8 + t8
                        for kc in range(2):
                            mm(pv[:, t8 * 64:(t8 + 1) * 64], ckvn[:, kc, tile_ * 128:(tile_ + 1) * 128], wukv[:, kc, 64:128],
                               kc == 0, kc == 1, [Bwukv, Bckvn[tile_ // 4]], [Bpv])
                    act(vt[:, g * 8:(g + 1) * 8, vlo:vlo + 64], pv[:].rearrange("p (a b) -> p a b", b=64), AF.Copy, [Bpv], [Bv])
                    yield

            PROJ[0] = C_PJ
            T1ENG[0] = "pool"
            for _ in head_proj_g(0):
                pass
            wo = Bwo = sz = Bsz = gT = BgT = None
            for h in range(16):
                c, hh = divmod(h, 2)
                if hh == 0:
                    wz, Bwz = wzr.get()
                    load_w("pool", wz[:], Wz[:, :, 672 + c * 128:672 + (c + 1) * 128], Bwz)
                    wo, Bwo = wor.get()
                    load_w("pool", wo[:], Wo[c * 128:(c + 1) * 128, :], Bwo)
                    sz, Bsz = szr.get()
                    for tb in range(4):
                        pb, Bpb = C_PJ.get()
                        for kc in range(8):
                            mm(pb[:], wz[:, kc, :], xnT[:, kc, tsl(tb)], kc == 0, kc == 7, [Bwz, BxnT[tb]], [Bpb])
                        act(sz[:, tsl(tb)], pb[:], AF.Silu, [Bpb], [Bsz])
                    gT, BgT = gr.get()
                side = head_proj_g(h + 1) if h + 1 < 16 else None
                main = attention_g([hds[h]], scale, None, gT, BgT, sz, Bsz, f32r, ptr, st_ring=C_ST, o_ring=C_O, recip_dve=True)
                FINE[0] = C_FINE
                co_run(main, side, C_RATIO)
                FINE[0] = False
                if hh == 1:
                    wout_partial(wo, Bwo, gT, BgT)
            PROJ[0] = PA
            T1ENG[0] = "dve"
            barrier()


    def layer_B():
        Wv = b_w_in[0].rearrange("(kc p) n -> p kc n", p=128)
        MOFF = [sum(2048 - 128 * ii for ii in range(i)) for i in range(16)]
        with ExitStack() as es:
            f32r = Ring(nc, es, "bf32_", 4, [128, 512], F32)
            bfr = Ring(nc, es, "bbf_", 6, [128, 512], BF16)
            emit_norm(CB, f32r, bfr)
            maskT = sb("b_maskT", [128, 17408], BF16, es)
            BmaskT = [Buf(f"maskT{i}") for i in range(16)]
            with ExitStack() as es2:
                qiT = sb("b_qiT", [128, 4, SEQ], BF16, es2)
                BqiT = [Buf(f"qiT{t}") for t in range(4)]
                kiT2 = sb("b_kiT2", [128, 2, SEQ], BF16, es2)
                BkiT = [Buf(f"kiT{t}") for t in range(4)]
                S.op("pool", lambda: nc.gpsimd.memset(kiT2[64:128, 0, :], 0.0), writes=BkiT)
                S.op("pool", lambda: nc.gpsimd.memset(kiT2[0:64, 1, :], 0.0), writes=BkiT)
                wi_sb = sb("b_wi", [128, 16, 8], F32, es2)
                Bwi = Buf("wi")
                with ExitStack() as es_w:
                    with ExitStack() as es_tmp:
                        Ct, St = make_tables(es_w, es_tmp, CK + 1, "bi")
                    wqi = sb("b_wqi", [128, 8, 512], BF16, es_w)
                    Bwqi = Buf("wqi")
                    wki2 = sb("b_wki2", [128, 8, 128], BF16, es_w)
                    Bwki2 = Buf("wki2")
                    wwi = sb("b_wwi", [128, 8, 8], BF16, es_w)
                    Bwwi = Buf("wwi")
                    load_w("pool", wqi[:], Wv[:, :, 4096:4608], Bwqi)
                    load_w("pool", wki2[:, :, 0:64], Wv[:, :, 4608:4672], Bwki2)
                    load_w("pool", wki2[:, :, 64:128], Wv[:, :, 4608:4672], Bwki2)
                    load_w("pool", wwi[:], Wv[:, :, 4672:4680], Bwwi)
                    gens = []
                    for tb in range(4):
                        for ch in range(4):
                            def srcqi(tb=tb, ch=ch):
                                pb, Bpb = PA.get()
                                for kc in range(8):
                                    mm(pb[:], wqi[:, kc, ch * 128:(ch + 1) * 128], xnT[:, kc, tsl(tb)], kc == 0, kc == 7,
                                       [Bwqi, BxnT[tb]], [Bpb])
                                return pb[:], Bpb
                            gens.append(norm_rope_g(srcqi, 128, None, None, None, mat(M_RIDX), Ct, St, tb, qiT[:, ch, tsl(tb)], BqiT[tb], f32r, bfr))

                        def srcki(tb=tb):
                            pb, Bpb = PA.get()
                            for kc in range(8):
                                mm(pb[:], wki2[:, kc, :], xnT[:, kc, tsl(tb)], kc == 0, kc == 7, [Bwki2, BxnT[tb]], [Bpb])
                            return pb[:], Bpb
                        gens.append(norm_rope_g(srcki, 128, None, None, None, mat(M_RIDX), Ct, St, tb, kiT2[0:64, 0, tsl(tb)], BkiT[tb], f32r, bfr,
                                                out_hi=kiT2[64:128, 1, tsl(tb)], Bout_hi=BkiT[tb]))
                    run_pipe(gens, PIPE_DEPTH)
                    for tb in range(4):
                        pw, Bpw = PO.get()
                        for t4 in range(4):
                            tile_ = tb * 4 + t4
                            for kc in range(8):
                                mm(pw[:, t4 * 8:(t4 + 1) * 8], xnT[:, kc, tile_ * 128:(tile_ + 1) * 128], wwi[:, kc, :], kc == 0, kc == 7,
                                   [Bwwi, BxnT[tb]], [Bpw])
                        ts("dve", wi_sb[:, tb * 4:(tb + 1) * 4, :], pw[:, 0:32].rearrange("p (a b) -> p a b", b=8),
                           (8.0 ** -0.5) * (64.0 ** -0.5), None, ALU.mult, None, [Bpw], [Bwi])
                    barrier()
                scr = Ring(nc, es2, "b_sc", 2, [128, SEQ], F32)
                junkr = Ring(nc, es2, "b_junk", 2, [128, SEQ], BF16)
                mskr = Ring(nc, es2, "b_msk", 1, [128, SEQ], BF16)
                dgr = Ring(nc, es2, "b_dg", 2, [128, 8, 128], BF16)
                str_ = Ring(nc, es2, "b_st", 2, [128, 8], F32)
                dlr = Ring(nc, es2, "b_dl", 2, [128, NBIS + 1], F32)
                ev = [0]

                def score_tile(t):
                    svis = 128 * (t + 1)
                    sc, Bsc = scr.get()
                    dg, Bdg = dgr.get()
                    for h in range(8):
                        ts("dve", dg[:, h, :], mat(M_ID), wi_sb[:, t, h:h + 1], None, ALU.mult, None, [Bwi, Bcmat], [Bdg])
                    nsb = (svis + 511) // 512
                    units = [(sbk, h) for sbk in range(nsb) for h in range(8)]

                    def issue_rp(sbk, h):
                        w = min(512, svis - 512 * sbk)
                        rp, Brp = PS.get()
                        mm(rp[:, 0:w], qiT[:, h // 2, t * 128:(t + 1) * 128], kiT2[:, h % 2, sbk * 512:sbk * 512 + w],
                           True, True, [BqiT[t // 4], BkiT[sbk]], [Brp])
                        return rp, Brp, w
                    LOOK = 3
                    pend = [issue_rp(*u) for u in units[:LOOK]]
                    sp = Bsp = None
                    for idx, (sbk, h) in enumerate(units):
                        rp, Brp, w = pend.pop(0)
                        if idx + LOOK < len(units):
                            pend.append(issue_rp(*units[idx + LOOK]))
                        if h == 0:
                            sp, Bsp = PO.get()
                        rl, Brl = bfr.get()
                        if h % 2 == 0:
                            act(rl[:, 0:w], rp[:, 0:w], AF.Relu, [Brp], [Brl])
                        else:
                            ts("dve", rl[:, 0:w], rp[:, 0:w], 0.0, None, ALU.max, None, [Brp], [Brl])
                        mm(sp[:, 0:w], dg[:, h, :], rl[:, 0:w], h == 0, h == 7, [Bdg, Brl], [Bsp], inc=True)
                        if h == 7:
                            act(sc[:, sbk * 512:sbk * 512 + w], sp[:, 0:w], AF.Copy, [Bsp], [Bsc])
                    st, Bst = str_.get()
                    dl, Bdl = dlr.get()
                    if t >= 2:
                        S.op("dve", lambda: nc.vector.tensor_reduce(out=st[:, 0:1], in_=sc[:, 0:svis], axis=AX.X, op=ALU.max),
                             reads=[Bsc], writes=[Bst])
                        S.op("dve", lambda: nc.vector.tensor_reduce(out=st[:, 1:2], in_=sc[:, 0:svis], axis=AX.X, op=ALU.min),
                             reads=[Bsc], writes=[Bst])
                    S.op("pool", lambda: nc.gpsimd.memset(sc[0:64, svis - 64:svis], NEG), writes=[Bsc])
                    junk, Bjunk = junkr.get()
                    return dict(t=t, svis=svis, sc=sc, Bsc=Bsc, st=st, Bst=Bst, dl=dl, Bdl=Bdl, junk=junk, Bjunk=Bjunk)

                def bisect(tiles):
                    tiles = [T for T in tiles if T["t"] >= 2]
                    for k_, T in enumerate(tiles):
                        T["neg"] = (k_ % 2 == 1)
                        st, Bst, dl, Bdl = T["st"], T["Bst"], T["dl"], T["Bdl"]
                        tt("dve", st[:, 2:3], st[:, 0:1], st[:, 1:2], ALU.subtract, [Bst], [Bst])
                        ts("dve", st[:, 2:3], st[:, 2:3], 1.001, 1e-6, ALU.mult, ALU.add, [Bst], [Bst])
                        if not T["neg"]:
                            ts("dve", dl[:], cols[:, CK + 3:CK + 4 + NBIS], st[:, 2:3], None, ALU.mult, None, [Bst, Bcols], [Bdl])
                            tt("dve", st[:, 3:4], st[:, 0:1], st[:, 2:3], ALU.subtract, [Bst], [Bst])
                            tt("dve", st[:, 3:4], st[:, 3:4], dl[:, 0:1], ALU.add, [Bst, Bdl], [Bst])
                        else:
                            ts("dve", st[:, 7:8], st[:, 2:3], -1.0, None, ALU.mult, None, [Bst], [Bst])
                            ts("dve", dl[:], cols[:, CK + 3:CK + 4 + NBIS], st[:, 7:8], None, ALU.mult, None, [Bst, Bcols], [Bdl])
                            tt("dve", st[:, 3:4], st[:, 2:3], st[:, 0:1], ALU.subtract, [Bst], [Bst])
                            tt("dve", st[:, 3:4], st[:, 3:4], dl[:, 0:1], ALU.add, [Bst, Bdl], [Bst])
                    for i in range(NBIS):
                        for T in tiles:
                            st, Bst, dl, Bdl = T["st"], T["Bst"], T["dl"], T["Bdl"]
                            if not T["neg"]:
                                ts("dve", T["junk"][:, 0:T["svis"]], T["sc"][:, 0:T["svis"]], st[:, 3:4], 0.0, ALU.is_ge, ALU.add,
                                   [T["Bsc"], Bst], [T["Bjunk"], Bst], accum_out=st[:, 4:5])
                            else:
                                act(T["junk"][:, 0:T["svis"]], T["sc"][:, 0:T["svis"]], AF.Sign, [T["Bsc"], Bst], [T["Bjunk"], Bst],
                                    bias=st[:, 3:4], scale=1.0, accum_out=st[:, 4:5])
                        for T in tiles:
                            st, Bst = T["st"], T["Bst"]
                            thr_cnt = float(TOPK) if not T["neg"] else float(2 * TOPK - T["svis"])
                            ts("dve", st[:, 5:6], st[:, 4:5], thr_cnt, 0.5, ALU.is_ge, ALU.subtract, [Bst], [Bst])
                        for T in tiles:
                            st, Bst, dl, Bdl = T["st"], T["Bst"], T["dl"], T["Bdl"]
                            stt("dve", st[:, 3:4], st[:, 5:6], dl[:, i:i + 1], st[:, 3:4], ALU.mult, ALU.add, [Bst, Bdl], [Bst])
                    for T in tiles:
                        st, Bst, dl, Bdl = T["st"], T["Bst"], T["dl"], T["Bdl"]
                        if not T["neg"]:
                            tt("dve", st[:, 6:7], st[:, 3:4], dl[:, NBIS:NBIS + 1], ALU.subtract, [Bst, Bdl], [Bst])
                        else:
                            stt("dve", st[:, 6:7], st[:, 3:4], -1.0, dl[:, NBIS:NBIS + 1], ALU.mult, ALU.add, [Bst, Bdl], [Bst])

                def mask_tile(T):
                    t, svis, sc, Bsc, st, Bst = T["t"], T["svis"], T["sc"], T["Bsc"], T["st"], T["Bst"]
                    if t < 2:
                        S.op("pool", lambda: nc.gpsimd.memset(st[:, 6:7], NEG / 2), writes=[Bst])
                    mk, Bmk = mskr.get()
                    ts("dve", mk[:, 0:svis], sc[:, 0:svis], st[:, 6:7], None, ALU.is_ge, None, [Bsc, Bst], [Bmk])
                    for i0 in range(0, t + 1, 4):
                        nb = min(4, t + 1 - i0)
                        tp, Btp = PS.get()
                        for bi in range(nb):
                            i = i0 + bi
                            mm(tp[:, bi * 128:(bi + 1) * 128], mk[:, i * 128:(i + 1) * 128], mat(M_ID), True, True, [Bmk, Bcmat], [Btp])
                        for bi in range(nb):
                            i = i0 + bi
                            dst = maskT[:, MOFF[i] + 128 * (t - i):MOFF[i] + 128 * (t - i) + 128]
                            act(dst, tp[:, bi * 128:(bi + 1) * 128], AF.Copy, [Btp], [BmaskT[i]])

                for tp_ in range(8 if STOP_AT >= 1.2 else 0):
                    Ts = [score_tile(2 * tp_), score_tile(2 * tp_ + 1)]
                    if STOP_AT >= 1.5:
                        bisect(Ts)
                    if STOP_AT >= 1.8:
                        for T in Ts:
                            mask_tile(T)
                barrier()
            with ExitStack() as es_tmp:
                Ct, St = make_tables(es, es_tmp, CK + 0, "b6")
            wr = Ring(nc, es, "b_w", 5, [128, 8, 128], BF16)
            wor = Ring(nc, es, "b_wo", 2, [128, 1024], BF16)
            qT = sb("b_qT", [128, SEQ], BF16, es)
            Bq = Buf("b_qT")
            kT = sb("b_kT", [128, 2, SEQ], BF16, es)
            Bk = Buf("b_kT")
            S.op("pool", lambda: nc.gpsimd.memset(kT[64:128, 0, :], 0.0), writes=[Bk])
            S.op("pool", lambda: nc.gpsimd.memset(kT[0:64, 1, :], 0.0), writes=[Bk])
            va = sb("b_va", [128, 16, 2, 128], BF16, es)
            Bva = Buf("b_va")
            sz = sb("b_sz", [128, SEQ], BF16, es)
            Bsz = Buf("b_sz")
            gT = sb("b_gT", [128, SEQ], BF16, es)
            BgT = Buf("b_gT")
            S.op("pool", lambda: nc.gpsimd.memset(va[:, :, 0, 64:128], 1.0), writes=[Bva])
            S.op("pool", lambda: nc.gpsimd.memset(va[:, :, 1, 0:64], 1.0), writes=[Bva])
            Wo = b_w_out[0]

            def mask_fn(i, q0, N):
                o = MOFF[i] + (q0 - 128 * i)
                return maskT[:, o:o + N], BmaskT[i]

            B_NR = subring([0, 1, 2, 3])
            B_ZV = subring([4])
            B_WO = subring([5, 6, 7])
            chunk_state = {}

            def proj_chunk_g(c):
                ws = []
                for g in range(4):
                    w_, Bw_ = wr.get()
                    load_w("pool", w_[:], Wv[:, :, g * 1024 + c * 128:g * 1024 + (c + 1) * 128], Bw_)
                    ws.append((w_, Bw_))
                wo, Bwo = wor.get()
                load_w("pool", wo[:], Wo[c * 128:(c + 1) * 128, :], Bwo)
                chunk_state[c] = (wo, Bwo)
                (wq, Bwq), (wk, Bwk), (wv_, Bwv), (wz, Bwz) = ws
                gens = []
                for tb in range(4):
                    for (w_, Bw_, gcl, isk) in ((wq, Bwq, CB + 8, False), (wk, Bwk, CB + 9, True)):
                        def srcp(tb=tb, w_=w_, Bw_=Bw_):
                            pb, Bpb = B_NR.get()
                            for kc in range(8):
                                mm(pb[:], w_[:, kc, :], xnT[:, kc, tsl(tb)], kc == 0, kc == 7, [Bw_, BxnT[tb]], [Bpb])
                            return pb[:], Bpb
                        if isk:
                            gens.append(norm_rope_g(srcp, 128, mat(M_B64), 64, col(gcl), mat(M_R64), Ct, St, tb, kT[0:64, 0, tsl(tb)], Bk, f32r, bfr,
                                                    out_hi=kT[64:128, 1, tsl(tb)], Bout_hi=Bk))
                        else:
                            gens.append(norm_rope_g(srcp, 128, mat(M_B64), 64, col(gcl), mat(M_R64), Ct, St, tb, qT[:, tsl(tb)], Bq, f32r, bfr))
                PROJ[0] = B_NR
                yield from run_pipe_g(gens, PIPE_DEPTH)
                PROJ[0] = PA
                for tb in range(4):
                    pb, Bpb = B_ZV.get()
                    for kc in range(8):
                        mm(pb[:], wz[:, kc, :], xnT[:, kc, tsl(tb)], kc == 0, kc == 7, [Bwz, BxnT[tb]], [Bpb])
                    act(sz[:, tsl(tb)], pb[:], AF.Silu, [Bpb], [Bsz])
                    yield
                    pv, Bpv = B_ZV.get()
                    for t4 in range(4):
                        tile_ = tb * 4 + t4
                        for kc in range(8):
                            mm(pv[:, t4 * 128:(t4 + 1) * 128], xnT[:, kc, tile_ * 128:(tile_ + 1) * 128], wv_[:, kc, :], kc == 0, kc == 7,
                               [Bwv, BxnT[tb]], [Bpv])
                    pv3 = pv[:].rearrange("p (a b) -> p a b", b=128)
                    act(va[:, tb * 4:(tb + 1) * 4, 0, 0:64], pv3[:, :, 0:64], AF.Copy, [Bpv], [Bva])
                    act(va[:, tb * 4:(tb + 1) * 4, 1, 64:128], pv3[:, :, 64:128], AF.Copy, [Bpv], [Bva])
                    yield

            heads = []
            for hh in range(2):
                lo = hh * 64
                heads.append(dict(k=(kT[:, hh, :], Bk), q=(qT[:, :], Bq),
                                  v=(lambda i, hh=hh: va[:, i, hh, :]), Bv=Bva,
                                  osl=slice(lo, lo + 64), lsl=slice(64 - lo, 128 - lo)))
            NCH = 8 if STOP_AT >= 3 else 0
            if NCH:
                for _ in proj_chunk_g(0):
                    pass
            for c in range(NCH):
                wo, Bwo = chunk_state[c]
                attention(heads, 0.125, mask_fn, gT, BgT, sz, Bsz, f32r, bfr, heat=HEAT)
                side = proj_chunk_g(c + 1) if c + 1 < NCH else None
                co_run(wout_g(wo, Bwo, gT, BgT, B_WO), side, 2)
            barrier()

    for L in layers:
        if L[0] == "A":
            layer_A(int(L[1]))
        elif L[0] == "C":
            layer_C()
        elif L[0] == "B":
            layer_B()
        else:
            raise NotImplementedError(L)

    Bout = Buf("out")
    for kc in range(8):
        S.dma("sp", yT_d[kc * 128:(kc + 1) * 128, :], xT[:, kc, :], reads=BxT[kc], own=Bout)
    S.finish([Bout])
    ges.close()
    return nc


def _pack_cols(inp):
    c = np.zeros((128, NCOLS), np.float32)

    def colmajor(v):
        return np.ascontiguousarray(v.reshape(-1, 128).T)
    for j in range(2):
        b = CA[j]
        c[:, b:b + 8] = colmajor(inp["a_norm"][j])
        for k in range(3):
            c[:, b + 8 + 8 * k:b + 16 + 8 * k] = colmajor(inp["a_conv_w"][j, k])
        c[:, b + 32:b + 40] = colmajor(inp["a_conv_b"][j])
    c[:, CB:CB + 8] = colmajor(inp["b_norm"][0])
    c[:, CB + 8] = np.tile(inp["b_q_norm"][0], 2)
    c[:, CB + 9] = np.tile(inp["b_k_norm"][0], 2)
    c[:, CC:CC + 8] = colmajor(inp["c_norm"][0])
    c[:, CC + 8:CC + 11] = colmajor(inp["c_q_lat_norm"][0])
    c[:, CC + 11:CC + 13] = colmajor(inp["c_kv_lat_norm"][0])
    c[:96, CC + 13] = inp["c_q_norm"][0]
    c[:96, CC + 14] = inp["c_k_norm"][0]
    p = np.arange(128)
    theta = 10000.0
    c[:, CK + 0] = (theta ** (-(np.arange(32, dtype=np.float32)) / 32.0)).astype(np.float32)[p % 32]
    inv16 = (theta ** (-(np.arange(16, dtype=np.float32)) / 16.0)).astype(np.float32)
    c[:, CK + 1] = np.where((p % 64) < 32, inv16[p % 16], 0.0)
    c[:, CK + 2] = np.where((p >= 64) & (p < 96), inv16[(p - 64) % 16], 0.0)
    for i in range(NBIS + 1):
        c[:, CK + 3 + i] = 2.0 ** -(i + 1)
    return c


def _const_mats():
    m = np.zeros((128, NMAT, 128), np.float32)
    m[:, M_ID, :] = np.eye(128)
    m[:, M_ONE, :] = 1.0
    m[0:64, M_B64, 0:64] = 1.0
    m[64:128, M_B64, 64:128] = 1.0
    for o in range(128):
        dd = o % 64
        if dd < 32:
            m[o + 32, M_R64, o] = -1.0
        else:
            m[o - 32, M_R64, o] = 1.0
        if dd < 16:
            m[o + 16, M_RIDX, o] = -1.0
        elif dd < 32:
            m[o - 16, M_RIDX, o] = 1.0
        if 64 <= o < 80:
            m[o + 16, M_RMLA, o] = -1.0
        elif 80 <= o < 96:
            m[o - 16, M_RMLA, o] = 1.0
    return m.reshape(128, NMAT * 128)


LAYERS = ["A0", "B0", "C0", "A1"]
_NC_CACHE = {}


def kernel(**inp):
    inp = {k: np.asarray(v) for k, v in inp.items()}
    key = tuple(LAYERS)
    if key not in _NC_CACHE:
        _NC_CACHE[key] = build(LAYERS)
    nc = _NC_CACHE[key]
    cols = _pack_cols(inp)
    cmat = _const_mats()
    x = inp["x"]
    in_maps = []
    for b in range(8):
        m = {
            "xT": np.ascontiguousarray(x[b].T),
            "pos": np.ascontiguousarray(inp["positions"][b:b + 1].astype(np.int32)),
            "cols": cols, "cmat": cmat,
        }
        for k in ("a_w_in", "a_w_out", "b_w_in", "b_w_out", "c_w_in", "c_w_uq", "c_w_ukv", "c_w_out"):
            m[k] = np.ascontiguousarray(inp[k], dtype=np.float32)
        in_maps.append(m)
    res = run_bass_kernel_spmd(nc, in_maps, core_ids=list(range(8)))
    out = np.stack([np.ascontiguousarray(res.results[b]["yT"].T) for b in range(8)], axis=0)
    return out.astype(np.float32)
```

```python
import math
from contextlib import ExitStack
import numpy as np
import concourse.bass as bass
import concourse.mybir as mybir
from concourse.bass_utils import run_bass_kernel_spmd

F32 = mybir.dt.float32
BF16 = mybir.dt.bfloat16
I32 = mybir.dt.int32
ALU = mybir.AluOpType
AF = mybir.ActivationFunctionType
AX = mybir.AxisListType

SEQ = 2048
D = 1024
EPS = 1e-6
NEG = -1.0e30
TOPK = 256
NBIS = 16
PIPE_DEPTH = 3
HEAT = 0
C_FINE = False
C_DEPTH = 3
C_RATIO = 10 ** 9
STOP_AT = 3


class Buf:
    __slots__ = ("name", "w", "r", "dsem", "dcnt")

    def __init__(self, name):
        self.name = name
        self.w = None
        self.r = []
        self.dsem = None
        self.dcnt = 0


class Sched:
    def __init__(self, nc):
        self.nc = nc
        self.eng = {"pe": nc.tensor, "act": nc.scalar, "dve": nc.vector,
                    "pool": nc.gpsimd, "sp": nc.sync}
        self.sem = {}
        self.cnt = {}
        for e in self.eng:
            self.sem[e] = nc.alloc_semaphore("c_" + e)
            self.cnt[e] = 0
        self.seen = {e: {} for e in self.eng}
        self.dsems = []
        self.free_dsems = []

    def _semof(self, key):
        return self.sem[key] if isinstance(key, str) else key

    def _deps(self, q, reads, writes, is_dma=False):
        need = {}

        def add(d, same_ok):
            if d is None:
                return
            key, val = d
            if same_ok and key == q and not is_dma and q != "pool":
                return
            if key == "pe" and q == "pe" and not is_dma:
                return
            if self.seen[q].get(key, 0) >= val:
                return
            if need.get(key, 0) < val:
                need[key] = val
        for b in reads:
            add(b.w, False)
        for b in writes:
            add(b.w, True)
            for d in b.r:
                add(d, True)
        return need

    def _emit(self, q, need, fn):
        items = list(need.items())
        eng = self.eng[q]
        for key, val in items[:-1]:
            eng.wait_ge(self._semof(key), val)
            self.seen[q][key] = val
        inst = fn()
        if items:
            key, val = items[-1]
            inst._wait_ge(self._semof(key), val)
            self.seen[q][key] = val
        return inst

    def op(self, q, fn, reads=(), writes=(), inc=True):
        need = self._deps(q, reads, writes)
        inst = self._emit(q, need, fn)
        if inc:
            self.cnt[q] += 1
            inst.then_inc(self.sem[q], 1)
            d = (q, self.cnt[q])
        else:
            d = (q, self.cnt[q] + 1)
        for b in reads:
            if len(b.r) > 24:
                b.r = self._compact(b.r)
            b.r.append(d)
        for b in writes:
            b.w = d
            b.r = []
        return inst

    @staticmethod
    def _compact(lst):
        best = {}
        for k, v in lst:
            if best.get(k, 0) < v:
                best[k] = v
        return list(best.items())

    def dma(self, q, out, in_, reads=(), writes=(), own=None):
        if own is None:
            own = writes[0] if writes else reads[0]
        if own.dsem is None:
            self.dsems.append(own)
            own.dsem = self.nc.alloc_semaphore(f"d{len(self.dsems)}_" + own.name)
        need = self._deps(q, reads, writes, is_dma=True)
        inst = self._emit(q, need, lambda: self.eng[q].dma_start(out=out, in_=in_))
        own.dcnt += 16
        inst.then_inc(own.dsem, 16)
        d = (own.dsem, own.dcnt)
        for b in reads:
            b.r.append(d)
        for b in writes:
            b.w = d
            b.r = []
        return inst

    def finish(self, bufs):
        need = {}
        for b in bufs:
            for d in ([b.w] if b.w else []) + list(b.r):
                key, val = d
                if need.get(key, 0) < val:
                    need[key] = val
        for key, val in need.items():
            self.eng["sp"].wait_ge(self._semof(key), val)


class Ring:
    def __init__(self, nc, es, name, n, shape, dtype, psum=False):
        self.items = []
        for i in range(n):
            nm = f"{name}{i}"
            if psum:
                t = es.enter_context(nc.psum_tensor(nm, shape, dtype))
            else:
                t = es.enter_context(nc.sbuf_tensor(nm, shape, dtype))
            self.items.append((t, Buf(nm)))
        self.i = 0

    def get(self):
        it = self.items[self.i % len(self.items)]
        self.i += 1
        return it


CA = [0, 40]
CB = 80
CC = 90
CK = 105
NCOLS = 108 + NBIS + 1
M_ID, M_ONE, M_B64, M_R64, M_RIDX, M_RMLA = 0, 1, 2, 3, 4, 5
NMAT = 6


def tsl(tb):
    return slice(tb * 512, (tb + 1) * 512)


def build(layers):
    nc = bass.Bass("TRN2", target_bir_lowering=False)
    S = Sched(nc)

    def din(name, shape, dt=F32):
        return nc.dram_tensor(name, shape, dt, kind="ExternalInput")

    xT_d = din("xT", [D, SEQ]).ap()
    pos_d = din("pos", [1, SEQ], I32)
    cols_d = din("cols", [128, NCOLS]).ap()
    cmat_d = din("cmat", [128, NMAT * 128]).ap()
    a_w_in = din("a_w_in", [2, 1024, 4096]).ap()
    a_w_out = din("a_w_out", [2, 1024, 1024]).ap()
    b_w_in = din("b_w_in", [1, 1024, 4680]).ap()
    b_w_out = din("b_w_out", [1, 1024, 1024]).ap()
    c_w_in = din("c_w_in", [1, 1024, 1696]).ap()
    c_w_uq = din("c_w_uq", [1, 384, 1536]).ap()
    c_w_ukv = din("c_w_ukv", [1, 256, 2048]).ap()
    c_w_out = din("c_w_out", [1, 1024, 1024]).ap()
    yT_d = nc.dram_tensor("yT", [D, SEQ], F32, kind="ExternalOutput").ap()

    ges = ExitStack()

    def sb(name, shape, dt, es=None):
        return (es or ges).enter_context(nc.sbuf_tensor(name, shape, dt))

    xT = sb("xT_sb", [128, 8, SEQ], F32)
    BxT = [[Buf(f"xT{k}_{t}") for t in range(4)] for k in range(8)]
    xnT = sb("xnT", [128, 8, SEQ], BF16)
    BxnT = [Buf(f"xnT{t}") for t in range(4)]
    cols = sb("cols_sb", [128, NCOLS], F32)
    Bcols = Buf("cols")
    cmat = sb("cmat_sb", [128, NMAT, 128], BF16)
    Bcmat = Buf("cmat")
    PS = Ring(nc, ges, "psg", 4, [128, 512], F32, psum=True)
    PO = Ring(nc, ges, "pso", 4, [128, 512], F32, psum=True)

    PA = Ring.__new__(Ring)
    PA.items = PS.items + PO.items
    PA.i = 0
    _all8 = list(PA.items)
    PA.items = _all8[0:7]
    PO.items = _all8[4:7]

    def subring(idx):
        r = Ring.__new__(Ring)
        r.items = [_all8[i] for i in idx]
        r.i = 0
        return r
    PROJ = [PA]
    T1ENG = ["dve"]
    FINE = [False]
    HEATBANK = _all8[7][0]
    heat_l = cmat[:, M_ONE, :]
    heat_r = cmat[:, 0:4, :].rearrange("p a b -> p (a b)")

    def col(i):
        return cols[:, i:i + 1]

    def mat(i, p=128, m=128):
        return cmat[0:p, i, 0:m]

    ALLDMA = []
    _orig_dma = S.dma

    def dma_track(q, out, in_, reads=(), writes=(), own=None):
        o = own if own is not None else (writes[0] if writes else reads[0])
        if o not in ALLDMA:
            ALLDMA.append(o)
        return _orig_dma(q, out, in_, reads=reads, writes=writes, own=own)
    S.dma = dma_track

    for kc in range(8):
        S.dma("sp", xT[:, kc, :], xT_d[kc * 128:(kc + 1) * 128, :], writes=BxT[kc])
    S.dma("sp", cols[:], cols_d[:, :], writes=[Bcols])
    S.dma("pool", cmat[:].rearrange("p a b -> p (a b)"), cmat_d[:, :], writes=[Bcmat])

    def mm(out, lhsT, rhs, start, stop, reads, writes, inc=None):
        if inc is None:
            inc = stop
        S.op("pe", lambda: nc.tensor.matmul(out, lhsT, rhs, start=start, stop=stop),
             reads=reads, writes=writes, inc=inc)

    def act(out, in_, func, reads, writes, **kw):
        S.op("act", lambda: nc.scalar.activation(out=out, in_=in_, func=func, **kw),
             reads=reads, writes=writes)

    def tt(q, out, in0, in1, op, reads, writes):
        e = S.eng[q]
        S.op(q, lambda: e.tensor_tensor(out=out, in0=in0, in1=in1, op=op), reads=reads, writes=writes)

    def ts(q, out, in0, s1, s2, op0, op1, reads, writes, **kw):
        e = S.eng[q]
        if s2 is None:
            S.op(q, lambda: e.tensor_scalar(out=out, in0=in0, scalar1=s1, scalar2=None, op0=op0, **kw),
                 reads=reads, writes=writes)
        else:
            S.op(q, lambda: e.tensor_scalar(out=out, in0=in0, scalar1=s1, scalar2=s2, op0=op0, op1=op1, **kw),
                 reads=reads, writes=writes)

    def stt(q, out, in0, scalar, in1, op0, op1, reads, writes):
        e = S.eng[q]
        S.op(q, lambda: e.scalar_tensor_tensor(out=out, in0=in0, scalar=scalar, in1=in1, op0=op0, op1=op1),
             reads=reads, writes=writes)

    def rstd_from_ssq(ssq_ps, Bssq, P, n, N, f32r):
        lnv, Bl = f32r.get()
        act(lnv[0:P, 0:N], ssq_ps, AF.Ln, [Bssq], [Bl], scale=1.0 / n, bias=EPS)
        rs, Br = f32r.get()
        act(rs[0:P, 0:N], lnv[0:P, 0:N], AF.Exp, [Bl], [Br], scale=-0.5)
        return rs, Br

    def emit_norm(gc, f32r, sqr):
        for tb in range(4):
            pb, Bpb = PS.get()
            for kc in range(8):
                sq, Bsq = sqr.get()
                act(sq[:], xT[:, kc, tsl(tb)], AF.Square, [BxT[kc][tb]], [Bsq])
                mm(pb[:], mat(M_ONE), sq[:], kc == 0, kc == 7, [Bsq, Bcmat], [Bpb], inc=True)
            rs, Br = rstd_from_ssq(pb[:], Bpb, 128, 1024.0, 512, f32r)
            for kc in range(8):
                stt("dve", xnT[:, kc, tsl(tb)], xT[:, kc, tsl(tb)], col(gc + kc), rs[:], ALU.mult, ALU.mult,
                    [BxT[kc][tb], Br, Bcols], [BxnT[tb]])

    def wout_partial(wo_ap, Bwo, g_ap, Bg, first_last=None):
        for dc in range(8):
            for tb in range(4):
                pb, Bpb = PS.get()
                mm(pb[:], wo_ap[:, dc * 128:(dc + 1) * 128], g_ap[:, tsl(tb)], True, True, [Bwo, Bg], [Bpb])
                tt("dve", xT[:, dc, tsl(tb)], xT[:, dc, tsl(tb)], pb[:], ALU.add,
                   [BxT[dc][tb], Bpb], [BxT[dc][tb]])

    def wout_g(wo_ap, Bwo, g_ap, Bg, ring):
        items = [(dc, tb) for dc in range(8) for tb in range(4)]
        look = len(ring.items) - 1

        def issue(k):
            dc, tb = items[k]
            pb, Bpb = ring.get()
            mm(pb[:], wo_ap[:, dc * 128:(dc + 1) * 128], g_ap[:, tsl(tb)], True, True, [Bwo, Bg], [Bpb])
            return pb, Bpb
        pend = [issue(k) for k in range(look)]
        for k, (dc, tb) in enumerate(items):
            pb, Bpb = pend.pop(0)
            if k + look < len(items):
                pend.append(issue(k + look))
            tt("dve", xT[:, dc, tsl(tb)], xT[:, dc, tsl(tb)], pb[:], ALU.add,
               [BxT[dc][tb], Bpb], [BxT[dc][tb]])
            yield

    def layer_A(j):
        cb = CA[j]
        with ExitStack() as es:
            f32r = Ring(nc, es, f"a{j}f32_", 6, [128, 512], F32)
            sqr = Ring(nc, es, f"a{j}sq_", 3, [128, 512], BF16)
            emit_norm(cb, f32r, sqr)
            gT = sb(f"a{j}_gT", [128, 8, SEQ], BF16, es)
            BgT = [[Buf(f"gT{c}_{t}") for t in range(4)] for c in range(8)]
            wout = sb(f"a{j}_wout", [128, 8, 1024], BF16, es)
            Bwout = Buf("a_wout")
            S.dma("pool", wout[:], a_w_out[j].rearrange("(cc p) d -> p cc d", p=128), writes=[Bwout])
            war = [(sb(f"a{j}_w{i}", [128, 8, 4, 128], BF16, es), [Buf(f"a{j}_w{i}_{g}") for g in range(4)]) for i in range(2)]
            stg = Ring(nc, es, f"a{j}_stg", 2, [128, 8, 128], F32)
            ur = [(sb(f"a{j}_u{i}", [128, 2 + SEQ], F32, es), [Buf(f"a_u{i}_{t}") for t in range(5)]) for i in range(2)]
            wv = a_w_in[j].rearrange("(kc p) (g c i) -> p kc g c i", p=128, g=4, c=8)

            def issue_loads(c):
                wa, Bwa = war[c % 2]
                for g in (0, 1):
                    S.dma("pool", wa[:, :, g, :], wv[:, :, g, c, :], writes=[Bwa[g]])
                stgs = []
                for g in (2, 3):
                    st_, Bst_ = stg.get()
                    S.dma("sp", st_[:], wv[:, :, g, c, :], writes=[Bst_])
                    stgs.append((g, st_, Bst_))
                return stgs

            def issue_casts(c, stgs):
                wa, Bwa = war[c % 2]
                for g, st_, Bst_ in stgs:
                    act(wa[:, :, g, :], st_[:], AF.Copy, [Bst_], [Bwa[g]])

            issue_casts(0, issue_loads(0))
            for c in range(8):
                wa, Bwa = war[c % 2]
                nxt = issue_loads(c + 1) if c + 1 < 8 else None
                u, Bu = ur[c % 2]
                S.op("pool", lambda: nc.gpsimd.memset(u[:, 0:2], 0.0), writes=[Bu[4]])
                for tb in range(4):
                    banks = [_all8[(tb % 2) * 4 + g_] for g_ in range(4)]
                    for g in range(4):
                        for kc in range(8):
                            mm(banks[g][0][:], wa[:, kc, g, :], xnT[:, kc, tsl(tb)], kc == 0, kc == 7,
                               [Bwa[g], BxnT[tb]], [banks[g][1]])
                    (bg, Bbg), (cg, Bcg), (hv, Bhv), (z, Bz) = banks
                    hvs, Bhvs = f32r.get()
                    act(hvs[:], hv[:], AF.Copy, [Bhv], [Bhvs])
                    us = slice(2 + tb * 512, 2 + (tb + 1) * 512)
                    tt("dve", u[:, us], cg[:], hvs[:], ALU.mult, [Bcg, Bhvs], [Bu[tb]])
                    prev = Bu[tb - 1] if tb > 0 else Bu[4]
                    y, By = f32r.get()
                    ts("pool", y[:], u[:, us], col(cb + 24 + c), col(cb + 32 + c), ALU.mult, ALU.add,
                       [Bu[tb], Bcols], [By])
                    stt("dve", y[:], u[:, 1 + tb * 512:1 + (tb + 1) * 512], col(cb + 16 + c), y[:], ALU.mult, ALU.add,
                        [Bu[tb], prev, Bcols, By], [By])
                    stt("dve", y[:], u[:, tb * 512:(tb + 1) * 512], col(cb + 8 + c), y[:], ALU.mult, ALU.add,
                        [Bu[tb], prev, Bcols, By], [By])
                    szs, Bszs = f32r.get()
                    act(szs[:], z[:], AF.Silu, [Bz], [Bszs])
                    tt("dve", szs[:], bg[:], szs[:], ALU.mult, [Bbg, Bszs], [Bszs])
                    tt("pool", gT[:, c, tsl(tb)], szs[:], y[:], ALU.mult, [Bszs, By], [BgT[c][tb]])
                    if tb == 1 and nxt is not None:
                        issue_casts(c + 1, nxt)
            for dc in range(8):
                for tb in range(4):
                    pb, Bpb = PS.get()
                    for cc in range(8):
                        mm(pb[:], wout[:, cc, dc * 128:(dc + 1) * 128], gT[:, cc, tsl(tb)], cc == 0, cc == 7,
                           [Bwout, BgT[cc][tb]], [Bpb])
                    tt("dve", xT[:, dc, tsl(tb)], xT[:, dc, tsl(tb)], pb[:], ALU.add,
                       [BxT[dc][tb], Bpb], [BxT[dc][tb]])
            barrier()

    def barrier():
        for q in ("pe", "act", "dve", "pool"):
            pass
        for q in S.eng:
            for e in S.eng:
                if e == q:
                    continue
                v = S.cnt[e]
                if v > 0 and S.seen[q].get(e, 0) < v:
                    S.eng[q].wait_ge(S.sem[e], v)
                    S.seen[q][e] = v
        for b in ALLDMA:
            if b.dsem is not None and b.dcnt > 0:
                for q in S.eng:
                    if S.seen[q].get(b.dsem, 0) < b.dcnt:
                        S.eng[q].wait_ge(b.dsem, b.dcnt)
                        S.seen[q][b.dsem] = b.dcnt


    def make_tables(es_tab, es_tmp, invc, name):
        tabs = []
        for nm in ("C", "S"):
            tabs.append((sb(name + "_" + nm, [128, SEQ], F32, es_tab), Buf(name + nm)))
        posi = sb(name + "_posi", [128, 512], I32, es_tmp)
        Bposi = Buf(name + "posi")
        a2 = sb(name + "_a2", [128, 512], F32, es_tmp)
        Ba2 = Buf(name + "a2")
        t1 = sb(name + "_t1", [128, 512], F32, es_tmp)
        Bt1 = Buf(name + "t1")
        ki = sb(name + "_ki", [128, 512], I32, es_tmp)
        Bki = Buf(name + "ki")
        for tb in range(4):
            S.dma("sp", posi[:], bass.AP(pos_d, tb * 512, [[0, 128], [1, 512]]), writes=[Bposi])
            S.op("dve", lambda: nc.vector.tensor_copy(out=a2[:], in_=posi[:]), reads=[Bposi], writes=[Ba2])
            ts("dve", a2[:], a2[:], col(invc), 1.0 / (2 * math.pi), ALU.mult, ALU.mult, [Ba2, Bcols], [Ba2])
            for (tab, Btab), c0 in zip(tabs, (0.25, 0.0)):
                ts("dve", t1[:], a2[:], c0, None, ALU.add, None, [Ba2], [Bt1])
                S.op("dve", lambda: nc.vector.tensor_copy(out=ki[:], in_=t1[:]), reads=[Bt1], writes=[Bki])
                S.op("dve", lambda: nc.vector.tensor_copy(out=tab[:, tsl(tb)], in_=ki[:]), reads=[Bki], writes=[Btab])
                tt("dve", t1[:], t1[:], tab[:, tsl(tb)], ALU.subtract, [Bt1, Btab], [Bt1])
                stt("dve", t1[:], t1[:], 0.5, t1[:], ALU.is_gt, ALU.subtract, [Bt1], [Bt1])
                act(tab[:, tsl(tb)], t1[:], AF.Sin, [Bt1], [Btab], scale=-2.0 * math.pi)
        barrier()
        return tabs

    def norm_rope_g(srcfn, P, blk, n, gcol, rot, Ct, St, tb, out, Bout, f32r, bfr, eng2="pool", out_hi=None, Bout_hi=None):
        (C, BC), (Sn, BS) = Ct, St
        fine = FINE[0]
        src, Bsrc = srcfn()
        if fine:
            yield
        if n is not None:
            sq, Bsq = bfr.get()
            act(sq[0:P, :], src, AF.Square, [Bsrc], [Bsq])
            yield
            pb, Bpb = PROJ[0].get()
            mm(pb[0:P, :], blk, sq[0:P, :], True, True, [Bsq, Bcmat], [Bpb])
            if fine:
                yield
            rs, Br = rstd_from_ssq(pb[0:P, :], Bpb, P, float(n), 512, f32r)
            yield
            xn, Bxn = bfr.get()
            stt("dve", xn[0:P, :], src, gcol, rs[0:P, :], ALU.mult, ALU.mult, [Bsrc, Br, Bcols], [Bxn])
        else:
            yield
            xn, Bxn = bfr.get()
            act(xn[0:P, :], src, AF.Copy, [Bsrc], [Bxn])
        if fine:
            yield
        rp, Brp = PROJ[0].get()
        mm(rp[0:P, :], rot, xn[0:P, :], True, True, [Bxn, Bcmat], [Brp])
        yield
        t1, Bt1 = f32r.get()
        tt(T1ENG[0], t1[0:P, :], xn[0:P, :], C[0:P, tsl(tb)], ALU.mult, [Bxn, BC], [Bt1])
        t2, Bt2 = f32r.get()
        tt("dve", t2[0:P, :], rp[0:P, :], Sn[0:P, tsl(tb)], ALU.mult, [Brp, BS], [Bt2])
        if fine:
            yield
        if out_hi is None:
            tt(eng2, out, t1[0:P, :], t2[0:P, :], ALU.add, [Bt1, Bt2], [Bout])
        else:
            tt(eng2, out, t1[0:64, :], t2[0:64, :], ALU.add, [Bt1, Bt2], [Bout])
            tt(eng2, out_hi, t1[64:128, :], t2[64:128, :], ALU.add, [Bt1, Bt2], [Bout_hi])

    def run_pipe_g(gens, depth):
        active = []
        it = iter(gens)
        more = True
        while True:
            for g in list(active):
                try:
                    next(g)
                except StopIteration:
                    active.remove(g)
            if more and len(active) < depth:
                try:
                    g = next(it)
                    active.append(g)
                    try:
                        next(g)
                    except StopIteration:
                        active.remove(g)
                except StopIteration:
                    more = False
            if not active and not more:
                break
            yield

    def run_pipe(gens, depth):
        for _ in run_pipe_g(gens, depth):
            pass

    def co_run(main, side, ratio):
        k = 0
        main_done = side is None and False
        while True:
            try:
                next(main)
            except StopIteration:
                break
            k += 1
            if side is not None and ratio < 0:
                for _ in range(-ratio):
                    try:
                        next(side)
                    except StopIteration:
                        side = None
                        break
            elif side is not None and k % ratio == 0:
                try:
                    next(side)
                except StopIteration:
                    side = None
        if side is not None:
            for _ in side:
                pass

    def attention_g(heads, scale, mask_fn, gT, BgT, sz, Bsz, f32r, ptr, st_ring=None, o_ring=None, heat=0, recip_dve=False):
        st_ring = st_ring or PS
        o_ring = o_ring or PO
        for j in range(4):
            for hd in heads:
                kT, Bk = hd["k"]
                qT, Bq = hd["q"]
                osl, lsl = hd["osl"], hd["lsl"]
                O, BO = o_ring.get()
                last = 4 * j + 3

                def issue_st(i):
                    off = max(0, 128 * (i - 4 * j))
                    N = 512 - off
                    q0 = 512 * j + off
                    st, Bst = st_ring.get()
                    mm(st[:, 0:N], kT[:, i * 128:(i + 1) * 128], qT[:, q0:q0 + N], True, True, [Bk, Bq], [Bst])
                    return st, Bst, off, N, q0
                LOOK = len(st_ring.items) - 1
                pend = [issue_st(i) for i in range(min(LOOK, last + 1))]
                for i in range(last + 1):
                    st, Bst, off, N, q0 = pend.pop(0)
                    if i + LOOK <= last:
                        pend.append(issue_st(i + LOOK))
                    pt, Bpt = ptr.get()
                    act(pt[:, 0:N], st[:, 0:N], AF.Exp, [Bst], [Bpt], scale=scale)
                    if mask_fn is not None:
                        m_ap, Bm = mask_fn(i, q0, N)
                        tt("dve", pt[:, 0:N], pt[:, 0:N], m_ap, ALU.mult, [Bpt, Bm], [Bpt])
                    elif i >= 4 * j:
                        S.op("pool", lambda: nc.gpsimd.memset(pt[64:128, 0:64], 0.0), writes=[Bpt])
                    mm(O[:, off:512], hd["v"](i), pt[:, 0:N], i == 0, i == last, [hd["Bv"], Bpt], [BO], inc=True)
                    for _ in range(heat):
                        nc.tensor.matmul(HEATBANK[:, 0:512], heat_l, heat_r, start=True, stop=True)
                    yield
                rl, Brl = f32r.get()
                if recip_dve and j < 3:
                    S.op("dve", lambda: nc.vector.reciprocal(out=rl[lsl, :], in_=O[lsl, :]), reads=[BO], writes=[Brl])
                else:
                    act(rl[lsl, :], O[lsl, :], AF.Ln, [BO], [Brl])
                    act(rl[lsl, :], rl[lsl, :], AF.Exp, [Brl], [Brl], scale=-1.0)
                tmp, Btmp = f32r.get()
                tt("dve", tmp[osl, :], O[osl, :], rl[lsl, :], ALU.mult, [BO, Brl], [Btmp])
                tt("pool", gT[osl, tsl(j)], tmp[osl, :], sz[osl, tsl(j)], ALU.mult, [Btmp, Bsz], [BgT])

    def attention(heads, scale, mask_fn, gT, BgT, sz, Bsz, f32r, ptr, heat=0):
        for _ in attention_g(heads, scale, mask_fn, gT, BgT, sz, Bsz, f32r, ptr, heat=heat):
            pass

    def load_w(q, dst, src, Bw):
        S.dma(q, dst, src, writes=[Bw])

    def layer_C():
        W = c_w_in[0]
        with ExitStack() as es:
            f32r = Ring(nc, es, "cf32_", 6, [128, 512], F32)
            bfr = Ring(nc, es, "cbf_", 6, [128, 512], BF16)
            emit_norm(CC, f32r, bfr)
            with ExitStack() as es_tmp:
                Ct, St = make_tables(es, es_tmp, CK + 2, "ct")
            cqn = sb("c_cqn", [128, 3, SEQ], BF16, es)
            Bcqn = [Buf(f"cqn{t}") for t in range(4)]
            ckvn = sb("c_ckvn", [128, 2, SEQ], BF16, es)
            Bckvn = [Buf(f"ckvn{t}") for t in range(4)]
            krT = sb("c_krT", [128, SEQ], F32, es)
            BkrT = [Buf(f"krT{t}") for t in range(4)]
            with ExitStack() as es1:
                wcq = sb("c_wcq", [128, 8, 384], BF16, es1)
                Bwcq = Buf("wcq")
                wckv = sb("c_wckv", [128, 8, 256], BF16, es1)
                Bwckv = Buf("wckv")
                wkr = sb("c_wkr", [128, 8, 96], BF16, es1)
                Bwkr = Buf("wkr")
                Wv = W.rearrange("(kc p) n -> p kc n", p=128)
                load_w("pool", wcq[:], Wv[:, :, 0:384], Bwcq)
                load_w("pool", wckv[:], Wv[:, :, 384:640], Bwckv)
                S.op("pool", lambda: nc.gpsimd.memset(wkr[:], 0.0), writes=[Bwkr])
                load_w("pool", wkr[:, :, 64:96], Wv[:, :, 640:672], Bwkr)
                for tb in range(4):
                    for (wt, Bwt, nch, dst, Bdst, gc, nn) in ((wcq, Bwcq, 3, cqn, Bcqn, CC + 8, 384.0),
                                                              (wckv, Bwckv, 2, ckvn, Bckvn, CC + 11, 256.0)):
                        banks = [PS.get() for _ in range(nch)]
                        for ch in range(nch):
                            for kc in range(8):
                                mm(banks[ch][0][:], wt[:, kc, ch * 128:(ch + 1) * 128], xnT[:, kc, tsl(tb)], kc == 0, kc == 7,
                                   [Bwt, BxnT[tb]], [banks[ch][1]])
                        ssq, Bssq = PO.get()
                        for ch in range(nch):
                            sq, Bsq = bfr.get()
                            act(sq[:], banks[ch][0][:], AF.Square, [banks[ch][1]], [Bsq])
                            mm(ssq[:], mat(M_ONE), sq[:], ch == 0, ch == nch - 1, [Bsq, Bcmat], [Bssq], inc=True)
                        rs, Br = rstd_from_ssq(ssq[:], Bssq, 128, nn, 512, f32r)
                        for ch in range(nch):
                            stt("dve", dst[:, ch, tsl(tb)], banks[ch][0][:], col(gc + ch), rs[:], ALU.mult, ALU.mult,
                                [banks[ch][1], Br, Bcols], [Bdst[tb]])
                    pb, Bpb = PS.get()
                    for kc in range(8):
                        mm(pb[0:96, :], wkr[:, kc, :], xnT[:, kc, tsl(tb)], kc == 0, kc == 7, [Bwkr, BxnT[tb]], [Bpb])
                    act(krT[64:96, tsl(tb)], pb[64:96, :], AF.Copy, [Bpb], [BkrT[tb]])
                barrier()
            wzr = Ring(nc, es, "c_wz", 2, [128, 8, 128], BF16)
            wor = Ring(nc, es, "c_wo", 2, [128, 1024], BF16)
            wuqr = Ring(nc, es, "c_wuq", 2, [128, 3, 96], BF16)
            wukvr = Ring(nc, es, "c_wukv", 2, [128, 2, 128], BF16)
            qr = Ring(nc, es, "c_q", 2, [128, SEQ], BF16)
            kr_ = Ring(nc, es, "c_k", 2, [128, SEQ], BF16)
            vr = Ring(nc, es, "c_v", 2, [128, 16, 128], BF16)
            szr = Ring(nc, es, "c_sz", 1, [128, SEQ], BF16)
            gr = Ring(nc, es, "c_g", 1, [128, SEQ], BF16)
            ptr = Ring(nc, es, "c_pt", 4, [128, 512], BF16)
            for sl_, (vt, Bv) in enumerate(vr.items):
                lo = 64 if sl_ == 0 else 0
                S.op("pool", lambda: nc.gpsimd.memset(vt[:, :, lo:lo + 64], 1.0), writes=[Bv])
            Wz = W.rearrange("(kc p) n -> p kc n", p=128)
            Wuq = c_w_uq[0].rearrange("(kc p) n -> p kc n", p=128)
            Wukv = c_w_ukv[0].rearrange("(kc p) n -> p kc n", p=128)
            Wo = c_w_out[0]
            scale = 96.0 ** -0.5
            C_ST = subring([0, 1, 2, 3])
            C_O = subring([4, 5])
            C_PJ = subring([4, 5, 6, 7])
            hds = {}

            def head_proj_g(h):
                hh = h % 2
                wuq, Bwuq = wuqr.get()
                load_w("pool", wuq[:], Wuq[:, :, h * 96:(h + 1) * 96], Bwuq)
                wukv, Bwukv = wukvr.get()
                load_w("pool", wukv[:], Wukv[:, :, h * 128:(h + 1) * 128], Bwukv)
                qT, Bq = qr.get()
                kT, Bk = kr_.get()
                vt, Bv = vr.items[hh]
                vlo = 0 if hh == 0 else 64
                hds[h] = dict(k=(kT[0:96, :], Bk), q=(qT[0:96, :], Bq), v=(lambda i, vt=vt: vt[:, i, :]), Bv=Bv,
                              osl=slice(vlo, vlo + 64), lsl=slice(64 - vlo, 128 - vlo))
                gens = []
                for tb in range(4):
                    def srcq(tb=tb):
                        pq, Bpq = C_PJ.get()
                        for kc in range(3):
                            mm(pq[0:96, :], wuq[:, kc, :], cqn[:, kc, tsl(tb)], kc == 0, kc == 2, [Bwuq, Bcqn[tb]], [Bpq])
                        return pq[0:96, :], Bpq
                    gens.append(norm_rope_g(srcq, 96, mat(M_ONE, 96, 96), 96, cols[0:96, CC + 13:CC + 14], mat(M_RMLA, 96, 96),
                                            Ct, St, tb, qT[0:96, tsl(tb)], Bq, f32r, bfr))

                    def srck(tb=tb):
                        pk, Bpk = C_PJ.get()
                        for kc in range(2):
                            mm(pk[0:64, :], wukv[:, kc, 0:64], ckvn[:, kc, tsl(tb)], kc == 0, kc == 1, [Bwukv, Bckvn[tb]], [Bpk])
                        act(pk[64:96, :], krT[64:96, tsl(tb)], AF.Copy, [BkrT[tb]], [Bpk])
                        return pk[0:96, :], Bpk
                    gens.append(norm_rope_g(srck, 96, mat(M_ONE, 96, 96), 96, cols[0:96, CC + 14:CC + 15], mat(M_RMLA, 96, 96),
                                            Ct, St, tb, kT[0:96, tsl(tb)], Bk, f32r, bfr))
                yield from run_pipe_g(gens, C_DEPTH)
                for g in range(2):
                    pv, Bpv = C_PJ.get()
                    for t8 in range(8):
                        tile_ = g * 8 + t8
                        for kc in range(2):
                            mm(pv[:, t8 * 64:(t8 + 1) * 64], ckvn[:, kc, tile_ * 128:(tile_ + 1) * 128], wukv[:, kc, 64:128],
                               kc == 0, kc == 1, [Bwukv, Bckvn[tile_ // 4]], [Bpv])
                    act(vt[:, g * 8:(g + 1) * 8, vlo:vlo + 64], pv[:].rearrange("p (a b) -> p a b", b=64), AF.Copy, [Bpv], [Bv])
                    yield

            PROJ[0] = C_PJ
            T1ENG[0] = "pool"
            for _ in head_proj_g(0):
                pass
            wo = Bwo = sz = Bsz = gT = BgT = None
            C_WO = subring([0, 1, 2])
            for h in range(16):
                c, hh = divmod(h, 2)
                if hh == 0:
                    wz, Bwz = wzr.get()
                    load_w("pool", wz[:], Wz[:, :, 672 + c * 128:672 + (c + 1) * 128], Bwz)
                    wo, Bwo = wor.get()
                    load_w("pool", wo[:], Wo[c * 128:(c + 1) * 128, :], Bwo)
                    sz, Bsz = szr.get()
                    for tb in range(4):
                        pb, Bpb = C_PJ.get()
                        for kc in range(8):
                            mm(pb[:], wz[:, kc, :], xnT[:, kc, tsl(tb)], kc == 0, kc == 7, [Bwz, BxnT[tb]], [Bpb])
                        act(sz[:, tsl(tb)], pb[:], AF.Silu, [Bpb], [Bsz])
                    gT, BgT = gr.get()
                main = attention_g([hds[h]], scale, None, gT, BgT, sz, Bsz, f32r, ptr, st_ring=C_ST, o_ring=C_O, recip_dve=True)
                for _ in main:
                    pass
                side = head_proj_g(h + 1) if h + 1 < 16 else None
                if hh == 1:
                    co_run(wout_g(wo, Bwo, gT, BgT, C_WO), side, 2)
                elif side is not None:
                    for _ in side:
                        pass
            PROJ[0] = PA
            T1ENG[0] = "dve"
            barrier()


    def layer_B():
        Wv = b_w_in[0].rearrange("(kc p) n -> p kc n", p=128)
        MOFF = [sum(2048 - 128 * ii for ii in range(i)) for i in range(16)]
        with ExitStack() as es:
            f32r = Ring(nc, es, "bf32_", 4, [128, 512], F32)
            bfr = Ring(nc, es, "bbf_", 6, [128, 512], BF16)
            emit_norm(CB, f32r, bfr)
            maskT = sb("b_maskT", [128, 17408], BF16, es)
            BmaskT = [Buf(f"maskT{i}") for i in range(16)]
            with ExitStack() as es2:
                qiT = sb("b_qiT", [128, 4, SEQ], BF16, es2)
                BqiT = [Buf(f"qiT{t}") for t in range(4)]
                kiT2 = sb("b_kiT2", [128, 2, SEQ], BF16, es2)
                BkiT = [Buf(f"kiT{t}") for t in range(4)]
                S.op("pool", lambda: nc.gpsimd.memset(kiT2[64:128, 0, :], 0.0), writes=BkiT)
                S.op("pool", lambda: nc.gpsimd.memset(kiT2[0:64, 1, :], 0.0), writes=BkiT)
                wi_sb = sb("b_wi", [128, 16, 8], F32, es2)
                Bwi = Buf("wi")
                with ExitStack() as es_w:
                    with ExitStack() as es_tmp:
                        Ct, St = make_tables(es_w, es_tmp, CK + 1, "bi")
                    wqi = sb("b_wqi", [128, 8, 512], BF16, es_w)
                    Bwqi = Buf("wqi")
                    wki2 = sb("b_wki2", [128, 8, 128], BF16, es_w)
                    Bwki2 = Buf("wki2")
                    wwi = sb("b_wwi", [128, 8, 8], BF16, es_w)
                    Bwwi = Buf("wwi")
                    load_w("pool", wqi[:], Wv[:, :, 4096:4608], Bwqi)
                    load_w("pool", wki2[:, :, 0:64], Wv[:, :, 4608:4672], Bwki2)
                    load_w("pool", wki2[:, :, 64:128], Wv[:, :, 4608:4672], Bwki2)
                    load_w("pool", wwi[:], Wv[:, :, 4672:4680], Bwwi)
                    gens = []
                    for tb in range(4):
                        for ch in range(4):
                            def srcqi(tb=tb, ch=ch):
                                pb, Bpb = PA.get()
                                for kc in range(8):
                                    mm(pb[:], wqi[:, kc, ch * 128:(ch + 1) * 128], xnT[:, kc, tsl(tb)], kc == 0, kc == 7,
                                       [Bwqi, BxnT[tb]], [Bpb])
                                return pb[:], Bpb
                            gens.append(norm_rope_g(srcqi, 128, None, None, None, mat(M_RIDX), Ct, St, tb, qiT[:, ch, tsl(tb)], BqiT[tb], f32r, bfr))

                        def srcki(tb=tb):
                            pb, Bpb = PA.get()
                            for kc in range(8):
                                mm(pb[:], wki2[:, kc, :], xnT[:, kc, tsl(tb)], kc == 0, kc == 7, [Bwki2, BxnT[tb]], [Bpb])
                            return pb[:], Bpb
                        gens.append(norm_rope_g(srcki, 128, None, None, None, mat(M_RIDX), Ct, St, tb, kiT2[0:64, 0, tsl(tb)], BkiT[tb], f32r, bfr,
                                                out_hi=kiT2[64:128, 1, tsl(tb)], Bout_hi=BkiT[tb]))
                    run_pipe(gens, PIPE_DEPTH)
                    for tb in range(4):
                        pw, Bpw = PO.get()
                        for t4 in range(4):
                            tile_ = tb * 4 + t4
                            for kc in range(8):
                                mm(pw[:, t4 * 8:(t4 + 1) * 8], xnT[:, kc, tile_ * 128:(tile_ + 1) * 128], wwi[:, kc, :], kc == 0, kc == 7,
                                   [Bwwi, BxnT[tb]], [Bpw])
                        ts("dve", wi_sb[:, tb * 4:(tb + 1) * 4, :], pw[:, 0:32].rearrange("p (a b) -> p a b", b=8),
                           (8.0 ** -0.5) * (64.0 ** -0.5), None, ALU.mult, None, [Bpw], [Bwi])
                    barrier()
                scr = Ring(nc, es2, "b_sc", 2, [128, SEQ], F32)
                junkr = Ring(nc, es2, "b_junk", 2, [128, SEQ], BF16)
                mskr = Ring(nc, es2, "b_msk", 1, [128, SEQ], BF16)
                dgr = Ring(nc, es2, "b_dg", 2, [128, 8, 128], BF16)
                str_ = Ring(nc, es2, "b_st", 2, [128, 8], F32)
                dlr = Ring(nc, es2, "b_dl", 2, [128, NBIS + 1], F32)
                ev = [0]

                def score_tile(t):
                    svis = 128 * (t + 1)
                    sc, Bsc = scr.get()
                    dg, Bdg = dgr.get()
                    for h in range(8):
                        ts("dve", dg[:, h, :], mat(M_ID), wi_sb[:, t, h:h + 1], None, ALU.mult, None, [Bwi, Bcmat], [Bdg])
                    nsb = (svis + 511) // 512
                    units = [(sbk, h) for sbk in range(nsb) for h in range(8)]

                    def issue_rp(sbk, h):
                        w = min(512, svis - 512 * sbk)
                        rp, Brp = PS.get()
                        mm(rp[:, 0:w], qiT[:, h // 2, t * 128:(t + 1) * 128], kiT2[:, h % 2, sbk * 512:sbk * 512 + w],
                           True, True, [BqiT[t // 4], BkiT[sbk]], [Brp])
                        return rp, Brp, w
                    LOOK = 3
                    pend = [issue_rp(*u) for u in units[:LOOK]]
                    sp = Bsp = None
                    for idx, (sbk, h) in enumerate(units):
                        rp, Brp, w = pend.pop(0)
                        if idx + LOOK < len(units):
                            pend.append(issue_rp(*units[idx + LOOK]))
                        if h == 0:
                            sp, Bsp = PO.get()
                        rl, Brl = bfr.get()
                        if h % 2 == 0:
                            act(rl[:, 0:w], rp[:, 0:w], AF.Relu, [Brp], [Brl])
                        else:
                            ts("dve", rl[:, 0:w], rp[:, 0:w], 0.0, None, ALU.max, None, [Brp], [Brl])
                        mm(sp[:, 0:w], dg[:, h, :], rl[:, 0:w], h == 0, h == 7, [Bdg, Brl], [Bsp], inc=True)
                        if h == 7:
                            act(sc[:, sbk * 512:sbk * 512 + w], sp[:, 0:w], AF.Copy, [Bsp], [Bsc])
                    st, Bst = str_.get()
                    dl, Bdl = dlr.get()
                    if t >= 2:
                        S.op("dve", lambda: nc.vector.tensor_reduce(out=st[:, 0:1], in_=sc[:, 0:svis], axis=AX.X, op=ALU.max),
                             reads=[Bsc], writes=[Bst])
                        S.op("dve", lambda: nc.vector.tensor_reduce(out=st[:, 1:2], in_=sc[:, 0:svis], axis=AX.X, op=ALU.min),
                             reads=[Bsc], writes=[Bst])
                    S.op("pool", lambda: nc.gpsimd.memset(sc[0:64, svis - 64:svis], NEG), writes=[Bsc])
                    junk, Bjunk = junkr.get()
                    return dict(t=t, svis=svis, sc=sc, Bsc=Bsc, st=st, Bst=Bst, dl=dl, Bdl=Bdl, junk=junk, Bjunk=Bjunk)

                def bisect(tiles):
                    tiles = [T for T in tiles if T["t"] >= 2]
                    for k_, T in enumerate(tiles):
                        T["neg"] = (k_ % 2 == 1)
                        st, Bst, dl, Bdl = T["st"], T["Bst"], T["dl"], T["Bdl"]
                        tt("dve", st[:, 2:3], st[:, 0:1], st[:, 1:2], ALU.subtract, [Bst], [Bst])
                        ts("dve", st[:, 2:3], st[:, 2:3], 1.001, 1e-6, ALU.mult, ALU.add, [Bst], [Bst])
                        if not T["neg"]:
                            ts("dve", dl[:], cols[:, CK + 3:CK + 4 + NBIS], st[:, 2:3], None, ALU.mult, None, [Bst, Bcols], [Bdl])
                            tt("dve", st[:, 3:4], st[:, 0:1], st[:, 2:3], ALU.subtract, [Bst], [Bst])
                            tt("dve", st[:, 3:4], st[:, 3:4], dl[:, 0:1], ALU.add, [Bst, Bdl], [Bst])
                        else:
                            ts("dve", st[:, 7:8], st[:, 2:3], -1.0, None, ALU.mult, None, [Bst], [Bst])
                            ts("dve", dl[:], cols[:, CK + 3:CK + 4 + NBIS], st[:, 7:8], None, ALU.mult, None, [Bst, Bcols], [Bdl])
                            tt("dve", st[:, 3:4], st[:, 2:3], st[:, 0:1], ALU.subtract, [Bst], [Bst])
                            tt("dve", st[:, 3:4], st[:, 3:4], dl[:, 0:1], ALU.add, [Bst, Bdl], [Bst])
                    for i in range(NBIS):
                        for T in tiles:
                            st, Bst, dl, Bdl = T["st"], T["Bst"], T["dl"], T["Bdl"]
                            if not T["neg"]:
                                ts("dve", T["junk"][:, 0:T["svis"]], T["sc"][:, 0:T["svis"]], st[:, 3:4], 0.0, ALU.is_ge, ALU.add,
                                   [T["Bsc"], Bst], [T["Bjunk"], Bst], accum_out=st[:, 4:5])
                            else:
                                act(T["junk"][:, 0:T["svis"]], T["sc"][:, 0:T["svis"]], AF.Sign, [T["Bsc"], Bst], [T["Bjunk"], Bst],
                                    bias=st[:, 3:4], scale=1.0, accum_out=st[:, 4:5])
                        for T in tiles:
                            st, Bst = T["st"], T["Bst"]
                            thr_cnt = float(TOPK) if not T["neg"] else float(2 * TOPK - T["svis"])
                            ts("dve", st[:, 5:6], st[:, 4:5], thr_cnt, 0.5, ALU.is_ge, ALU.subtract, [Bst], [Bst])
                        for T in tiles:
                            st, Bst, dl, Bdl = T["st"], T["Bst"], T["dl"], T["Bdl"]
                            stt("dve", st[:, 3:4], st[:, 5:6], dl[:, i:i + 1], st[:, 3:4], ALU.mult, ALU.add, [Bst, Bdl], [Bst])
                    for T in tiles:
                        st, Bst, dl, Bdl = T["st"], T["Bst"], T["dl"], T["Bdl"]
                        if not T["neg"]:
                            tt("dve", st[:, 6:7], st[:, 3:4], dl[:, NBIS:NBIS + 1], ALU.subtract, [Bst, Bdl], [Bst])
                        else:
                            stt("dve", st[:, 6:7], st[:, 3:4], -1.0, dl[:, NBIS:NBIS + 1], ALU.mult, ALU.add, [Bst, Bdl], [Bst])

                def mask_tile(T):
                    t, svis, sc, Bsc, st, Bst = T["t"], T["svis"], T["sc"], T["Bsc"], T["st"], T["Bst"]
                    if t < 2:
                        S.op("pool", lambda: nc.gpsimd.memset(st[:, 6:7], NEG / 2), writes=[Bst])
                    mk, Bmk = mskr.get()
                    ts("dve", mk[:, 0:svis], sc[:, 0:svis], st[:, 6:7], None, ALU.is_ge, None, [Bsc, Bst], [Bmk])
                    for i0 in range(0, t + 1, 4):
                        nb = min(4, t + 1 - i0)
                        tp, Btp = PS.get()
                        for bi in range(nb):
                            i = i0 + bi
                            mm(tp[:, bi * 128:(bi + 1) * 128], mk[:, i * 128:(i + 1) * 128], mat(M_ID), True, True, [Bmk, Bcmat], [Btp])
                        for bi in range(nb):
                            i = i0 + bi
                            dst = maskT[:, MOFF[i] + 128 * (t - i):MOFF[i] + 128 * (t - i) + 128]
                            act(dst, tp[:, bi * 128:(bi + 1) * 128], AF.Copy, [Btp], [BmaskT[i]])

                for tp_ in range(8 if STOP_AT >= 1.2 else 0):
                    Ts = [score_tile(2 * tp_), score_tile(2 * tp_ + 1)]
                    if STOP_AT >= 1.5:
                        bisect(Ts)
                    if STOP_AT >= 1.8:
                        for T in Ts:
                            mask_tile(T)
                barrier()
            with ExitStack() as es_tmp:
                Ct, St = make_tables(es, es_tmp, CK + 0, "b6")
            wr = Ring(nc, es, "b_w", 5, [128, 8, 128], BF16)
            wor = Ring(nc, es, "b_wo", 2, [128, 1024], BF16)
            qT = sb("b_qT", [128, SEQ], BF16, es)
            Bq = Buf("b_qT")
            kT = sb("b_kT", [128, 2, SEQ], BF16, es)
            Bk = Buf("b_kT")
            S.op("pool", lambda: nc.gpsimd.memset(kT[64:128, 0, :], 0.0), writes=[Bk])
            S.op("pool", lambda: nc.gpsimd.memset(kT[0:64, 1, :], 0.0), writes=[Bk])
            va = sb("b_va", [128, 16, 2, 128], BF16, es)
            Bva = Buf("b_va")
            sz = sb("b_sz", [128, SEQ], BF16, es)
            Bsz = Buf("b_sz")
            gT = sb("b_gT", [128, SEQ], BF16, es)
            BgT = Buf("b_gT")
            S.op("pool", lambda: nc.gpsimd.memset(va[:, :, 0, 64:128], 1.0), writes=[Bva])
            S.op("pool", lambda: nc.gpsimd.memset(va[:, :, 1, 0:64], 1.0), writes=[Bva])
            Wo = b_w_out[0]

            def mask_fn(i, q0, N):
                o = MOFF[i] + (q0 - 128 * i)
                return maskT[:, o:o + N], BmaskT[i]

            B_NR = subring([0, 1, 2, 3])
            B_ZV = subring([4])
            B_WO = subring([5, 6, 7])
            chunk_state = {}

            def proj_chunk_g(c):
                ws = []
                for g in range(4):
                    w_, Bw_ = wr.get()
                    load_w("pool", w_[:], Wv[:, :, g * 1024 + c * 128:g * 1024 + (c + 1) * 128], Bw_)
                    ws.append((w_, Bw_))
                wo, Bwo = wor.get()
                load_w("pool", wo[:], Wo[c * 128:(c + 1) * 128, :], Bwo)
                chunk_state[c] = (wo, Bwo)
                (wq, Bwq), (wk, Bwk), (wv_, Bwv), (wz, Bwz) = ws
                gens = []
                for tb in range(4):
                    for (w_, Bw_, gcl, isk) in ((wq, Bwq, CB + 8, False), (wk, Bwk, CB + 9, True)):
                        def srcp(tb=tb, w_=w_, Bw_=Bw_):
                            pb, Bpb = B_NR.get()
                            for kc in range(8):
                                mm(pb[:], w_[:, kc, :], xnT[:, kc, tsl(tb)], kc == 0, kc == 7, [Bw_, BxnT[tb]], [Bpb])
                            return pb[:], Bpb
                        if isk:
                            gens.append(norm_rope_g(srcp, 128, mat(M_B64), 64, col(gcl), mat(M_R64), Ct, St, tb, kT[0:64, 0, tsl(tb)], Bk, f32r, bfr,
                                                    out_hi=kT[64:128, 1, tsl(tb)], Bout_hi=Bk))
                        else:
                            gens.append(norm_rope_g(srcp, 128, mat(M_B64), 64, col(gcl), mat(M_R64), Ct, St, tb, qT[:, tsl(tb)], Bq, f32r, bfr))
                PROJ[0] = B_NR
                yield from run_pipe_g(gens, PIPE_DEPTH)
                PROJ[0] = PA
                for tb in range(4):
                    pb, Bpb = B_ZV.get()
                    for kc in range(8):
                        mm(pb[:], wz[:, kc, :], xnT[:, kc, tsl(tb)], kc == 0, kc == 7, [Bwz, BxnT[tb]], [Bpb])
                    act(sz[:, tsl(tb)], pb[:], AF.Silu, [Bpb], [Bsz])
                    yield
                    pv, Bpv = B_ZV.get()
                    for t4 in range(4):
                        tile_ = tb * 4 + t4
                        for kc in range(8):
                            mm(pv[:, t4 * 128:(t4 + 1) * 128], xnT[:, kc, tile_ * 128:(tile_ + 1) * 128], wv_[:, kc, :], kc == 0, kc == 7,
                               [Bwv, BxnT[tb]], [Bpv])
                    pv3 = pv[:].rearrange("p (a b) -> p a b", b=128)
                    act(va[:, tb * 4:(tb + 1) * 4, 0, 0:64], pv3[:, :, 0:64], AF.Copy, [Bpv], [Bva])
                    act(va[:, tb * 4:(tb + 1) * 4, 1, 64:128], pv3[:, :, 64:128], AF.Copy, [Bpv], [Bva])
                    yield

            heads = []
            for hh in range(2):
                lo = hh * 64
                heads.append(dict(k=(kT[:, hh, :], Bk), q=(qT[:, :], Bq),
                                  v=(lambda i, hh=hh: va[:, i, hh, :]), Bv=Bva,
                                  osl=slice(lo, lo + 64), lsl=slice(64 - lo, 128 - lo)))
            NCH = 8 if STOP_AT >= 3 else 0
            if NCH:
                for _ in proj_chunk_g(0):
                    pass
            for c in range(NCH):
                wo, Bwo = chunk_state[c]
                attention(heads, 0.125, mask_fn, gT, BgT, sz, Bsz, f32r, bfr, heat=HEAT)
                side = proj_chunk_g(c + 1) if c + 1 < NCH else None
                co_run(wout_g(wo, Bwo, gT, BgT, B_WO), side, 2)
            barrier()

    for L in layers:
        if L[0] == "A":
            layer_A(int(L[1]))
        elif L[0] == "C":
            layer_C()
        elif L[0] == "B":
            layer_B()
        else:
            raise NotImplementedError(L)

    Bout = Buf("out")
    for kc in range(8):
        S.dma("sp", yT_d[kc * 128:(kc + 1) * 128, :], xT[:, kc, :], reads=BxT[kc], own=Bout)
    S.finish([Bout])
    ges.close()
    return nc


def _pack_cols(inp):
    c = np.zeros((128, NCOLS), np.float32)

    def colmajor(v):
        return np.ascontiguousarray(v.reshape(-1, 128).T)
    for j in range(2):
        b = CA[j]
        c[:, b:b + 8] = colmajor(inp["a_norm"][j])
        for k in range(3):
            c[:, b + 8 + 8 * k:b + 16 + 8 * k] = colmajor(inp["a_conv_w"][j, k])
        c[:, b + 32:b + 40] = colmajor(inp["a_conv_b"][j])
    c[:, CB:CB + 8] = colmajor(inp["b_norm"][0])
    c[:, CB + 8] = np.tile(inp["b_q_norm"][0], 2)
    c[:, CB + 9] = np.tile(inp["b_k_norm"][0], 2)
    c[:, CC:CC + 8] = colmajor(inp["c_norm"][0])
    c[:, CC + 8:CC + 11] = colmajor(inp["c_q_lat_norm"][0])
    c[:, CC + 11:CC + 13] = colmajor(inp["c_kv_lat_norm"][0])
    c[:96, CC + 13] = inp["c_q_norm"][0]
    c[:96, CC + 14] = inp["c_k_norm"][0]
    p = np.arange(128)
    theta = 10000.0
    c[:, CK + 0] = (theta ** (-(np.arange(32, dtype=np.float32)) / 32.0)).astype(np.float32)[p % 32]
    inv16 = (theta ** (-(np.arange(16, dtype=np.float32)) / 16.0)).astype(np.float32)
    c[:, CK + 1] = np.where((p % 64) < 32, inv16[p % 16], 0.0)
    c[:, CK + 2] = np.where((p >= 64) & (p < 96), inv16[(p - 64) % 16], 0.0)
    for i in range(NBIS + 1):
        c[:, CK + 3 + i] = 2.0 ** -(i + 1)
    return c


def _const_mats():
    m = np.zeros((128, NMAT, 128), np.float32)
    m[:, M_ID, :] = np.eye(128)
    m[:, M_ONE, :] = 1.0
    m[0:64, M_B64, 0:64] = 1.0
    m[64:128, M_B64, 64:128] = 1.0
    for o in range(128):
        dd = o % 64
        if dd < 32:
            m[o + 32, M_R64, o] = -1.0
        else:
            m[o - 32, M_R64, o] = 1.0
        if dd < 16:
            m[o + 16, M_RIDX, o] = -1.0
        elif dd < 32:
            m[o - 16, M_RIDX, o] = 1.0
        if 64 <= o < 80:
            m[o + 16, M_RMLA, o] = -1.0
        elif 80 <= o < 96:
            m[o - 16, M_RMLA, o] = 1.0
    return m.reshape(128, NMAT * 128)


LAYERS = ["A0", "B0", "C0", "A1"]
_NC_CACHE = {}


def kernel(**inp):
    inp = {k: np.asarray(v) for k, v in inp.items()}
    key = tuple(LAYERS)
    if key not in _NC_CACHE:
        _NC_CACHE[key] = build(LAYERS)
    nc = _NC_CACHE[key]
    cols = _pack_cols(inp)
    cmat = _const_mats()
    x = inp["x"]
    in_maps = []
    for b in range(8):
        m = {
            "xT": np.ascontiguousarray(x[b].T),
            "pos": np.ascontiguousarray(inp["positions"][b:b + 1].astype(np.int32)),
            "cols": cols, "cmat": cmat,
        }
        for k in ("a_w_in", "a_w_out", "b_w_in", "b_w_out", "c_w_in", "c_w_uq", "c_w_ukv", "c_w_out"):
            m[k] = np.ascontiguousarray(inp[k], dtype=np.float32)
        in_maps.append(m)
    res = run_bass_kernel_spmd(nc, in_maps, core_ids=list(range(8)))
    out = np.stack([np.ascontiguousarray(res.results[b]["yT"].T) for b in range(8)], axis=0)
    return out.astype(np.float32)
```

```python
import math
from contextlib import ExitStack
import numpy as np
import concourse.bass as bass
import concourse.mybir as mybir
from concourse.bass_utils import run_bass_kernel_spmd

F32 = mybir.dt.float32
BF16 = mybir.dt.bfloat16
I32 = mybir.dt.int32
ALU = mybir.AluOpType
AF = mybir.ActivationFunctionType
AX = mybir.AxisListType

SEQ = 2048
D = 1024
EPS = 1e-6
NEG = -1.0e30
TOPK = 256
NBIS = 16
PIPE_DEPTH = 3
HEAT = 0
PV_DELAY = 1
C_FINE = False
C_DEPTH = 3
C_RATIO = 10 ** 9
STOP_AT = 3


class Buf:
    __slots__ = ("name", "w", "r", "dsem", "dcnt")

    def __init__(self, name):
        self.name = name
        self.w = None
        self.r = []
        self.dsem = None
        self.dcnt = 0


class Sched:
    def __init__(self, nc):
        self.nc = nc
        self.eng = {"pe": nc.tensor, "act": nc.scalar, "dve": nc.vector,
                    "pool": nc.gpsimd, "sp": nc.sync}
        self.sem = {}
        self.cnt = {}
        for e in self.eng:
            self.sem[e] = nc.alloc_semaphore("c_" + e)
            self.cnt[e] = 0
        self.seen = {e: {} for e in self.eng}
        self.dsems = []
        self.free_dsems = []

    def _semof(self, key):
        return self.sem[key] if isinstance(key, str) else key

    def _deps(self, q, reads, writes, is_dma=False):
        need = {}

        def add(d, same_ok):
            if d is None:
                return
            key, val = d
            if same_ok and key == q and not is_dma and q != "pool":
                return
            if key == "pe" and q == "pe" and not is_dma:
                return
            if self.seen[q].get(key, 0) >= val:
                return
            if need.get(key, 0) < val:
                need[key] = val
        for b in reads:
            add(b.w, False)
        for b in writes:
            add(b.w, True)
            for d in b.r:
                add(d, True)
        return need

    def _emit(self, q, need, fn):
        items = list(need.items())
        eng = self.eng[q]
        for key, val in items[:-1]:
            eng.wait_ge(self._semof(key), val)
            self.seen[q][key] = val
        inst = fn()
        if items:
            key, val = items[-1]
            inst._wait_ge(self._semof(key), val)
            self.seen[q][key] = val
        return inst

    def op(self, q, fn, reads=(), writes=(), inc=True):
        need = self._deps(q, reads, writes)
        inst = self._emit(q, need, fn)
        if inc:
            self.cnt[q] += 1
            inst.then_inc(self.sem[q], 1)
            d = (q, self.cnt[q])
        else:
            d = (q, self.cnt[q] + 1)
        for b in reads:
            if len(b.r) > 24:
                b.r = self._compact(b.r)
            b.r.append(d)
        for b in writes:
            b.w = d
            b.r = []
        return inst

    @staticmethod
    def _compact(lst):
        best = {}
        for k, v in lst:
            if best.get(k, 0) < v:
                best[k] = v
        return list(best.items())

    def dma(self, q, out, in_, reads=(), writes=(), own=None):
        if own is None:
            own = writes[0] if writes else reads[0]
        if own.dsem is None:
            self.dsems.append(own)
            own.dsem = self.nc.alloc_semaphore(f"d{len(self.dsems)}_" + own.name)
        need = self._deps(q, reads, writes, is_dma=True)
        inst = self._emit(q, need, lambda: self.eng[q].dma_start(out=out, in_=in_))
        own.dcnt += 16
        inst.then_inc(own.dsem, 16)
        d = (own.dsem, own.dcnt)
        for b in reads:
            b.r.append(d)
        for b in writes:
            b.w = d
            b.r = []
        return inst

    def finish(self, bufs):
        need = {}
        for b in bufs:
            for d in ([b.w] if b.w else []) + list(b.r):
                key, val = d
                if need.get(key, 0) < val:
                    need[key] = val
        for key, val in need.items():
            self.eng["sp"].wait_ge(self._semof(key), val)


class Ring:
    def __init__(self, nc, es, name, n, shape, dtype, psum=False):
        self.items = []
        for i in range(n):
            nm = f"{name}{i}"
            if psum:
                t = es.enter_context(nc.psum_tensor(nm, shape, dtype))
            else:
                t = es.enter_context(nc.sbuf_tensor(nm, shape, dtype))
            self.items.append((t, Buf(nm)))
        self.i = 0

    def get(self):
        it = self.items[self.i % len(self.items)]
        self.i += 1
        return it


CA = [0, 40]
CB = 80
CC = 90
CK = 105
NCOLS = 108 + NBIS + 1
M_ID, M_ONE, M_B64, M_R64, M_RIDX, M_RMLA = 0, 1, 2, 3, 4, 5
NMAT = 6


def tsl(tb):
    return slice(tb * 512, (tb + 1) * 512)


def build(layers):
    nc = bass.Bass("TRN2", target_bir_lowering=False)
    S = Sched(nc)

    def din(name, shape, dt=F32):
        return nc.dram_tensor(name, shape, dt, kind="ExternalInput")

    xT_d = din("xT", [D, SEQ]).ap()
    pos_d = din("pos", [1, SEQ], I32)
    cols_d = din("cols", [128, NCOLS]).ap()
    cmat_d = din("cmat", [128, NMAT * 128]).ap()
    a_w_in = din("a_w_in", [2, 1024, 4096]).ap()
    a_w_out = din("a_w_out", [2, 1024, 1024]).ap()
    b_w_in = din("b_w_in", [1, 1024, 4680]).ap()
    b_w_out = din("b_w_out", [1, 1024, 1024]).ap()
    c_w_in = din("c_w_in", [1, 1024, 1696]).ap()
    c_w_uq = din("c_w_uq", [1, 384, 1536]).ap()
    c_w_ukv = din("c_w_ukv", [1, 256, 2048]).ap()
    c_w_out = din("c_w_out", [1, 1024, 1024]).ap()
    yT_d = nc.dram_tensor("yT", [D, SEQ], F32, kind="ExternalOutput").ap()

    ges = ExitStack()

    def sb(name, shape, dt, es=None):
        return (es or ges).enter_context(nc.sbuf_tensor(name, shape, dt))

    xT = sb("xT_sb", [128, 8, SEQ], F32)
    BxT = [[Buf(f"xT{k}_{t}") for t in range(4)] for k in range(8)]
    xnT = sb("xnT", [128, 8, SEQ], BF16)
    BxnT = [Buf(f"xnT{t}") for t in range(4)]
    cols = sb("cols_sb", [128, NCOLS], F32)
    Bcols = Buf("cols")
    cmat = sb("cmat_sb", [128, NMAT, 128], BF16)
    Bcmat = Buf("cmat")
    PS = Ring(nc, ges, "psg", 4, [128, 512], F32, psum=True)
    PO = Ring(nc, ges, "pso", 4, [128, 512], F32, psum=True)

    PA = Ring.__new__(Ring)
    PA.items = PS.items + PO.items
    PA.i = 0
    _all8 = list(PA.items)
    PA.items = _all8[0:7]
    PO.items = _all8[4:7]

    def subring(idx):
        r = Ring.__new__(Ring)
        r.items = [_all8[i] for i in idx]
        r.i = 0
        return r
    PROJ = [PA]
    T1ENG = ["dve"]
    FINE = [False]
    HEATBANK = _all8[7][0]
    heat_l = cmat[:, M_ONE, :]
    heat_r = cmat[:, 0:4, :].rearrange("p a b -> p (a b)")

    def col(i):
        return cols[:, i:i + 1]

    def mat(i, p=128, m=128):
        return cmat[0:p, i, 0:m]

    ALLDMA = []
    _orig_dma = S.dma

    def dma_track(q, out, in_, reads=(), writes=(), own=None):
        o = own if own is not None else (writes[0] if writes else reads[0])
        if o not in ALLDMA:
            ALLDMA.append(o)
        return _orig_dma(q, out, in_, reads=reads, writes=writes, own=own)
    S.dma = dma_track

    for kc in range(8):
        S.dma("sp", xT[:, kc, :], xT_d[kc * 128:(kc + 1) * 128, :], writes=BxT[kc])
    S.dma("sp", cols[:], cols_d[:, :], writes=[Bcols])
    S.dma("pool", cmat[:].rearrange("p a b -> p (a b)"), cmat_d[:, :], writes=[Bcmat])

    def mm(out, lhsT, rhs, start, stop, reads, writes, inc=None):
        if inc is None:
            inc = stop
        S.op("pe", lambda: nc.tensor.matmul(out, lhsT, rhs, start=start, stop=stop),
             reads=reads, writes=writes, inc=inc)

    def act(out, in_, func, reads, writes, **kw):
        S.op("act", lambda: nc.scalar.activation(out=out, in_=in_, func=func, **kw),
             reads=reads, writes=writes)

    def tt(q, out, in0, in1, op, reads, writes):
        e = S.eng[q]
        S.op(q, lambda: e.tensor_tensor(out=out, in0=in0, in1=in1, op=op), reads=reads, writes=writes)

    def ts(q, out, in0, s1, s2, op0, op1, reads, writes, **kw):
        e = S.eng[q]
        if s2 is None:
            S.op(q, lambda: e.tensor_scalar(out=out, in0=in0, scalar1=s1, scalar2=None, op0=op0, **kw),
                 reads=reads, writes=writes)
        else:
            S.op(q, lambda: e.tensor_scalar(out=out, in0=in0, scalar1=s1, scalar2=s2, op0=op0, op1=op1, **kw),
                 reads=reads, writes=writes)

    def stt(q, out, in0, scalar, in1, op0, op1, reads, writes):
        e = S.eng[q]
        S.op(q, lambda: e.scalar_tensor_tensor(out=out, in0=in0, scalar=scalar, in1=in1, op0=op0, op1=op1),
             reads=reads, writes=writes)

    def rstd_from_ssq(ssq_ps, Bssq, P, n, N, f32r):
        lnv, Bl = f32r.get()
        act(lnv[0:P, 0:N], ssq_ps, AF.Ln, [Bssq], [Bl], scale=1.0 / n, bias=EPS)
        rs, Br = f32r.get()
        act(rs[0:P, 0:N], lnv[0:P, 0:N], AF.Exp, [Bl], [Br], scale=-0.5)
        return rs, Br

    def emit_norm(gc, f32r, sqr):
        for tb in range(4):
            pb, Bpb = PS.get()
            for kc in range(8):
                sq, Bsq = sqr.get()
                act(sq[:], xT[:, kc, tsl(tb)], AF.Square, [BxT[kc][tb]], [Bsq])
                mm(pb[:], mat(M_ONE), sq[:], kc == 0, kc == 7, [Bsq, Bcmat], [Bpb], inc=True)
            rs, Br = rstd_from_ssq(pb[:], Bpb, 128, 1024.0, 512, f32r)
            for kc in range(8):
                stt("dve", xnT[:, kc, tsl(tb)], xT[:, kc, tsl(tb)], col(gc + kc), rs[:], ALU.mult, ALU.mult,
                    [BxT[kc][tb], Br, Bcols], [BxnT[tb]])

    def wout_partial(wo_ap, Bwo, g_ap, Bg, first_last=None):
        for dc in range(8):
            for tb in range(4):
                pb, Bpb = PS.get()
                mm(pb[:], wo_ap[:, dc * 128:(dc + 1) * 128], g_ap[:, tsl(tb)], True, True, [Bwo, Bg], [Bpb])
                tt("dve", xT[:, dc, tsl(tb)], xT[:, dc, tsl(tb)], pb[:], ALU.add,
                   [BxT[dc][tb], Bpb], [BxT[dc][tb]])

    def wout_g(wo_ap, Bwo, g_ap, Bg, ring):
        items = [(dc, tb) for dc in range(8) for tb in range(4)]
        look = len(ring.items) - 1

        def issue(k):
            dc, tb = items[k]
            pb, Bpb = ring.get()
            mm(pb[:], wo_ap[:, dc * 128:(dc + 1) * 128], g_ap[:, tsl(tb)], True, True, [Bwo, Bg], [Bpb])
            return pb, Bpb
        pend = [issue(k) for k in range(look)]
        for k, (dc, tb) in enumerate(items):
            pb, Bpb = pend.pop(0)
            if k + look < len(items):
                pend.append(issue(k + look))
            tt("dve", xT[:, dc, tsl(tb)], xT[:, dc, tsl(tb)], pb[:], ALU.add,
               [BxT[dc][tb], Bpb], [BxT[dc][tb]])
            yield

    def layer_A(j):
        cb = CA[j]
        with ExitStack() as es:
            f32r = Ring(nc, es, f"a{j}f32_", 6, [128, 512], F32)
            sqr = Ring(nc, es, f"a{j}sq_", 3, [128, 512], BF16)
            emit_norm(cb, f32r, sqr)
            gT = sb(f"a{j}_gT", [128, 8, SEQ], BF16, es)
            BgT = [[Buf(f"gT{c}_{t}") for t in range(4)] for c in range(8)]
            wout = sb(f"a{j}_wout", [128, 8, 1024], BF16, es)
            Bwout = Buf("a_wout")
            S.dma("pool", wout[:], a_w_out[j].rearrange("(cc p) d -> p cc d", p=128), writes=[Bwout])
            war = [(sb(f"a{j}_w{i}", [128, 8, 4, 128], BF16, es), [Buf(f"a{j}_w{i}_{g}") for g in range(4)]) for i in range(2)]
            stg = Ring(nc, es, f"a{j}_stg", 2, [128, 8, 128], F32)
            ur = [(sb(f"a{j}_u{i}", [128, 2 + SEQ], F32, es), [Buf(f"a_u{i}_{t}") for t in range(5)]) for i in range(2)]
            wv = a_w_in[j].rearrange("(kc p) (g c i) -> p kc g c i", p=128, g=4, c=8)

            def issue_loads(c):
                wa, Bwa = war[c % 2]
                for g in (0, 1):
                    S.dma("pool", wa[:, :, g, :], wv[:, :, g, c, :], writes=[Bwa[g]])
                stgs = []
                for g in (2, 3):
                    st_, Bst_ = stg.get()
                    S.dma("sp", st_[:], wv[:, :, g, c, :], writes=[Bst_])
                    stgs.append((g, st_, Bst_))
                return stgs

            def issue_casts(c, stgs):
                wa, Bwa = war[c % 2]
                for g, st_, Bst_ in stgs:
                    act(wa[:, :, g, :], st_[:], AF.Copy, [Bst_], [Bwa[g]])

            issue_casts(0, issue_loads(0))
            for c in range(8):
                wa, Bwa = war[c % 2]
                nxt = issue_loads(c + 1) if c + 1 < 8 else None
                u, Bu = ur[c % 2]
                S.op("pool", lambda: nc.gpsimd.memset(u[:, 0:2], 0.0), writes=[Bu[4]])
                for tb in range(4):
                    banks = [_all8[(tb % 2) * 4 + g_] for g_ in range(4)]
                    for g in range(4):
                        for kc in range(8):
                            mm(banks[g][0][:], wa[:, kc, g, :], xnT[:, kc, tsl(tb)], kc == 0, kc == 7,
                               [Bwa[g], BxnT[tb]], [banks[g][1]])
                    (bg, Bbg), (cg, Bcg), (hv, Bhv), (z, Bz) = banks
                    hvs, Bhvs = f32r.get()
                    act(hvs[:], hv[:], AF.Copy, [Bhv], [Bhvs])
                    us = slice(2 + tb * 512, 2 + (tb + 1) * 512)
                    tt("dve", u[:, us], cg[:], hvs[:], ALU.mult, [Bcg, Bhvs], [Bu[tb]])
                    prev = Bu[tb - 1] if tb > 0 else Bu[4]
                    y, By = f32r.get()
                    ts("pool", y[:], u[:, us], col(cb + 24 + c), col(cb + 32 + c), ALU.mult, ALU.add,
                       [Bu[tb], Bcols], [By])
                    stt("dve", y[:], u[:, 1 + tb * 512:1 + (tb + 1) * 512], col(cb + 16 + c), y[:], ALU.mult, ALU.add,
                        [Bu[tb], prev, Bcols, By], [By])
                    stt("dve", y[:], u[:, tb * 512:(tb + 1) * 512], col(cb + 8 + c), y[:], ALU.mult, ALU.add,
                        [Bu[tb], prev, Bcols, By], [By])
                    szs, Bszs = f32r.get()
                    act(szs[:], z[:], AF.Silu, [Bz], [Bszs])
                    tt("dve", szs[:], bg[:], szs[:], ALU.mult, [Bbg, Bszs], [Bszs])
                    tt("pool", gT[:, c, tsl(tb)], szs[:], y[:], ALU.mult, [Bszs, By], [BgT[c][tb]])
                    if tb == 1 and nxt is not None:
                        issue_casts(c + 1, nxt)
            for dc in range(8):
                for tb in range(4):
                    pb, Bpb = PS.get()
                    for cc in range(8):
                        mm(pb[:], wout[:, cc, dc * 128:(dc + 1) * 128], gT[:, cc, tsl(tb)], cc == 0, cc == 7,
                           [Bwout, BgT[cc][tb]], [Bpb])
                    tt("dve", xT[:, dc, tsl(tb)], xT[:, dc, tsl(tb)], pb[:], ALU.add,
                       [BxT[dc][tb], Bpb], [BxT[dc][tb]])
            barrier()

    def barrier():
        for q in ("pe", "act", "dve", "pool"):
            pass
        for q in S.eng:
            for e in S.eng:
                if e == q:
                    continue
                v = S.cnt[e]
                if v > 0 and S.seen[q].get(e, 0) < v:
                    S.eng[q].wait_ge(S.sem[e], v)
                    S.seen[q][e] = v
        for b in ALLDMA:
            if b.dsem is not None and b.dcnt > 0:
                for q in S.eng:
                    if S.seen[q].get(b.dsem, 0) < b.dcnt:
                        S.eng[q].wait_ge(b.dsem, b.dcnt)
                        S.seen[q][b.dsem] = b.dcnt


    def make_tables(es_tab, es_tmp, invc, name):
        tabs = []
        for nm in ("C", "S"):
            tabs.append((sb(name + "_" + nm, [128, SEQ], F32, es_tab), Buf(name + nm)))
        posi = sb(name + "_posi", [128, 512], I32, es_tmp)
        Bposi = Buf(name + "posi")
        a2 = sb(name + "_a2", [128, 512], F32, es_tmp)
        Ba2 = Buf(name + "a2")
        t1 = sb(name + "_t1", [128, 512], F32, es_tmp)
        Bt1 = Buf(name + "t1")
        ki = sb(name + "_ki", [128, 512], I32, es_tmp)
        Bki = Buf(name + "ki")
        for tb in range(4):
            S.dma("sp", posi[:], bass.AP(pos_d, tb * 512, [[0, 128], [1, 512]]), writes=[Bposi])
            S.op("dve", lambda: nc.vector.tensor_copy(out=a2[:], in_=posi[:]), reads=[Bposi], writes=[Ba2])
            ts("dve", a2[:], a2[:], col(invc), 1.0 / (2 * math.pi), ALU.mult, ALU.mult, [Ba2, Bcols], [Ba2])
            for (tab, Btab), c0 in zip(tabs, (0.25, 0.0)):
                ts("dve", t1[:], a2[:], c0, None, ALU.add, None, [Ba2], [Bt1])
                S.op("dve", lambda: nc.vector.tensor_copy(out=ki[:], in_=t1[:]), reads=[Bt1], writes=[Bki])
                S.op("dve", lambda: nc.vector.tensor_copy(out=tab[:, tsl(tb)], in_=ki[:]), reads=[Bki], writes=[Btab])
                tt("dve", t1[:], t1[:], tab[:, tsl(tb)], ALU.subtract, [Bt1, Btab], [Bt1])
                stt("dve", t1[:], t1[:], 0.5, t1[:], ALU.is_gt, ALU.subtract, [Bt1], [Bt1])
                act(tab[:, tsl(tb)], t1[:], AF.Sin, [Bt1], [Btab], scale=-2.0 * math.pi)
        barrier()
        return tabs

    def norm_rope_g(srcfn, P, blk, n, gcol, rot, Ct, St, tb, out, Bout, f32r, bfr, eng2="pool", out_hi=None, Bout_hi=None):
        (C, BC), (Sn, BS) = Ct, St
        fine = FINE[0]
        src, Bsrc = srcfn()
        if fine:
            yield
        if n is not None:
            sq, Bsq = bfr.get()
            act(sq[0:P, :], src, AF.Square, [Bsrc], [Bsq])
            yield
            pb, Bpb = PROJ[0].get()
            mm(pb[0:P, :], blk, sq[0:P, :], True, True, [Bsq, Bcmat], [Bpb])
            if fine:
                yield
            rs, Br = rstd_from_ssq(pb[0:P, :], Bpb, P, float(n), 512, f32r)
            yield
            xn, Bxn = bfr.get()
            stt("dve", xn[0:P, :], src, gcol, rs[0:P, :], ALU.mult, ALU.mult, [Bsrc, Br, Bcols], [Bxn])
        else:
            yield
            xn, Bxn = bfr.get()
            act(xn[0:P, :], src, AF.Copy, [Bsrc], [Bxn])
        if fine:
            yield
        rp, Brp = PROJ[0].get()
        mm(rp[0:P, :], rot, xn[0:P, :], True, True, [Bxn, Bcmat], [Brp])
        yield
        t1, Bt1 = f32r.get()
        tt(T1ENG[0], t1[0:P, :], xn[0:P, :], C[0:P, tsl(tb)], ALU.mult, [Bxn, BC], [Bt1])
        t2, Bt2 = f32r.get()
        tt("dve", t2[0:P, :], rp[0:P, :], Sn[0:P, tsl(tb)], ALU.mult, [Brp, BS], [Bt2])
        if fine:
            yield
        if out_hi is None:
            tt(eng2, out, t1[0:P, :], t2[0:P, :], ALU.add, [Bt1, Bt2], [Bout])
        else:
            tt(eng2, out, t1[0:64, :], t2[0:64, :], ALU.add, [Bt1, Bt2], [Bout])
            tt(eng2, out_hi, t1[64:128, :], t2[64:128, :], ALU.add, [Bt1, Bt2], [Bout_hi])

    def run_pipe_g(gens, depth):
        active = []
        it = iter(gens)
        more = True
        while True:
            for g in list(active):
                try:
                    next(g)
                except StopIteration:
                    active.remove(g)
            if more and len(active) < depth:
                try:
                    g = next(it)
                    active.append(g)
                    try:
                        next(g)
                    except StopIteration:
                        active.remove(g)
                except StopIteration:
                    more = False
            if not active and not more:
                break
            yield

    def run_pipe(gens, depth):
        for _ in run_pipe_g(gens, depth):
            pass

    def co_run(main, side, ratio):
        k = 0
        main_done = side is None and False
        while True:
            try:
                next(main)
            except StopIteration:
                break
            k += 1
            if side is not None and ratio < 0:
                for _ in range(-ratio):
                    try:
                        next(side)
                    except StopIteration:
                        side = None
                        break
            elif side is not None and k % ratio == 0:
                try:
                    next(side)
                except StopIteration:
                    side = None
        if side is not None:
            for _ in side:
                pass

    def attention_g(heads, scale, mask_fn, gT, BgT, sz, Bsz, f32r, ptr, st_ring=None, o_ring=None, heat=0, recip_dve=False):
        st_ring = st_ring or PS
        o_ring = o_ring or PO
        for j in range(4):
            for hd in heads:
                kT, Bk = hd["k"]
                qT, Bq = hd["q"]
                osl, lsl = hd["osl"], hd["lsl"]
                O, BO = o_ring.get()
                last = 4 * j + 3

                def issue_st(i):
                    off = max(0, 128 * (i - 4 * j))
                    N = 512 - off
                    q0 = 512 * j + off
                    st, Bst = st_ring.get()
                    mm(st[:, 0:N], kT[:, i * 128:(i + 1) * 128], qT[:, q0:q0 + N], True, True, [Bk, Bq], [Bst])
                    return st, Bst, off, N, q0
                LOOK = len(st_ring.items) - 1
                pv_pend = []
                pend = [issue_st(i) for i in range(min(LOOK, last + 1))]
                for i in range(last + 1):
                    st, Bst, off, N, q0 = pend.pop(0)
                    if i + LOOK <= last:
                        pend.append(issue_st(i + LOOK))
                    pt, Bpt = ptr.get()
                    act(pt[:, 0:N], st[:, 0:N], AF.Exp, [Bst], [Bpt], scale=scale)
                    if mask_fn is not None:
                        m_ap, Bm = mask_fn(i, q0, N)
                        tt("dve", pt[:, 0:N], pt[:, 0:N], m_ap, ALU.mult, [Bpt, Bm], [Bpt])
                    elif i >= 4 * j:
                        S.op("pool", lambda: nc.gpsimd.memset(pt[64:128, 0:64], 0.0), writes=[Bpt])
                    pv_pend.append((i, off, N, pt, Bpt))
                    while len(pv_pend) > PV_DELAY:
                        i_, off_, N_, pt_, Bpt_ = pv_pend.pop(0)
                        mm(O[:, off_:512], hd["v"](i_), pt_[:, 0:N_], i_ == 0, i_ == last, [hd["Bv"], Bpt_], [BO], inc=True)
                    yield
                while pv_pend:
                    i_, off_, N_, pt_, Bpt_ = pv_pend.pop(0)
                    mm(O[:, off_:512], hd["v"](i_), pt_[:, 0:N_], i_ == 0, i_ == last, [hd["Bv"], Bpt_], [BO], inc=True)
                rl, Brl = f32r.get()
                if recip_dve and j < 3:
                    S.op("dve", lambda: nc.vector.reciprocal(out=rl[lsl, :], in_=O[lsl, :]), reads=[BO], writes=[Brl])
                else:
                    act(rl[lsl, :], O[lsl, :], AF.Ln, [BO], [Brl])
                    act(rl[lsl, :], rl[lsl, :], AF.Exp, [Brl], [Brl], scale=-1.0)
                tmp, Btmp = f32r.get()
                tt("dve", tmp[osl, :], O[osl, :], rl[lsl, :], ALU.mult, [BO, Brl], [Btmp])
                tt("pool", gT[osl, tsl(j)], tmp[osl, :], sz[osl, tsl(j)], ALU.mult, [Btmp, Bsz], [BgT])

    def attention(heads, scale, mask_fn, gT, BgT, sz, Bsz, f32r, ptr, heat=0):
        for _ in attention_g(heads, scale, mask_fn, gT, BgT, sz, Bsz, f32r, ptr, heat=heat):
            pass

    def load_w(q, dst, src, Bw):
        S.dma(q, dst, src, writes=[Bw])

    def layer_C():
        W = c_w_in[0]
        with ExitStack() as es:
            f32r = Ring(nc, es, "cf32_", 6, [128, 512], F32)
            bfr = Ring(nc, es, "cbf_", 6, [128, 512], BF16)
            emit_norm(CC, f32r, bfr)
            with ExitStack() as es_tmp:
                Ct, St = make_tables(es, es_tmp, CK + 2, "ct")
            cqn = sb("c_cqn", [128, 3, SEQ], BF16, es)
            Bcqn = [Buf(f"cqn{t}") for t in range(4)]
            ckvn = sb("c_ckvn", [128, 2, SEQ], BF16, es)
            Bckvn = [Buf(f"ckvn{t}") for t in range(4)]
            krT = sb("c_krT", [128, SEQ], F32, es)
            BkrT = [Buf(f"krT{t}") for t in range(4)]
            with ExitStack() as es1:
                wcq = sb("c_wcq", [128, 8, 384], BF16, es1)
                Bwcq = Buf("wcq")
                wckv = sb("c_wckv", [128, 8, 256], BF16, es1)
                Bwckv = Buf("wckv")
                wkr = sb("c_wkr", [128, 8, 96], BF16, es1)
                Bwkr = Buf("wkr")
                Wv = W.rearrange("(kc p) n -> p kc n", p=128)
                load_w("pool", wcq[:], Wv[:, :, 0:384], Bwcq)
                load_w("pool", wckv[:], Wv[:, :, 384:640], Bwckv)
                S.op("pool", lambda: nc.gpsimd.memset(wkr[:], 0.0), writes=[Bwkr])
                load_w("pool", wkr[:, :, 64:96], Wv[:, :, 640:672], Bwkr)
                for tb in range(4):
                    for (wt, Bwt, nch, dst, Bdst, gc, nn) in ((wcq, Bwcq, 3, cqn, Bcqn, CC + 8, 384.0),
                                                              (wckv, Bwckv, 2, ckvn, Bckvn, CC + 11, 256.0)):
                        banks = [PS.get() for _ in range(nch)]
                        for ch in range(nch):
                            for kc in range(8):
                                mm(banks[ch][0][:], wt[:, kc, ch * 128:(ch + 1) * 128], xnT[:, kc, tsl(tb)], kc == 0, kc == 7,
                                   [Bwt, BxnT[tb]], [banks[ch][1]])
                        ssq, Bssq = PO.get()
                        for ch in range(nch):
                            sq, Bsq = bfr.get()
                            act(sq[:], banks[ch][0][:], AF.Square, [banks[ch][1]], [Bsq])
                            mm(ssq[:], mat(M_ONE), sq[:], ch == 0, ch == nch - 1, [Bsq, Bcmat], [Bssq], inc=True)
                        rs, Br = rstd_from_ssq(ssq[:], Bssq, 128, nn, 512, f32r)
                        for ch in range(nch):
                            stt("dve", dst[:, ch, tsl(tb)], banks[ch][0][:], col(gc + ch), rs[:], ALU.mult, ALU.mult,
                                [banks[ch][1], Br, Bcols], [Bdst[tb]])
                    pb, Bpb = PS.get()
                    for kc in range(8):
                        mm(pb[0:96, :], wkr[:, kc, :], xnT[:, kc, tsl(tb)], kc == 0, kc == 7, [Bwkr, BxnT[tb]], [Bpb])
                    act(krT[64:96, tsl(tb)], pb[64:96, :], AF.Copy, [Bpb], [BkrT[tb]])
                barrier()
            wzr = Ring(nc, es, "c_wz", 2, [128, 8, 128], BF16)
            wor = Ring(nc, es, "c_wo", 2, [128, 1024], BF16)
            wuqr = Ring(nc, es, "c_wuq", 2, [128, 3, 96], BF16)
            wukvr = Ring(nc, es, "c_wukv", 2, [128, 2, 128], BF16)
            qr = Ring(nc, es, "c_q", 2, [128, SEQ], BF16)
            kr_ = Ring(nc, es, "c_k", 2, [128, SEQ], BF16)
            vr = Ring(nc, es, "c_v", 2, [128, 16, 128], BF16)
            szr = Ring(nc, es, "c_sz", 1, [128, SEQ], BF16)
            gr = Ring(nc, es, "c_g", 1, [128, SEQ], BF16)
            ptr = Ring(nc, es, "c_pt", 4, [128, 512], BF16)
            for sl_, (vt, Bv) in enumerate(vr.items):
                lo = 64 if sl_ == 0 else 0
                S.op("pool", lambda: nc.gpsimd.memset(vt[:, :, lo:lo + 64], 1.0), writes=[Bv])
            Wz = W.rearrange("(kc p) n -> p kc n", p=128)
            Wuq = c_w_uq[0].rearrange("(kc p) n -> p kc n", p=128)
            Wukv = c_w_ukv[0].rearrange("(kc p) n -> p kc n", p=128)
            Wo = c_w_out[0]
            scale = 96.0 ** -0.5
            C_ST = subring([0, 1, 2, 3])
            C_O = subring([4, 5])
            C_PJ = subring([4, 5, 6, 7])
            hds = {}

            def head_proj_g(h):
                hh = h % 2
                wuq, Bwuq = wuqr.get()
                load_w("pool", wuq[:], Wuq[:, :, h * 96:(h + 1) * 96], Bwuq)
                wukv, Bwukv = wukvr.get()
                load_w("pool", wukv[:], Wukv[:, :, h * 128:(h + 1) * 128], Bwukv)
                qT, Bq = qr.get()
                kT, Bk = kr_.get()
                vt, Bv = vr.items[hh]
                vlo = 0 if hh == 0 else 64
                hds[h] = dict(k=(kT[0:96, :], Bk), q=(qT[0:96, :], Bq), v=(lambda i, vt=vt: vt[:, i, :]), Bv=Bv,
                              osl=slice(vlo, vlo + 64), lsl=slice(64 - vlo, 128 - vlo))
                gens = []
                for tb in range(4):
                    def srcq(tb=tb):
                        pq, Bpq = C_PJ.get()
                        for kc in range(3):
                            mm(pq[0:96, :], wuq[:, kc, :], cqn[:, kc, tsl(tb)], kc == 0, kc == 2, [Bwuq, Bcqn[tb]], [Bpq])
                        return pq[0:96, :], Bpq
                    gens.append(norm_rope_g(srcq, 96, mat(M_ONE, 96, 96), 96, cols[0:96, CC + 13:CC + 14], mat(M_RMLA, 96, 96),
                                            Ct, St, tb, qT[0:96, tsl(tb)], Bq, f32r, bfr))

                    def srck(tb=tb):
                        pk, Bpk = C_PJ.get()
                        for kc in range(2):
                            mm(pk[0:64, :], wukv[:, kc, 0:64], ckvn[:, kc, tsl(tb)], kc == 0, kc == 1, [Bwukv, Bckvn[tb]], [Bpk])
                        act(pk[64:96, :], krT[64:96, tsl(tb)], AF.Copy, [BkrT[tb]], [Bpk])
                        return pk[0:96, :], Bpk
                    gens.append(norm_rope_g(srck, 96, mat(M_ONE, 96, 96), 96, cols[0:96, CC + 14:CC + 15], mat(M_RMLA, 96, 96),
                                            Ct, St, tb, kT[0:96, tsl(tb)], Bk, f32r, bfr))
                yield from run_pipe_g(gens, C_DEPTH)
                for g in range(2):
                    pv, Bpv = C_PJ.get()
                    for t8 in range(8):
                        tile_ = g * 8 + t8
                        for kc in range(2):
                            mm(pv[:, t8 * 64:(t8 + 1) * 64], ckvn[:, kc, tile_ * 128:(tile_ + 1) * 128], wukv[:, kc, 64:128],
                               kc == 0, kc == 1, [Bwukv, Bckvn[tile_ // 4]], [Bpv])
                    act(vt[:, g * 8:(g + 1) * 8, vlo:vlo + 64], pv[:].rearrange("p (a b) -> p a b", b=64), AF.Copy, [Bpv], [Bv])
                    yield

            PROJ[0] = C_PJ
            T1ENG[0] = "pool"
            for _ in head_proj_g(0):
                pass
            wo = Bwo = sz = Bsz = gT = BgT = None
            C_WO = subring([0, 1, 2])
            for h in range(16):
                c, hh = divmod(h, 2)
                if hh == 0:
                    wz, Bwz = wzr.get()
                    load_w("pool", wz[:], Wz[:, :, 672 + c * 128:672 + (c + 1) * 128], Bwz)
                    wo, Bwo = wor.get()
                    load_w("pool", wo[:], Wo[c * 128:(c + 1) * 128, :], Bwo)
                    sz, Bsz = szr.get()
                    for tb in range(4):
                        pb, Bpb = C_PJ.get()
                        for kc in range(8):
                            mm(pb[:], wz[:, kc, :], xnT[:, kc, tsl(tb)], kc == 0, kc == 7, [Bwz, BxnT[tb]], [Bpb])
                        act(sz[:, tsl(tb)], pb[:], AF.Silu, [Bpb], [Bsz])
                    gT, BgT = gr.get()
                main = attention_g([hds[h]], scale, None, gT, BgT, sz, Bsz, f32r, ptr, st_ring=C_ST, o_ring=C_O, recip_dve=True)
                for _ in main:
                    pass
                side = head_proj_g(h + 1) if h + 1 < 16 else None
                if hh == 1:
                    co_run(wout_g(wo, Bwo, gT, BgT, C_WO), side, 2)
                elif side is not None:
                    for _ in side:
                        pass
            PROJ[0] = PA
            T1ENG[0] = "dve"
            barrier()


    def layer_B():
        Wv = b_w_in[0].rearrange("(kc p) n -> p kc n", p=128)
        MOFF = [sum(2048 - 128 * ii for ii in range(i)) for i in range(16)]
        with ExitStack() as es:
            f32r = Ring(nc, es, "bf32_", 4, [128, 512], F32)
            bfr = Ring(nc, es, "bbf_", 6, [128, 512], BF16)
            emit_norm(CB, f32r, bfr)
            maskT = sb("b_maskT", [128, 17408], BF16, es)
            BmaskT = [Buf(f"maskT{i}") for i in range(16)]
            with ExitStack() as es2:
                qiT = sb("b_qiT", [128, 4, SEQ], BF16, es2)
                BqiT = [Buf(f"qiT{t}") for t in range(4)]
                kiT2 = sb("b_kiT2", [128, 2, SEQ], BF16, es2)
                BkiT = [Buf(f"kiT{t}") for t in range(4)]
                S.op("pool", lambda: nc.gpsimd.memset(kiT2[64:128, 0, :], 0.0), writes=BkiT)
                S.op("pool", lambda: nc.gpsimd.memset(kiT2[0:64, 1, :], 0.0), writes=BkiT)
                wi_sb = sb("b_wi", [128, 16, 8], F32, es2)
                Bwi = Buf("wi")
                with ExitStack() as es_w:
                    with ExitStack() as es_tmp:
                        Ct, St = make_tables(es_w, es_tmp, CK + 1, "bi")
                    wqi = sb("b_wqi", [128, 8, 512], BF16, es_w)
                    Bwqi = Buf("wqi")
                    wki2 = sb("b_wki2", [128, 8, 128], BF16, es_w)
                    Bwki2 = Buf("wki2")
                    wwi = sb("b_wwi", [128, 8, 8], BF16, es_w)
                    Bwwi = Buf("wwi")
                    load_w("pool", wqi[:], Wv[:, :, 4096:4608], Bwqi)
                    load_w("pool", wki2[:, :, 0:64], Wv[:, :, 4608:4672], Bwki2)
                    load_w("pool", wki2[:, :, 64:128], Wv[:, :, 4608:4672], Bwki2)
                    load_w("pool", wwi[:], Wv[:, :, 4672:4680], Bwwi)
                    gens = []
                    for tb in range(4):
                        for ch in range(4):
                            def srcqi(tb=tb, ch=ch):
                                pb, Bpb = PA.get()
                                for kc in range(8):
                                    mm(pb[:], wqi[:, kc, ch * 128:(ch + 1) * 128], xnT[:, kc, tsl(tb)], kc == 0, kc == 7,
                                       [Bwqi, BxnT[tb]], [Bpb])
                                return pb[:], Bpb
                            gens.append(norm_rope_g(srcqi, 128, None, None, None, mat(M_RIDX), Ct, St, tb, qiT[:, ch, tsl(tb)], BqiT[tb], f32r, bfr))

                        def srcki(tb=tb):
                            pb, Bpb = PA.get()
                            for kc in range(8):
                                mm(pb[:], wki2[:, kc, :], xnT[:, kc, tsl(tb)], kc == 0, kc == 7, [Bwki2, BxnT[tb]], [Bpb])
                            return pb[:], Bpb
                        gens.append(norm_rope_g(srcki, 128, None, None, None, mat(M_RIDX), Ct, St, tb, kiT2[0:64, 0, tsl(tb)], BkiT[tb], f32r, bfr,
                                                out_hi=kiT2[64:128, 1, tsl(tb)], Bout_hi=BkiT[tb]))
                    run_pipe(gens, PIPE_DEPTH)
                    for tb in range(4):
                        pw, Bpw = PO.get()
                        for t4 in range(4):
                            tile_ = tb * 4 + t4
                            for kc in range(8):
                                mm(pw[:, t4 * 8:(t4 + 1) * 8], xnT[:, kc, tile_ * 128:(tile_ + 1) * 128], wwi[:, kc, :], kc == 0, kc == 7,
                                   [Bwwi, BxnT[tb]], [Bpw])
                        ts("dve", wi_sb[:, tb * 4:(tb + 1) * 4, :], pw[:, 0:32].rearrange("p (a b) -> p a b", b=8),
                           (8.0 ** -0.5) * (64.0 ** -0.5), None, ALU.mult, None, [Bpw], [Bwi])
                    barrier()
                scr = Ring(nc, es2, "b_sc", 2, [128, SEQ], F32)
                junkr = Ring(nc, es2, "b_junk", 2, [128, SEQ], BF16)
                mskr = Ring(nc, es2, "b_msk", 1, [128, SEQ], BF16)
                dgr = Ring(nc, es2, "b_dg", 2, [128, 8, 128], BF16)
                str_ = Ring(nc, es2, "b_st", 2, [128, 8], F32)
                dlr = Ring(nc, es2, "b_dl", 2, [128, NBIS + 1], F32)
                ev = [0]

                def score_tile(t):
                    svis = 128 * (t + 1)
                    sc, Bsc = scr.get()
                    dg, Bdg = dgr.get()
                    for h in range(8):
                        ts("dve", dg[:, h, :], mat(M_ID), wi_sb[:, t, h:h + 1], None, ALU.mult, None, [Bwi, Bcmat], [Bdg])
                    nsb = (svis + 511) // 512
                    units = [(sbk, h) for sbk in range(nsb) for h in range(8)]

                    def issue_rp(sbk, h):
                        w = min(512, svis - 512 * sbk)
                        rp, Brp = PS.get()
                        mm(rp[:, 0:w], qiT[:, h // 2, t * 128:(t + 1) * 128], kiT2[:, h % 2, sbk * 512:sbk * 512 + w],
                           True, True, [BqiT[t // 4], BkiT[sbk]], [Brp])
                        return rp, Brp, w
                    LOOK = 3
                    pend = [issue_rp(*u) for u in units[:LOOK]]
                    sp = Bsp = None
                    for idx, (sbk, h) in enumerate(units):
                        rp, Brp, w = pend.pop(0)
                        if idx + LOOK < len(units):
                            pend.append(issue_rp(*units[idx + LOOK]))
                        if h == 0:
                            sp, Bsp = PO.get()
                        rl, Brl = bfr.get()
                        if h % 2 == 0:
                            act(rl[:, 0:w], rp[:, 0:w], AF.Relu, [Brp], [Brl])
                        else:
                            ts("dve", rl[:, 0:w], rp[:, 0:w], 0.0, None, ALU.max, None, [Brp], [Brl])
                        mm(sp[:, 0:w], dg[:, h, :], rl[:, 0:w], h == 0, h == 7, [Bdg, Brl], [Bsp], inc=True)
                        if h == 7:
                            act(sc[:, sbk * 512:sbk * 512 + w], sp[:, 0:w], AF.Copy, [Bsp], [Bsc])
                    st, Bst = str_.get()
                    dl, Bdl = dlr.get()
                    if t >= 2:
                        S.op("dve", lambda: nc.vector.tensor_reduce(out=st[:, 0:1], in_=sc[:, 0:svis], axis=AX.X, op=ALU.max),
                             reads=[Bsc], writes=[Bst])
                        S.op("dve", lambda: nc.vector.tensor_reduce(out=st[:, 1:2], in_=sc[:, 0:svis], axis=AX.X, op=ALU.min),
                             reads=[Bsc], writes=[Bst])
                    S.op("pool", lambda: nc.gpsimd.memset(sc[0:64, svis - 64:svis], NEG), writes=[Bsc])
                    junk, Bjunk = junkr.get()
                    return dict(t=t, svis=svis, sc=sc, Bsc=Bsc, st=st, Bst=Bst, dl=dl, Bdl=Bdl, junk=junk, Bjunk=Bjunk)

                def bisect(tiles):
                    tiles = [T for T in tiles if T["t"] >= 2]
                    for k_, T in enumerate(tiles):
                        T["neg"] = (k_ % 2 == 1)
                        st, Bst, dl, Bdl = T["st"], T["Bst"], T["dl"], T["Bdl"]
                        tt("dve", st[:, 2:3], st[:, 0:1], st[:, 1:2], ALU.subtract, [Bst], [Bst])
                        ts("dve", st[:, 2:3], st[:, 2:3], 1.001, 1e-6, ALU.mult, ALU.add, [Bst], [Bst])
                        if not T["neg"]:
                            ts("dve", dl[:], cols[:, CK + 3:CK + 4 + NBIS], st[:, 2:3], None, ALU.mult, None, [Bst, Bcols], [Bdl])
                            tt("dve", st[:, 3:4], st[:, 0:1], st[:, 2:3], ALU.subtract, [Bst], [Bst])
                            tt("dve", st[:, 3:4], st[:, 3:4], dl[:, 0:1], ALU.add, [Bst, Bdl], [Bst])
                        else:
                            ts("dve", st[:, 7:8], st[:, 2:3], -1.0, None, ALU.mult, None, [Bst], [Bst])
                            ts("dve", dl[:], cols[:, CK + 3:CK + 4 + NBIS], st[:, 7:8], None, ALU.mult, None, [Bst, Bcols], [Bdl])
                            tt("dve", st[:, 3:4], st[:, 2:3], st[:, 0:1], ALU.subtract, [Bst], [Bst])
                            tt("dve", st[:, 3:4], st[:, 3:4], dl[:, 0:1], ALU.add, [Bst, Bdl], [Bst])
                    for i in range(NBIS):
                        for T in tiles:
                            st, Bst, dl, Bdl = T["st"], T["Bst"], T["dl"], T["Bdl"]
                            if not T["neg"]:
                                ts("dve", T["junk"][:, 0:T["svis"]], T["sc"][:, 0:T["svis"]], st[:, 3:4], 0.0, ALU.is_ge, ALU.add,
                                   [T["Bsc"], Bst], [T["Bjunk"], Bst], accum_out=st[:, 4:5])
                            else:
                                act(T["junk"][:, 0:T["svis"]], T["sc"][:, 0:T["svis"]], AF.Sign, [T["Bsc"], Bst], [T["Bjunk"], Bst],
                                    bias=st[:, 3:4], scale=1.0, accum_out=st[:, 4:5])
                        for T in tiles:
                            st, Bst = T["st"], T["Bst"]
                            thr_cnt = float(TOPK) if not T["neg"] else float(2 * TOPK - T["svis"])
                            ts("dve", st[:, 5:6], st[:, 4:5], thr_cnt, 0.5, ALU.is_ge, ALU.subtract, [Bst], [Bst])
                        for T in tiles:
                            st, Bst, dl, Bdl = T["st"], T["Bst"], T["dl"], T["Bdl"]
                            stt("dve", st[:, 3:4], st[:, 5:6], dl[:, i:i + 1], st[:, 3:4], ALU.mult, ALU.add, [Bst, Bdl], [Bst])
                    for T in tiles:
                        st, Bst, dl, Bdl = T["st"], T["Bst"], T["dl"], T["Bdl"]
                        if not T["neg"]:
                            tt("dve", st[:, 6:7], st[:, 3:4], dl[:, NBIS:NBIS + 1], ALU.subtract, [Bst, Bdl], [Bst])
                        else:
                            stt("dve", st[:, 6:7], st[:, 3:4], -1.0, dl[:, NBIS:NBIS + 1], ALU.mult, ALU.add, [Bst, Bdl], [Bst])

                def mask_tile(T):
                    t, svis, sc, Bsc, st, Bst = T["t"], T["svis"], T["sc"], T["Bsc"], T["st"], T["Bst"]
                    if t < 2:
                        S.op("pool", lambda: nc.gpsimd.memset(st[:, 6:7], NEG / 2), writes=[Bst])
                    mk, Bmk = mskr.get()
                    ts("dve", mk[:, 0:svis], sc[:, 0:svis], st[:, 6:7], None, ALU.is_ge, None, [Bsc, Bst], [Bmk])
                    for i0 in range(0, t + 1, 4):
                        nb = min(4, t + 1 - i0)
                        tp, Btp = PS.get()
                        for bi in range(nb):
                            i = i0 + bi
                            mm(tp[:, bi * 128:(bi + 1) * 128], mk[:, i * 128:(i + 1) * 128], mat(M_ID), True, True, [Bmk, Bcmat], [Btp])
                        for bi in range(nb):
                            i = i0 + bi
                            dst = maskT[:, MOFF[i] + 128 * (t - i):MOFF[i] + 128 * (t - i) + 128]
                            act(dst, tp[:, bi * 128:(bi + 1) * 128], AF.Copy, [Btp], [BmaskT[i]])

                for tp_ in range(8 if STOP_AT >= 1.2 else 0):
                    Ts = [score_tile(2 * tp_), score_tile(2 * tp_ + 1)]
                    if STOP_AT >= 1.5:
                        bisect(Ts)
                    if STOP_AT >= 1.8:
                        for T in Ts:
                            mask_tile(T)
                barrier()
            with ExitStack() as es_tmp:
                Ct, St = make_tables(es, es_tmp, CK + 0, "b6")
            wr = Ring(nc, es, "b_w", 5, [128, 8, 128], BF16)
            wor = Ring(nc, es, "b_wo", 2, [128, 1024], BF16)
            qT = sb("b_qT", [128, SEQ], BF16, es)
            Bq = Buf("b_qT")
            kT = sb("b_kT", [128, 2, SEQ], BF16, es)
            Bk = Buf("b_kT")
            S.op("pool", lambda: nc.gpsimd.memset(kT[64:128, 0, :], 0.0), writes=[Bk])
            S.op("pool", lambda: nc.gpsimd.memset(kT[0:64, 1, :], 0.0), writes=[Bk])
            va = sb("b_va", [128, 16, 2, 128], BF16, es)
            Bva = Buf("b_va")
            sz = sb("b_sz", [128, SEQ], BF16, es)
            Bsz = Buf("b_sz")
            gT = sb("b_gT", [128, SEQ], BF16, es)
            BgT = Buf("b_gT")
            S.op("pool", lambda: nc.gpsimd.memset(va[:, :, 0, 64:128], 1.0), writes=[Bva])
            S.op("pool", lambda: nc.gpsimd.memset(va[:, :, 1, 0:64], 1.0), writes=[Bva])
            Wo = b_w_out[0]

            def mask_fn(i, q0, N):
                o = MOFF[i] + (q0 - 128 * i)
                return maskT[:, o:o + N], BmaskT[i]

            B_NR = subring([0, 1, 2, 3])
            B_ZV = subring([4])
            B_WO = subring([5, 6, 7])
            chunk_state = {}

            def proj_chunk_g(c):
                ws = []
                for g in range(4):
                    w_, Bw_ = wr.get()
                    load_w("pool", w_[:], Wv[:, :, g * 1024 + c * 128:g * 1024 + (c + 1) * 128], Bw_)
                    ws.append((w_, Bw_))
                wo, Bwo = wor.get()
                load_w("pool", wo[:], Wo[c * 128:(c + 1) * 128, :], Bwo)
                chunk_state[c] = (wo, Bwo)
                (wq, Bwq), (wk, Bwk), (wv_, Bwv), (wz, Bwz) = ws
                gens = []
                for tb in range(4):
                    for (w_, Bw_, gcl, isk) in ((wq, Bwq, CB + 8, False), (wk, Bwk, CB + 9, True)):
                        def srcp(tb=tb, w_=w_, Bw_=Bw_):
                            pb, Bpb = B_NR.get()
                            for kc in range(8):
                                mm(pb[:], w_[:, kc, :], xnT[:, kc, tsl(tb)], kc == 0, kc == 7, [Bw_, BxnT[tb]], [Bpb])
                            return pb[:], Bpb
                        if isk:
                            gens.append(norm_rope_g(srcp, 128, mat(M_B64), 64, col(gcl), mat(M_R64), Ct, St, tb, kT[0:64, 0, tsl(tb)], Bk, f32r, bfr,
                                                    out_hi=kT[64:128, 1, tsl(tb)], Bout_hi=Bk))
                        else:
                            gens.append(norm_rope_g(srcp, 128, mat(M_B64), 64, col(gcl), mat(M_R64), Ct, St, tb, qT[:, tsl(tb)], Bq, f32r, bfr))
                PROJ[0] = B_NR
                yield from run_pipe_g(gens, PIPE_DEPTH)
                PROJ[0] = PA
                for tb in range(4):
                    pb, Bpb = B_ZV.get()
                    for kc in range(8):
                        mm(pb[:], wz[:, kc, :], xnT[:, kc, tsl(tb)], kc == 0, kc == 7, [Bwz, BxnT[tb]], [Bpb])
                    act(sz[:, tsl(tb)], pb[:], AF.Silu, [Bpb], [Bsz])
                    yield
                    pv, Bpv = B_ZV.get()
                    for t4 in range(4):
                        tile_ = tb * 4 + t4
                        for kc in range(8):
                            mm(pv[:, t4 * 128:(t4 + 1) * 128], xnT[:, kc, tile_ * 128:(tile_ + 1) * 128], wv_[:, kc, :], kc == 0, kc == 7,
                               [Bwv, BxnT[tb]], [Bpv])
                    pv3 = pv[:].rearrange("p (a b) -> p a b", b=128)
                    act(va[:, tb * 4:(tb + 1) * 4, 0, 0:64], pv3[:, :, 0:64], AF.Copy, [Bpv], [Bva])
                    act(va[:, tb * 4:(tb + 1) * 4, 1, 64:128], pv3[:, :, 64:128], AF.Copy, [Bpv], [Bva])
                    yield

            heads = []
            for hh in range(2):
                lo = hh * 64
                heads.append(dict(k=(kT[:, hh, :], Bk), q=(qT[:, :], Bq),
                                  v=(lambda i, hh=hh: va[:, i, hh, :]), Bv=Bva,
                                  osl=slice(lo, lo + 64), lsl=slice(64 - lo, 128 - lo)))
            NCH = 8 if STOP_AT >= 3 else 0
            if NCH:
                for _ in proj_chunk_g(0):
                    pass
            for c in range(NCH):
                wo, Bwo = chunk_state[c]
                attention(heads, 0.125, mask_fn, gT, BgT, sz, Bsz, f32r, bfr, heat=HEAT)
                side = proj_chunk_g(c + 1) if c + 1 < NCH else None
                co_run(wout_g(wo, Bwo, gT, BgT, B_WO), side, 2)
            barrier()

    for L in layers:
        if L[0] == "A":
            layer_A(int(L[1]))
        elif L[0] == "C":
            layer_C()
        elif L[0] == "B":
            layer_B()
        else:
            raise NotImplementedError(L)

    Bout = Buf("out")
    for kc in range(8):
        S.dma("sp", yT_d[kc * 128:(kc + 1) * 128, :], xT[:, kc, :], reads=BxT[kc], own=Bout)
    S.finish([Bout])
    ges.close()
    return nc


def _pack_cols(inp):
    c = np.zeros((128, NCOLS), np.float32)

    def colmajor(v):
        return np.ascontiguousarray(v.reshape(-1, 128).T)
    for j in range(2):
        b = CA[j]
        c[:, b:b + 8] = colmajor(inp["a_norm"][j])
        for k in range(3):
            c[:, b + 8 + 8 * k:b + 16 + 8 * k] = colmajor(inp["a_conv_w"][j, k])
        c[:, b + 32:b + 40] = colmajor(inp["a_conv_b"][j])
    c[:, CB:CB + 8] = colmajor(inp["b_norm"][0])
    c[:, CB + 8] = np.tile(inp["b_q_norm"][0], 2)
    c[:, CB + 9] = np.tile(inp["b_k_norm"][0], 2)
    c[:, CC:CC + 8] = colmajor(inp["c_norm"][0])
    c[:, CC + 8:CC + 11] = colmajor(inp["c_q_lat_norm"][0])
    c[:, CC + 11:CC + 13] = colmajor(inp["c_kv_lat_norm"][0])
    c[:96, CC + 13] = inp["c_q_norm"][0]
    c[:96, CC + 14] = inp["c_k_norm"][0]
    p = np.arange(128)
    theta = 10000.0
    c[:, CK + 0] = (theta ** (-(np.arange(32, dtype=np.float32)) / 32.0)).astype(np.float32)[p % 32]
    inv16 = (theta ** (-(np.arange(16, dtype=np.float32)) / 16.0)).astype(np.float32)
    c[:, CK + 1] = np.where((p % 64) < 32, inv16[p % 16], 0.0)
    c[:, CK + 2] = np.where((p >= 64) & (p < 96), inv16[(p - 64) % 16], 0.0)
    for i in range(NBIS + 1):
        c[:, CK + 3 + i] = 2.0 ** -(i + 1)
    return c


def _const_mats():
    m = np.zeros((128, NMAT, 128), np.float32)
    m[:, M_ID, :] = np.eye(128)
    m[:, M_ONE, :] = 1.0
    m[0:64, M_B64, 0:64] = 1.0
    m[64:128, M_B64, 64:128] = 1.0
    for o in range(128):
        dd = o % 64
        if dd < 32:
            m[o + 32, M_R64, o] = -1.0
        else:
            m[o - 32, M_R64, o] = 1.0
        if dd < 16:
            m[o + 16, M_RIDX, o] = -1.0
        elif dd < 32:
            m[o - 16, M_RIDX, o] = 1.0
        if 64 <= o < 80:
            m[o + 16, M_RMLA, o] = -1.0
        elif 80 <= o < 96:
            m[o - 16, M_RMLA, o] = 1.0
    return m.reshape(128, NMAT * 128)


LAYERS = ["A0", "B0", "C0", "A1"]
_NC_CACHE = {}


def kernel(**inp):
    inp = {k: np.asarray(v) for k, v in inp.items()}
    key = tuple(LAYERS)
    if key not in _NC_CACHE:
        _NC_CACHE[key] = build(LAYERS)
    nc = _NC_CACHE[key]
    cols = _pack_cols(inp)
    cmat = _const_mats()
    x = inp["x"]
    in_maps = []
    for b in range(8):
        m = {
            "xT": np.ascontiguousarray(x[b].T),
            "pos": np.ascontiguousarray(inp["positions"][b:b + 1].astype(np.int32)),
            "cols": cols, "cmat": cmat,
        }
        for k in ("a_w_in", "a_w_out", "b_w_in", "b_w_out", "c_w_in", "c_w_uq", "c_w_ukv", "c_w_out"):
            m[k] = np.ascontiguousarray(inp[k], dtype=np.float32)
        in_maps.append(m)
    res = run_bass_kernel_spmd(nc, in_maps, core_ids=list(range(8)))
    out = np.stack([np.ascontiguousarray(res.results[b]["yT"].T) for b in range(8)], axis=0)
    return out.astype(np.float32)
```

```python
import math
from contextlib import ExitStack
import numpy as np
import concourse.bass as bass
import concourse.mybir as mybir
from concourse.bass_utils import run_bass_kernel_spmd

F32 = mybir.dt.float32
BF16 = mybir.dt.bfloat16
I32 = mybir.dt.int32
ALU = mybir.AluOpType
AF = mybir.ActivationFunctionType
AX = mybir.AxisListType

SEQ = 2048
D = 1024
EPS = 1e-6
NEG = -1.0e30
TOPK = 256
NBIS = 16
PIPE_DEPTH = 3
HEAT = 0
PV_DELAY = 2
C_FINE = False
C_DEPTH = 3
C_RATIO = 10 ** 9
STOP_AT = 3


class Buf:
    __slots__ = ("name", "w", "r", "dsem", "dcnt")

    def __init__(self, name):
        self.name = name
        self.w = None
        self.r = []
        self.dsem = None
        self.dcnt = 0


class Sched:
    def __init__(self, nc):
        self.nc = nc
        self.eng = {"pe": nc.tensor, "act": nc.scalar, "dve": nc.vector,
                    "pool": nc.gpsimd, "sp": nc.sync}
        self.sem = {}
        self.cnt = {}
        for e in self.eng:
            self.sem[e] = nc.alloc_semaphore("c_" + e)
            self.cnt[e] = 0
        self.seen = {e: {} for e in self.eng}
        self.dsems = []
        self.free_dsems = []

    def _semof(self, key):
        return self.sem[key] if isinstance(key, str) else key

    def _deps(self, q, reads, writes, is_dma=False):
        need = {}

        def add(d, same_ok):
            if d is None:
                return
            key, val = d
            if same_ok and key == q and not is_dma and q != "pool":
                return
            if key == "pe" and q == "pe" and not is_dma:
                return
            if self.seen[q].get(key, 0) >= val:
                return
            if need.get(key, 0) < val:
                need[key] = val
        for b in reads:
            add(b.w, False)
        for b in writes:
            add(b.w, True)
            for d in b.r:
                add(d, True)
        return need

    def _emit(self, q, need, fn):
        items = list(need.items())
        eng = self.eng[q]
        for key, val in items[:-1]:
            eng.wait_ge(self._semof(key), val)
            self.seen[q][key] = val
        inst = fn()
        if items:
            key, val = items[-1]
            inst._wait_ge(self._semof(key), val)
            self.seen[q][key] = val
        return inst

    def op(self, q, fn, reads=(), writes=(), inc=True):
        need = self._deps(q, reads, writes)
        inst = self._emit(q, need, fn)
        if inc:
            self.cnt[q] += 1
            inst.then_inc(self.sem[q], 1)
            d = (q, self.cnt[q])
        else:
            d = (q, self.cnt[q] + 1)
        for b in reads:
            if len(b.r) > 24:
                b.r = self._compact(b.r)
            b.r.append(d)
        for b in writes:
            b.w = d
            b.r = []
        return inst

    @staticmethod
    def _compact(lst):
        best = {}
        for k, v in lst:
            if best.get(k, 0) < v:
                best[k] = v
        return list(best.items())

    def dma(self, q, out, in_, reads=(), writes=(), own=None):
        if own is None:
            own = writes[0] if writes else reads[0]
        if own.dsem is None:
            self.dsems.append(own)
            own.dsem = self.nc.alloc_semaphore(f"d{len(self.dsems)}_" + own.name)
        need = self._deps(q, reads, writes, is_dma=True)
        inst = self._emit(q, need, lambda: self.eng[q].dma_start(out=out, in_=in_))
        own.dcnt += 16
        inst.then_inc(own.dsem, 16)
        d = (own.dsem, own.dcnt)
        for b in reads:
            b.r.append(d)
        for b in writes:
            b.w = d
            b.r = []
        return inst

    def finish(self, bufs):
        need = {}
        for b in bufs:
            for d in ([b.w] if b.w else []) + list(b.r):
                key, val = d
                if need.get(key, 0) < val:
                    need[key] = val
        for key, val in need.items():
            self.eng["sp"].wait_ge(self._semof(key), val)


class Ring:
    def __init__(self, nc, es, name, n, shape, dtype, psum=False):
        self.items = []
        for i in range(n):
            nm = f"{name}{i}"
            if psum:
                t = es.enter_context(nc.psum_tensor(nm, shape, dtype))
            else:
                t = es.enter_context(nc.sbuf_tensor(nm, shape, dtype))
            self.items.append((t, Buf(nm)))
        self.i = 0

    def get(self):
        it = self.items[self.i % len(self.items)]
        self.i += 1
        return it


CA = [0, 40]
CB = 80
CC = 90
CK = 105
NCOLS = 108 + NBIS + 1
M_ID, M_ONE, M_B64, M_R64, M_RIDX, M_RMLA = 0, 1, 2, 3, 4, 5
NMAT = 6


def tsl(tb):
    return slice(tb * 512, (tb + 1) * 512)


def build(layers):
    nc = bass.Bass("TRN2", target_bir_lowering=False)
    S = Sched(nc)

    def din(name, shape, dt=F32):
        return nc.dram_tensor(name, shape, dt, kind="ExternalInput")

    xT_d = din("xT", [D, SEQ]).ap()
    pos_d = din("pos", [1, SEQ], I32)
    cols_d = din("cols", [128, NCOLS]).ap()
    cmat_d = din("cmat", [128, NMAT * 128]).ap()
    a_w_in = din("a_w_in", [2, 1024, 4096]).ap()
    a_w_out = din("a_w_out", [2, 1024, 1024]).ap()
    b_w_in = din("b_w_in", [1, 1024, 4680]).ap()
    b_w_out = din("b_w_out", [1, 1024, 1024]).ap()
    c_w_in = din("c_w_in", [1, 1024, 1696]).ap()
    c_w_uq = din("c_w_uq", [1, 384, 1536]).ap()
    c_w_ukv = din("c_w_ukv", [1, 256, 2048]).ap()
    c_w_out = din("c_w_out", [1, 1024, 1024]).ap()
    yT_d = nc.dram_tensor("yT", [D, SEQ], F32, kind="ExternalOutput").ap()

    ges = ExitStack()

    def sb(name, shape, dt, es=None):
        return (es or ges).enter_context(nc.sbuf_tensor(name, shape, dt))

    xT = sb("xT_sb", [128, 8, SEQ], F32)
    BxT = [[Buf(f"xT{k}_{t}") for t in range(4)] for k in range(8)]
    xnT = sb("xnT", [128, 8, SEQ], BF16)
    BxnT = [Buf(f"xnT{t}") for t in range(4)]
    cols = sb("cols_sb", [128, NCOLS], F32)
    Bcols = Buf("cols")
    cmat = sb("cmat_sb", [128, NMAT, 128], BF16)
    Bcmat = Buf("cmat")
    PS = Ring(nc, ges, "psg", 4, [128, 512], F32, psum=True)
    PO = Ring(nc, ges, "pso", 4, [128, 512], F32, psum=True)

    PA = Ring.__new__(Ring)
    PA.items = PS.items + PO.items
    PA.i = 0
    _all8 = list(PA.items)
    PA.items = _all8[0:7]
    PO.items = _all8[4:7]

    def subring(idx):
        r = Ring.__new__(Ring)
        r.items = [_all8[i] for i in idx]
        r.i = 0
        return r
    PROJ = [PA]
    T1ENG = ["dve"]
    FINE = [False]
    HEATBANK = _all8[7][0]
    heat_l = cmat[:, M_ONE, :]
    heat_r = cmat[:, 0:4, :].rearrange("p a b -> p (a b)")

    def col(i):
        return cols[:, i:i + 1]

    def mat(i, p=128, m=128):
        return cmat[0:p, i, 0:m]

    ALLDMA = []
    _orig_dma = S.dma

    def dma_track(q, out, in_, reads=(), writes=(), own=None):
        o = own if own is not None else (writes[0] if writes else reads[0])
        if o not in ALLDMA:
            ALLDMA.append(o)
        return _orig_dma(q, out, in_, reads=reads, writes=writes, own=own)
    S.dma = dma_track

    for kc in range(8):
        S.dma("sp", xT[:, kc, :], xT_d[kc * 128:(kc + 1) * 128, :], writes=BxT[kc])
    S.dma("sp", cols[:], cols_d[:, :], writes=[Bcols])
    S.dma("pool", cmat[:].rearrange("p a b -> p (a b)"), cmat_d[:, :], writes=[Bcmat])

    def mm(out, lhsT, rhs, start, stop, reads, writes, inc=None):
        if inc is None:
            inc = stop
        S.op("pe", lambda: nc.tensor.matmul(out, lhsT, rhs, start=start, stop=stop),
             reads=reads, writes=writes, inc=inc)

    def act(out, in_, func, reads, writes, **kw):
        S.op("act", lambda: nc.scalar.activation(out=out, in_=in_, func=func, **kw),
             reads=reads, writes=writes)

    def tt(q, out, in0, in1, op, reads, writes):
        e = S.eng[q]
        S.op(q, lambda: e.tensor_tensor(out=out, in0=in0, in1=in1, op=op), reads=reads, writes=writes)

    def ts(q, out, in0, s1, s2, op0, op1, reads, writes, **kw):
        e = S.eng[q]
        if s2 is None:
            S.op(q, lambda: e.tensor_scalar(out=out, in0=in0, scalar1=s1, scalar2=None, op0=op0, **kw),
                 reads=reads, writes=writes)
        else:
            S.op(q, lambda: e.tensor_scalar(out=out, in0=in0, scalar1=s1, scalar2=s2, op0=op0, op1=op1, **kw),
                 reads=reads, writes=writes)

    def stt(q, out, in0, scalar, in1, op0, op1, reads, writes):
        e = S.eng[q]
        S.op(q, lambda: e.scalar_tensor_tensor(out=out, in0=in0, scalar=scalar, in1=in1, op0=op0, op1=op1),
             reads=reads, writes=writes)

    def rstd_from_ssq(ssq_ps, Bssq, P, n, N, f32r):
        lnv, Bl = f32r.get()
        act(lnv[0:P, 0:N], ssq_ps, AF.Ln, [Bssq], [Bl], scale=1.0 / n, bias=EPS)
        rs, Br = f32r.get()
        act(rs[0:P, 0:N], lnv[0:P, 0:N], AF.Exp, [Bl], [Br], scale=-0.5)
        return rs, Br

    def emit_norm(gc, f32r, sqr):
        for tb in range(4):
            pb, Bpb = PS.get()
            for kc in range(8):
                sq, Bsq = sqr.get()
                act(sq[:], xT[:, kc, tsl(tb)], AF.Square, [BxT[kc][tb]], [Bsq])
                mm(pb[:], mat(M_ONE), sq[:], kc == 0, kc == 7, [Bsq, Bcmat], [Bpb], inc=True)
            rs, Br = rstd_from_ssq(pb[:], Bpb, 128, 1024.0, 512, f32r)
            for kc in range(8):
                stt("dve", xnT[:, kc, tsl(tb)], xT[:, kc, tsl(tb)], col(gc + kc), rs[:], ALU.mult, ALU.mult,
                    [BxT[kc][tb], Br, Bcols], [BxnT[tb]])

    def wout_partial(wo_ap, Bwo, g_ap, Bg, first_last=None):
        for dc in range(8):
            for tb in range(4):
                pb, Bpb = PS.get()
                mm(pb[:], wo_ap[:, dc * 128:(dc + 1) * 128], g_ap[:, tsl(tb)], True, True, [Bwo, Bg], [Bpb])
                tt("dve", xT[:, dc, tsl(tb)], xT[:, dc, tsl(tb)], pb[:], ALU.add,
                   [BxT[dc][tb], Bpb], [BxT[dc][tb]])

    def wout_g(wo_ap, Bwo, g_ap, Bg, ring):
        items = [(dc, tb) for dc in range(8) for tb in range(4)]
        look = len(ring.items) - 1

        def issue(k):
            dc, tb = items[k]
            pb, Bpb = ring.get()
            mm(pb[:], wo_ap[:, dc * 128:(dc + 1) * 128], g_ap[:, tsl(tb)], True, True, [Bwo, Bg], [Bpb])
            return pb, Bpb
        pend = [issue(k) for k in range(look)]
        for k, (dc, tb) in enumerate(items):
            pb, Bpb = pend.pop(0)
            if k + look < len(items):
                pend.append(issue(k + look))
            tt("dve", xT[:, dc, tsl(tb)], xT[:, dc, tsl(tb)], pb[:], ALU.add,
               [BxT[dc][tb], Bpb], [BxT[dc][tb]])
            yield

    def layer_A(j):
        cb = CA[j]
        with ExitStack() as es:
            f32r = Ring(nc, es, f"a{j}f32_", 6, [128, 512], F32)
            sqr = Ring(nc, es, f"a{j}sq_", 3, [128, 512], BF16)
            emit_norm(cb, f32r, sqr)
            gT = sb(f"a{j}_gT", [128, 8, SEQ], BF16, es)
            BgT = [[Buf(f"gT{c}_{t}") for t in range(4)] for c in range(8)]
            wout = sb(f"a{j}_wout", [128, 8, 1024], BF16, es)
            Bwout = Buf("a_wout")
            S.dma("pool", wout[:], a_w_out[j].rearrange("(cc p) d -> p cc d", p=128), writes=[Bwout])
            war = [(sb(f"a{j}_w{i}", [128, 8, 4, 128], BF16, es), [Buf(f"a{j}_w{i}_{g}") for g in range(4)]) for i in range(2)]
            stg = Ring(nc, es, f"a{j}_stg", 2, [128, 8, 128], F32)
            ur = [(sb(f"a{j}_u{i}", [128, 2 + SEQ], F32, es), [Buf(f"a_u{i}_{t}") for t in range(5)]) for i in range(2)]
            wv = a_w_in[j].rearrange("(kc p) (g c i) -> p kc g c i", p=128, g=4, c=8)

            def issue_loads(c):
                wa, Bwa = war[c % 2]
                for g in (0, 1):
                    S.dma("pool", wa[:, :, g, :], wv[:, :, g, c, :], writes=[Bwa[g]])
                stgs = []
                for g in (2, 3):
                    st_, Bst_ = stg.get()
                    S.dma("sp", st_[:], wv[:, :, g, c, :], writes=[Bst_])
                    stgs.append((g, st_, Bst_))
                return stgs

            def issue_casts(c, stgs):
                wa, Bwa = war[c % 2]
                for g, st_, Bst_ in stgs:
                    act(wa[:, :, g, :], st_[:], AF.Copy, [Bst_], [Bwa[g]])

            issue_casts(0, issue_loads(0))
            for c in range(8):
                wa, Bwa = war[c % 2]
                nxt = issue_loads(c + 1) if c + 1 < 8 else None
                u, Bu = ur[c % 2]
                S.op("pool", lambda: nc.gpsimd.memset(u[:, 0:2], 0.0), writes=[Bu[4]])
                for tb in range(4):
                    banks = [_all8[(tb % 2) * 4 + g_] for g_ in range(4)]
                    for g in range(4):
                        for kc in range(8):
                            mm(banks[g][0][:], wa[:, kc, g, :], xnT[:, kc, tsl(tb)], kc == 0, kc == 7,
                               [Bwa[g], BxnT[tb]], [banks[g][1]])
                    (bg, Bbg), (cg, Bcg), (hv, Bhv), (z, Bz) = banks
                    hvs, Bhvs = f32r.get()
                    act(hvs[:], hv[:], AF.Copy, [Bhv], [Bhvs])
                    us = slice(2 + tb * 512, 2 + (tb + 1) * 512)
                    tt("dve", u[:, us], cg[:], hvs[:], ALU.mult, [Bcg, Bhvs], [Bu[tb]])
                    prev = Bu[tb - 1] if tb > 0 else Bu[4]
                    y, By = f32r.get()
                    ts("pool", y[:], u[:, us], col(cb + 24 + c), col(cb + 32 + c), ALU.mult, ALU.add,
                       [Bu[tb], Bcols], [By])
                    stt("dve", y[:], u[:, 1 + tb * 512:1 + (tb + 1) * 512], col(cb + 16 + c), y[:], ALU.mult, ALU.add,
                        [Bu[tb], prev, Bcols, By], [By])
                    stt("dve", y[:], u[:, tb * 512:(tb + 1) * 512], col(cb + 8 + c), y[:], ALU.mult, ALU.add,
                        [Bu[tb], prev, Bcols, By], [By])
                    szs, Bszs = f32r.get()
                    act(szs[:], z[:], AF.Silu, [Bz], [Bszs])
                    tt("dve", szs[:], bg[:], szs[:], ALU.mult, [Bbg, Bszs], [Bszs])
                    tt("pool", gT[:, c, tsl(tb)], szs[:], y[:], ALU.mult, [Bszs, By], [BgT[c][tb]])
                    if tb == 1 and nxt is not None:
                        issue_casts(c + 1, nxt)
            for dc in range(8):
                for tb in range(4):
                    pb, Bpb = PS.get()
                    for cc in range(8):
                        mm(pb[:], wout[:, cc, dc * 128:(dc + 1) * 128], gT[:, cc, tsl(tb)], cc == 0, cc == 7,
                           [Bwout, BgT[cc][tb]], [Bpb])
                    tt("dve", xT[:, dc, tsl(tb)], xT[:, dc, tsl(tb)], pb[:], ALU.add,
                       [BxT[dc][tb], Bpb], [BxT[dc][tb]])
            barrier()

    def barrier():
        for q in ("pe", "act", "dve", "pool"):
            pass
        for q in S.eng:
            for e in S.eng:
                if e == q:
                    continue
                v = S.cnt[e]
                if v > 0 and S.seen[q].get(e, 0) < v:
                    S.eng[q].wait_ge(S.sem[e], v)
                    S.seen[q][e] = v
        for b in ALLDMA:
            if b.dsem is not None and b.dcnt > 0:
                for q in S.eng:
                    if S.seen[q].get(b.dsem, 0) < b.dcnt:
                        S.eng[q].wait_ge(b.dsem, b.dcnt)
                        S.seen[q][b.dsem] = b.dcnt


    def make_tables(es_tab, es_tmp, invc, name):
        tabs = []
        for nm in ("C", "S"):
            tabs.append((sb(name + "_" + nm, [128, SEQ], F32, es_tab), Buf(name + nm)))
        posi = sb(name + "_posi", [128, 512], I32, es_tmp)
        Bposi = Buf(name + "posi")
        a2 = sb(name + "_a2", [128, 512], F32, es_tmp)
        Ba2 = Buf(name + "a2")
        t1 = sb(name + "_t1", [128, 512], F32, es_tmp)
        Bt1 = Buf(name + "t1")
        ki = sb(name + "_ki", [128, 512], I32, es_tmp)
        Bki = Buf(name + "ki")
        for tb in range(4):
            S.dma("sp", posi[:], bass.AP(pos_d, tb * 512, [[0, 128], [1, 512]]), writes=[Bposi])
            S.op("dve", lambda: nc.vector.tensor_copy(out=a2[:], in_=posi[:]), reads=[Bposi], writes=[Ba2])
            ts("dve", a2[:], a2[:], col(invc), 1.0 / (2 * math.pi), ALU.mult, ALU.mult, [Ba2, Bcols], [Ba2])
            for (tab, Btab), c0 in zip(tabs, (0.25, 0.0)):
                ts("dve", t1[:], a2[:], c0, None, ALU.add, None, [Ba2], [Bt1])
                S.op("dve", lambda: nc.vector.tensor_copy(out=ki[:], in_=t1[:]), reads=[Bt1], writes=[Bki])
                S.op("dve", lambda: nc.vector.tensor_copy(out=tab[:, tsl(tb)], in_=ki[:]), reads=[Bki], writes=[Btab])
                tt("dve", t1[:], t1[:], tab[:, tsl(tb)], ALU.subtract, [Bt1, Btab], [Bt1])
                stt("dve", t1[:], t1[:], 0.5, t1[:], ALU.is_gt, ALU.subtract, [Bt1], [Bt1])
                act(tab[:, tsl(tb)], t1[:], AF.Sin, [Bt1], [Btab], scale=-2.0 * math.pi)
        barrier()
        return tabs

    def norm_rope_g(srcfn, P, blk, n, gcol, rot, Ct, St, tb, out, Bout, f32r, bfr, eng2="pool", out_hi=None, Bout_hi=None):
        (C, BC), (Sn, BS) = Ct, St
        fine = FINE[0]
        src, Bsrc = srcfn()
        if fine:
            yield
        if n is not None:
            sq, Bsq = bfr.get()
            act(sq[0:P, :], src, AF.Square, [Bsrc], [Bsq])
            yield
            pb, Bpb = PROJ[0].get()
            mm(pb[0:P, :], blk, sq[0:P, :], True, True, [Bsq, Bcmat], [Bpb])
            if fine:
                yield
            rs, Br = rstd_from_ssq(pb[0:P, :], Bpb, P, float(n), 512, f32r)
            yield
            xn, Bxn = bfr.get()
            stt("dve", xn[0:P, :], src, gcol, rs[0:P, :], ALU.mult, ALU.mult, [Bsrc, Br, Bcols], [Bxn])
        else:
            yield
            xn, Bxn = bfr.get()
            act(xn[0:P, :], src, AF.Copy, [Bsrc], [Bxn])
        if fine:
            yield
        rp, Brp = PROJ[0].get()
        mm(rp[0:P, :], rot, xn[0:P, :], True, True, [Bxn, Bcmat], [Brp])
        yield
        t1, Bt1 = f32r.get()
        tt(T1ENG[0], t1[0:P, :], xn[0:P, :], C[0:P, tsl(tb)], ALU.mult, [Bxn, BC], [Bt1])
        t2, Bt2 = f32r.get()
        tt("dve", t2[0:P, :], rp[0:P, :], Sn[0:P, tsl(tb)], ALU.mult, [Brp, BS], [Bt2])
        if fine:
            yield
        if out_hi is None:
            tt(eng2, out, t1[0:P, :], t2[0:P, :], ALU.add, [Bt1, Bt2], [Bout])
        else:
            tt(eng2, out, t1[0:64, :], t2[0:64, :], ALU.add, [Bt1, Bt2], [Bout])
            tt(eng2, out_hi, t1[64:128, :], t2[64:128, :], ALU.add, [Bt1, Bt2], [Bout_hi])

    def run_pipe_g(gens, depth):
        active = []
        it = iter(gens)
        more = True
        while True:
            for g in list(active):
                try:
                    next(g)
                except StopIteration:
                    active.remove(g)
            if more and len(active) < depth:
                try:
                    g = next(it)
                    active.append(g)
                    try:
                        next(g)
                    except StopIteration:
                        active.remove(g)
                except StopIteration:
                    more = False
            if not active and not more:
                break
            yield

    def run_pipe(gens, depth):
        for _ in run_pipe_g(gens, depth):
            pass

    def co_run(main, side, ratio):
        k = 0
        main_done = side is None and False
        while True:
            try:
                next(main)
            except StopIteration:
                break
            k += 1
            if side is not None and ratio < 0:
                for _ in range(-ratio):
                    try:
                        next(side)
                    except StopIteration:
                        side = None
                        break
            elif side is not None and k % ratio == 0:
                try:
                    next(side)
                except StopIteration:
                    side = None
        if side is not None:
            for _ in side:
                pass

    def attention_g(heads, scale, mask_fn, gT, BgT, sz, Bsz, f32r, ptr, st_ring=None, o_ring=None, heat=0, recip_dve=False):
        st_ring = st_ring or PS
        o_ring = o_ring or PO
        for j in range(4):
            for hd in heads:
                kT, Bk = hd["k"]
                qT, Bq = hd["q"]
                osl, lsl = hd["osl"], hd["lsl"]
                O, BO = o_ring.get()
                last = 4 * j + 3

                def issue_st(i):
                    off = max(0, 128 * (i - 4 * j))
                    N = 512 - off
                    q0 = 512 * j + off
                    st, Bst = st_ring.get()
                    mm(st[:, 0:N], kT[:, i * 128:(i + 1) * 128], qT[:, q0:q0 + N], True, True, [Bk, Bq], [Bst])
                    return st, Bst, off, N, q0
                LOOK = len(st_ring.items) - 1
                pv_pend = []
                pend = [issue_st(i) for i in range(min(LOOK, last + 1))]
                for i in range(last + 1):
                    st, Bst, off, N, q0 = pend.pop(0)
                    if i + LOOK <= last:
                        pend.append(issue_st(i + LOOK))
                    pt, Bpt = ptr.get()
                    act(pt[:, 0:N], st[:, 0:N], AF.Exp, [Bst], [Bpt], scale=scale)
                    if mask_fn is not None:
                        m_ap, Bm = mask_fn(i, q0, N)
                        tt("dve", pt[:, 0:N], pt[:, 0:N], m_ap, ALU.mult, [Bpt, Bm], [Bpt])
                    elif i >= 4 * j:
                        S.op("pool", lambda: nc.gpsimd.memset(pt[64:128, 0:64], 0.0), writes=[Bpt])
                    pv_pend.append((i, off, N, pt, Bpt))
                    while len(pv_pend) > PV_DELAY:
                        i_, off_, N_, pt_, Bpt_ = pv_pend.pop(0)
                        mm(O[:, off_:512], hd["v"](i_), pt_[:, 0:N_], i_ == 0, i_ == last, [hd["Bv"], Bpt_], [BO], inc=True)
                    yield
                while pv_pend:
                    i_, off_, N_, pt_, Bpt_ = pv_pend.pop(0)
                    mm(O[:, off_:512], hd["v"](i_), pt_[:, 0:N_], i_ == 0, i_ == last, [hd["Bv"], Bpt_], [BO], inc=True)
                rl, Brl = f32r.get()
                if recip_dve and j < 3:
                    S.op("dve", lambda: nc.vector.reciprocal(out=rl[lsl, :], in_=O[lsl, :]), reads=[BO], writes=[Brl])
                else:
                    act(rl[lsl, :], O[lsl, :], AF.Ln, [BO], [Brl])
                    act(rl[lsl, :], rl[lsl, :], AF.Exp, [Brl], [Brl], scale=-1.0)
                tmp, Btmp = f32r.get()
                tt("dve", tmp[osl, :], O[osl, :], rl[lsl, :], ALU.mult, [BO, Brl], [Btmp])
                tt("pool", gT[osl, tsl(j)], tmp[osl, :], sz[osl, tsl(j)], ALU.mult, [Btmp, Bsz], [BgT])

    def attention(heads, scale, mask_fn, gT, BgT, sz, Bsz, f32r, ptr, heat=0):
        for _ in attention_g(heads, scale, mask_fn, gT, BgT, sz, Bsz, f32r, ptr, heat=heat):
            pass

    def load_w(q, dst, src, Bw):
        S.dma(q, dst, src, writes=[Bw])

    def layer_C():
        W = c_w_in[0]
        with ExitStack() as es:
            f32r = Ring(nc, es, "cf32_", 6, [128, 512], F32)
            bfr = Ring(nc, es, "cbf_", 6, [128, 512], BF16)
            emit_norm(CC, f32r, bfr)
            with ExitStack() as es_tmp:
                Ct, St = make_tables(es, es_tmp, CK + 2, "ct")
            cqn = sb("c_cqn", [128, 3, SEQ], BF16, es)
            Bcqn = [Buf(f"cqn{t}") for t in range(4)]
            ckvn = sb("c_ckvn", [128, 2, SEQ], BF16, es)
            Bckvn = [Buf(f"ckvn{t}") for t in range(4)]
            krT = sb("c_krT", [128, SEQ], F32, es)
            BkrT = [Buf(f"krT{t}") for t in range(4)]
            with ExitStack() as es1:
                wcq = sb("c_wcq", [128, 8, 384], BF16, es1)
                Bwcq = Buf("wcq")
                wckv = sb("c_wckv", [128, 8, 256], BF16, es1)
                Bwckv = Buf("wckv")
                wkr = sb("c_wkr", [128, 8, 96], BF16, es1)
                Bwkr = Buf("wkr")
                Wv = W.rearrange("(kc p) n -> p kc n", p=128)
                load_w("pool", wcq[:], Wv[:, :, 0:384], Bwcq)
                load_w("pool", wckv[:], Wv[:, :, 384:640], Bwckv)
                S.op("pool", lambda: nc.gpsimd.memset(wkr[:], 0.0), writes=[Bwkr])
                load_w("pool", wkr[:, :, 64:96], Wv[:, :, 640:672], Bwkr)
                for tb in range(4):
                    for (wt, Bwt, nch, dst, Bdst, gc, nn) in ((wcq, Bwcq, 3, cqn, Bcqn, CC + 8, 384.0),
                                                              (wckv, Bwckv, 2, ckvn, Bckvn, CC + 11, 256.0)):
                        banks = [PS.get() for _ in range(nch)]
                        for ch in range(nch):
                            for kc in range(8):
                                mm(banks[ch][0][:], wt[:, kc, ch * 128:(ch + 1) * 128], xnT[:, kc, tsl(tb)], kc == 0, kc == 7,
                                   [Bwt, BxnT[tb]], [banks[ch][1]])
                        ssq, Bssq = PO.get()
                        for ch in range(nch):
                            sq, Bsq = bfr.get()
                            act(sq[:], banks[ch][0][:], AF.Square, [banks[ch][1]], [Bsq])
                            mm(ssq[:], mat(M_ONE), sq[:], ch == 0, ch == nch - 1, [Bsq, Bcmat], [Bssq], inc=True)
                        rs, Br = rstd_from_ssq(ssq[:], Bssq, 128, nn, 512, f32r)
                        for ch in range(nch):
                            stt("dve", dst[:, ch, tsl(tb)], banks[ch][0][:], col(gc + ch), rs[:], ALU.mult, ALU.mult,
                                [banks[ch][1], Br, Bcols], [Bdst[tb]])
                    pb, Bpb = PS.get()
                    for kc in range(8):
                        mm(pb[0:96, :], wkr[:, kc, :], xnT[:, kc, tsl(tb)], kc == 0, kc == 7, [Bwkr, BxnT[tb]], [Bpb])
                    act(krT[64:96, tsl(tb)], pb[64:96, :], AF.Copy, [Bpb], [BkrT[tb]])
                barrier()
            wzr = Ring(nc, es, "c_wz", 2, [128, 8, 128], BF16)
            wor = Ring(nc, es, "c_wo", 2, [128, 1024], BF16)
            wuqr = Ring(nc, es, "c_wuq", 2, [128, 3, 96], BF16)
            wukvr = Ring(nc, es, "c_wukv", 2, [128, 2, 128], BF16)
            qr = Ring(nc, es, "c_q", 2, [128, SEQ], BF16)
            kr_ = Ring(nc, es, "c_k", 2, [128, SEQ], BF16)
            vr = Ring(nc, es, "c_v", 2, [128, 16, 128], BF16)
            szr = Ring(nc, es, "c_sz", 1, [128, SEQ], BF16)
            gr = Ring(nc, es, "c_g", 1, [128, SEQ], BF16)
            ptr = Ring(nc, es, "c_pt", 4, [128, 512], BF16)
            for sl_, (vt, Bv) in enumerate(vr.items):
                lo = 64 if sl_ == 0 else 0
                S.op("pool", lambda: nc.gpsimd.memset(vt[:, :, lo:lo + 64], 1.0), writes=[Bv])
            Wz = W.rearrange("(kc p) n -> p kc n", p=128)
            Wuq = c_w_uq[0].rearrange("(kc p) n -> p kc n", p=128)
            Wukv = c_w_ukv[0].rearrange("(kc p) n -> p kc n", p=128)
            Wo = c_w_out[0]
            scale = 96.0 ** -0.5
            C_ST = subring([0, 1, 2, 3])
            C_O = subring([4, 5])
            C_PJ = subring([4, 5, 6, 7])
            hds = {}

            def head_proj_g(h):
                hh = h % 2
                wuq, Bwuq = wuqr.get()
                load_w("pool", wuq[:], Wuq[:, :, h * 96:(h + 1) * 96], Bwuq)
                wukv, Bwukv = wukvr.get()
                load_w("pool", wukv[:], Wukv[:, :, h * 128:(h + 1) * 128], Bwukv)
                qT, Bq = qr.get()
                kT, Bk = kr_.get()
                vt, Bv = vr.items[hh]
                vlo = 0 if hh == 0 else 64
                hds[h] = dict(k=(kT[0:96, :], Bk), q=(qT[0:96, :], Bq), v=(lambda i, vt=vt: vt[:, i, :]), Bv=Bv,
                              osl=slice(vlo, vlo + 64), lsl=slice(64 - vlo, 128 - vlo))
                gens = []
                for tb in range(4):
                    def srcq(tb=tb):
                        pq, Bpq = C_PJ.get()
                        for kc in range(3):
                            mm(pq[0:96, :], wuq[:, kc, :], cqn[:, kc, tsl(tb)], kc == 0, kc == 2, [Bwuq, Bcqn[tb]], [Bpq])
                        return pq[0:96, :], Bpq
                    gens.append(norm_rope_g(srcq, 96, mat(M_ONE, 96, 96), 96, cols[0:96, CC + 13:CC + 14], mat(M_RMLA, 96, 96),
                                            Ct, St, tb, qT[0:96, tsl(tb)], Bq, f32r, bfr))

                    def srck(tb=tb):
                        pk, Bpk = C_PJ.get()
                        for kc in range(2):
                            mm(pk[0:64, :], wukv[:, kc, 0:64], ckvn[:, kc, tsl(tb)], kc == 0, kc == 1, [Bwukv, Bckvn[tb]], [Bpk])
                        act(pk[64:96, :], krT[64:96, tsl(tb)], AF.Copy, [BkrT[tb]], [Bpk])
                        return pk[0:96, :], Bpk
                    gens.append(norm_rope_g(srck, 96, mat(M_ONE, 96, 96), 96, cols[0:96, CC + 14:CC + 15], mat(M_RMLA, 96, 96),
                                            Ct, St, tb, kT[0:96, tsl(tb)], Bk, f32r, bfr))
                yield from run_pipe_g(gens, C_DEPTH)
                for g in range(2):
                    pv, Bpv = C_PJ.get()
                    for t8 in range(8):
                        tile_ = g * 8 + t8
                        for kc in range(2):
                            mm(pv[:, t8 * 64:(t8 + 1) * 64], ckvn[:, kc, tile_ * 128:(tile_ + 1) * 128], wukv[:, kc, 64:128],
                               kc == 0, kc == 1, [Bwukv, Bckvn[tile_ // 4]], [Bpv])
                    act(vt[:, g * 8:(g + 1) * 8, vlo:vlo + 64], pv[:].rearrange("p (a b) -> p a b", b=64), AF.Copy, [Bpv], [Bv])
                    yield

            PROJ[0] = C_PJ
            T1ENG[0] = "pool"
            for _ in head_proj_g(0):
                pass
            wo = Bwo = sz = Bsz = gT = BgT = None
            C_WO = subring([0, 1, 2])
            for h in range(16):
                c, hh = divmod(h, 2)
                if hh == 0:
                    wz, Bwz = wzr.get()
                    load_w("pool", wz[:], Wz[:, :, 672 + c * 128:672 + (c + 1) * 128], Bwz)
                    wo, Bwo = wor.get()
                    load_w("pool", wo[:], Wo[c * 128:(c + 1) * 128, :], Bwo)
                    sz, Bsz = szr.get()
                    for tb in range(4):
                        pb, Bpb = C_PJ.get()
                        for kc in range(8):
                            mm(pb[:], wz[:, kc, :], xnT[:, kc, tsl(tb)], kc == 0, kc == 7, [Bwz, BxnT[tb]], [Bpb])
                        act(sz[:, tsl(tb)], pb[:], AF.Silu, [Bpb], [Bsz])
                    gT, BgT = gr.get()
                main = attention_g([hds[h]], scale, None, gT, BgT, sz, Bsz, f32r, ptr, st_ring=C_ST, o_ring=C_O, recip_dve=True)
                for _ in main:
                    pass
                side = head_proj_g(h + 1) if h + 1 < 16 else None
                if hh == 1:
                    co_run(wout_g(wo, Bwo, gT, BgT, C_WO), side, 2)
                elif side is not None:
                    for _ in side:
                        pass
            PROJ[0] = PA
            T1ENG[0] = "dve"
            barrier()


    def layer_B():
        Wv = b_w_in[0].rearrange("(kc p) n -> p kc n", p=128)
        MOFF = [sum(2048 - 128 * ii for ii in range(i)) for i in range(16)]
        with ExitStack() as es:
            f32r = Ring(nc, es, "bf32_", 4, [128, 512], F32)
            bfr = Ring(nc, es, "bbf_", 6, [128, 512], BF16)
            emit_norm(CB, f32r, bfr)
            maskT = sb("b_maskT", [128, 17408], BF16, es)
            BmaskT = [Buf(f"maskT{i}") for i in range(16)]
            with ExitStack() as es2:
                qiT = sb("b_qiT", [128, 4, SEQ], BF16, es2)
                BqiT = [Buf(f"qiT{t}") for t in range(4)]
                kiT2 = sb("b_kiT2", [128, 2, SEQ], BF16, es2)
                BkiT = [Buf(f"kiT{t}") for t in range(4)]
                S.op("pool", lambda: nc.gpsimd.memset(kiT2[64:128, 0, :], 0.0), writes=BkiT)
                S.op("pool", lambda: nc.gpsimd.memset(kiT2[0:64, 1, :], 0.0), writes=BkiT)
                wi_sb = sb("b_wi", [128, 16, 8], F32, es2)
                Bwi = Buf("wi")
                with ExitStack() as es_w:
                    with ExitStack() as es_tmp:
                        Ct, St = make_tables(es_w, es_tmp, CK + 1, "bi")
                    wqi = sb("b_wqi", [128, 8, 512], BF16, es_w)
                    Bwqi = Buf("wqi")
                    wki2 = sb("b_wki2", [128, 8, 128], BF16, es_w)
                    Bwki2 = Buf("wki2")
                    wwi = sb("b_wwi", [128, 8, 8], BF16, es_w)
                    Bwwi = Buf("wwi")
                    load_w("pool", wqi[:], Wv[:, :, 4096:4608], Bwqi)
                    load_w("pool", wki2[:, :, 0:64], Wv[:, :, 4608:4672], Bwki2)
                    load_w("pool", wki2[:, :, 64:128], Wv[:, :, 4608:4672], Bwki2)
                    load_w("pool", wwi[:], Wv[:, :, 4672:4680], Bwwi)
                    gens = []
                    for tb in range(4):
                        for ch in range(4):
                            def srcqi(tb=tb, ch=ch):
                                pb, Bpb = PA.get()
                                for kc in range(8):
                                    mm(pb[:], wqi[:, kc, ch * 128:(ch + 1) * 128], xnT[:, kc, tsl(tb)], kc == 0, kc == 7,
                                       [Bwqi, BxnT[tb]], [Bpb])
                                return pb[:], Bpb
                            gens.append(norm_rope_g(srcqi, 128, None, None, None, mat(M_RIDX), Ct, St, tb, qiT[:, ch, tsl(tb)], BqiT[tb], f32r, bfr))

                        def srcki(tb=tb):
                            pb, Bpb = PA.get()
                            for kc in range(8):
                                mm(pb[:], wki2[:, kc, :], xnT[:, kc, tsl(tb)], kc == 0, kc == 7, [Bwki2, BxnT[tb]], [Bpb])
                            return pb[:], Bpb
                        gens.append(norm_rope_g(srcki, 128, None, None, None, mat(M_RIDX), Ct, St, tb, kiT2[0:64, 0, tsl(tb)], BkiT[tb], f32r, bfr,
                                                out_hi=kiT2[64:128, 1, tsl(tb)], Bout_hi=BkiT[tb]))
                    run_pipe(gens, PIPE_DEPTH)
                    for tb in range(4):
                        pw, Bpw = PO.get()
                        for t4 in range(4):
                            tile_ = tb * 4 + t4
                            for kc in range(8):
                                mm(pw[:, t4 * 8:(t4 + 1) * 8], xnT[:, kc, tile_ * 128:(tile_ + 1) * 128], wwi[:, kc, :], kc == 0, kc == 7,
                                   [Bwwi, BxnT[tb]], [Bpw])
                        ts("dve", wi_sb[:, tb * 4:(tb + 1) * 4, :], pw[:, 0:32].rearrange("p (a b) -> p a b", b=8),
                           (8.0 ** -0.5) * (64.0 ** -0.5), None, ALU.mult, None, [Bpw], [Bwi])
                    barrier()
                scr = Ring(nc, es2, "b_sc", 2, [128, SEQ], F32)
                junkr = Ring(nc, es2, "b_junk", 2, [128, SEQ], BF16)
                mskr = Ring(nc, es2, "b_msk", 1, [128, SEQ], BF16)
                dgr = Ring(nc, es2, "b_dg", 2, [128, 8, 128], BF16)
                str_ = Ring(nc, es2, "b_st", 2, [128, 8], F32)
                dlr = Ring(nc, es2, "b_dl", 2, [128, NBIS + 1], F32)
                ev = [0]

                def score_tile(t):
                    svis = 128 * (t + 1)
                    sc, Bsc = scr.get()
                    dg, Bdg = dgr.get()
                    for h in range(8):
                        ts("dve", dg[:, h, :], mat(M_ID), wi_sb[:, t, h:h + 1], None, ALU.mult, None, [Bwi, Bcmat], [Bdg])
                    nsb = (svis + 511) // 512
                    units = [(sbk, h) for sbk in range(nsb) for h in range(8)]

                    def issue_rp(sbk, h):
                        w = min(512, svis - 512 * sbk)
                        rp, Brp = PS.get()
                        mm(rp[:, 0:w], qiT[:, h // 2, t * 128:(t + 1) * 128], kiT2[:, h % 2, sbk * 512:sbk * 512 + w],
                           True, True, [BqiT[t // 4], BkiT[sbk]], [Brp])
                        return rp, Brp, w
                    LOOK = 3
                    pend = [issue_rp(*u) for u in units[:LOOK]]
                    sp = Bsp = None
                    for idx, (sbk, h) in enumerate(units):
                        rp, Brp, w = pend.pop(0)
                        if idx + LOOK < len(units):
                            pend.append(issue_rp(*units[idx + LOOK]))
                        if h == 0:
                            sp, Bsp = PO.get()
                        rl, Brl = bfr.get()
                        if h % 2 == 0:
                            act(rl[:, 0:w], rp[:, 0:w], AF.Relu, [Brp], [Brl])
                        else:
                            ts("dve", rl[:, 0:w], rp[:, 0:w], 0.0, None, ALU.max, None, [Brp], [Brl])
                        mm(sp[:, 0:w], dg[:, h, :], rl[:, 0:w], h == 0, h == 7, [Bdg, Brl], [Bsp], inc=True)
                        if h == 7:
                            act(sc[:, sbk * 512:sbk * 512 + w], sp[:, 0:w], AF.Copy, [Bsp], [Bsc])
                    st, Bst = str_.get()
                    dl, Bdl = dlr.get()
                    if t >= 2:
                        S.op("dve", lambda: nc.vector.tensor_reduce(out=st[:, 0:1], in_=sc[:, 0:svis], axis=AX.X, op=ALU.max),
                             reads=[Bsc], writes=[Bst])
                        S.op("dve", lambda: nc.vector.tensor_reduce(out=st[:, 1:2], in_=sc[:, 0:svis], axis=AX.X, op=ALU.min),
                             reads=[Bsc], writes=[Bst])
                    S.op("pool", lambda: nc.gpsimd.memset(sc[0:64, svis - 64:svis], NEG), writes=[Bsc])
                    junk, Bjunk = junkr.get()
                    return dict(t=t, svis=svis, sc=sc, Bsc=Bsc, st=st, Bst=Bst, dl=dl, Bdl=Bdl, junk=junk, Bjunk=Bjunk)

                def bisect(tiles):
                    tiles = [T for T in tiles if T["t"] >= 2]
                    for k_, T in enumerate(tiles):
                        T["neg"] = (k_ % 2 == 1)
                        st, Bst, dl, Bdl = T["st"], T["Bst"], T["dl"], T["Bdl"]
                        tt("dve", st[:, 2:3], st[:, 0:1], st[:, 1:2], ALU.subtract, [Bst], [Bst])
                        ts("dve", st[:, 2:3], st[:, 2:3], 1.001, 1e-6, ALU.mult, ALU.add, [Bst], [Bst])
                        if not T["neg"]:
                            ts("dve", dl[:], cols[:, CK + 3:CK + 4 + NBIS], st[:, 2:3], None, ALU.mult, None, [Bst, Bcols], [Bdl])
                            tt("dve", st[:, 3:4], st[:, 0:1], st[:, 2:3], ALU.subtract, [Bst], [Bst])
                            tt("dve", st[:, 3:4], st[:, 3:4], dl[:, 0:1], ALU.add, [Bst, Bdl], [Bst])
                        else:
                            ts("dve", st[:, 7:8], st[:, 2:3], -1.0, None, ALU.mult, None, [Bst], [Bst])
                            ts("dve", dl[:], cols[:, CK + 3:CK + 4 + NBIS], st[:, 7:8], None, ALU.mult, None, [Bst, Bcols], [Bdl])
                            tt("dve", st[:, 3:4], st[:, 2:3], st[:, 0:1], ALU.subtract, [Bst], [Bst])
                            tt("dve", st[:, 3:4], st[:, 3:4], dl[:, 0:1], ALU.add, [Bst, Bdl], [Bst])
                    for i in range(NBIS):
                        for T in tiles:
                            st, Bst, dl, Bdl = T["st"], T["Bst"], T["dl"], T["Bdl"]
                            if not T["neg"]:
                                ts("dve", T["junk"][:, 0:T["svis"]], T["sc"][:, 0:T["svis"]], st[:, 3:4], 0.0, ALU.is_ge, ALU.add,
                                   [T["Bsc"], Bst], [T["Bjunk"], Bst], accum_out=st[:, 4:5])
                            else:
                                act(T["junk"][:, 0:T["svis"]], T["sc"][:, 0:T["svis"]], AF.Sign, [T["Bsc"], Bst], [T["Bjunk"], Bst],
                                    bias=st[:, 3:4], scale=1.0, accum_out=st[:, 4:5])
                        for T in tiles:
                            st, Bst = T["st"], T["Bst"]
                            thr_cnt = float(TOPK) if not T["neg"] else float(2 * TOPK - T["svis"])
                            ts("dve", st[:, 5:6], st[:, 4:5], thr_cnt, 0.5, ALU.is_ge, ALU.subtract, [Bst], [Bst])
                        for T in tiles:
                            st, Bst, dl, Bdl = T["st"], T["Bst"], T["dl"], T["Bdl"]
                            stt("dve", st[:, 3:4], st[:, 5:6], dl[:, i:i + 1], st[:, 3:4], ALU.mult, ALU.add, [Bst, Bdl], [Bst])
                    for T in tiles:
                        st, Bst, dl, Bdl = T["st"], T["Bst"], T["dl"], T["Bdl"]
                        if not T["neg"]:
                            tt("dve", st[:, 6:7], st[:, 3:4], dl[:, NBIS:NBIS + 1], ALU.subtract, [Bst, Bdl], [Bst])
                        else:
                            stt("dve", st[:, 6:7], st[:, 3:4], -1.0, dl[:, NBIS:NBIS + 1], ALU.mult, ALU.add, [Bst, Bdl], [Bst])

                def mask_tile(T):
                    t, svis, sc, Bsc, st, Bst = T["t"], T["svis"], T["sc"], T["Bsc"], T["st"], T["Bst"]
                    if t < 2:
                        S.op("pool", lambda: nc.gpsimd.memset(st[:, 6:7], NEG / 2), writes=[Bst])
                    mk, Bmk = mskr.get()
                    ts("dve", mk[:, 0:svis], sc[:, 0:svis], st[:, 6:7], None, ALU.is_ge, None, [Bsc, Bst], [Bmk])
                    for i0 in range(0, t + 1, 4):
                        nb = min(4, t + 1 - i0)
                        tp, Btp = PS.get()
                        for bi in range(nb):
                            i = i0 + bi
                            mm(tp[:, bi * 128:(bi + 1) * 128], mk[:, i * 128:(i + 1) * 128], mat(M_ID), True, True, [Bmk, Bcmat], [Btp])
                        for bi in range(nb):
                            i = i0 + bi
                            dst = maskT[:, MOFF[i] + 128 * (t - i):MOFF[i] + 128 * (t - i) + 128]
                            act(dst, tp[:, bi * 128:(bi + 1) * 128], AF.Copy, [Btp], [BmaskT[i]])

                for tp_ in range(8 if STOP_AT >= 1.2 else 0):
                    Ts = [score_tile(2 * tp_), score_tile(2 * tp_ + 1)]
                    if STOP_AT >= 1.5:
                        bisect(Ts)
                    if STOP_AT >= 1.8:
                        for T in Ts:
                            mask_tile(T)
                barrier()
            with ExitStack() as es_tmp:
                Ct, St = make_tables(es, es_tmp, CK + 0, "b6")
            wr = Ring(nc, es, "b_w", 5, [128, 8, 128], BF16)
            wor = Ring(nc, es, "b_wo", 2, [128, 1024], BF16)
            qT = sb("b_qT", [128, SEQ], BF16, es)
            Bq = Buf("b_qT")
            kT = sb("b_kT", [128, 2, SEQ], BF16, es)
            Bk = Buf("b_kT")
            S.op("pool", lambda: nc.gpsimd.memset(kT[64:128, 0, :], 0.0), writes=[Bk])
            S.op("pool", lambda: nc.gpsimd.memset(kT[0:64, 1, :], 0.0), writes=[Bk])
            va = sb("b_va", [128, 16, 2, 128], BF16, es)
            Bva = Buf("b_va")
            sz = sb("b_sz", [128, SEQ], BF16, es)
            Bsz = Buf("b_sz")
            gT = sb("b_gT", [128, SEQ], BF16, es)
            BgT = Buf("b_gT")
            S.op("pool", lambda: nc.gpsimd.memset(va[:, :, 0, 64:128], 1.0), writes=[Bva])
            S.op("pool", lambda: nc.gpsimd.memset(va[:, :, 1, 0:64], 1.0), writes=[Bva])
            Wo = b_w_out[0]

            def mask_fn(i, q0, N):
                o = MOFF[i] + (q0 - 128 * i)
                return maskT[:, o:o + N], BmaskT[i]

            B_NR = subring([0, 1, 2, 3])
            B_ZV = subring([4])
            B_WO = subring([5, 6, 7])
            chunk_state = {}

            def proj_chunk_g(c):
                ws = []
                for g in range(4):
                    w_, Bw_ = wr.get()
                    load_w("pool", w_[:], Wv[:, :, g * 1024 + c * 128:g * 1024 + (c + 1) * 128], Bw_)
                    ws.append((w_, Bw_))
                wo, Bwo = wor.get()
                load_w("pool", wo[:], Wo[c * 128:(c + 1) * 128, :], Bwo)
                chunk_state[c] = (wo, Bwo)
                (wq, Bwq), (wk, Bwk), (wv_, Bwv), (wz, Bwz) = ws
                gens = []
                for tb in range(4):
                    for (w_, Bw_, gcl, isk) in ((wq, Bwq, CB + 8, False), (wk, Bwk, CB + 9, True)):
                        def srcp(tb=tb, w_=w_, Bw_=Bw_):
                            pb, Bpb = B_NR.get()
                            for kc in range(8):
                                mm(pb[:], w_[:, kc, :], xnT[:, kc, tsl(tb)], kc == 0, kc == 7, [Bw_, BxnT[tb]], [Bpb])
                            return pb[:], Bpb
                        if isk:
                            gens.append(norm_rope_g(srcp, 128, mat(M_B64), 64, col(gcl), mat(M_R64), Ct, St, tb, kT[0:64, 0, tsl(tb)], Bk, f32r, bfr,
                                                    out_hi=kT[64:128, 1, tsl(tb)], Bout_hi=Bk))
                        else:
                            gens.append(norm_rope_g(srcp, 128, mat(M_B64), 64, col(gcl), mat(M_R64), Ct, St, tb, qT[:, tsl(tb)], Bq, f32r, bfr))
                PROJ[0] = B_NR
                yield from run_pipe_g(gens, PIPE_DEPTH)
                PROJ[0] = PA
                for tb in range(4):
                    pb, Bpb = B_ZV.get()
                    for kc in range(8):
                        mm(pb[:], wz[:, kc, :], xnT[:, kc, tsl(tb)], kc == 0, kc == 7, [Bwz, BxnT[tb]], [Bpb])
                    act(sz[:, tsl(tb)], pb[:], AF.Silu, [Bpb], [Bsz])
                    yield
                    pv, Bpv = B_ZV.get()
                    for t4 in range(4):
                        tile_ = tb * 4 + t4
                        for kc in range(8):
                            mm(pv[:, t4 * 128:(t4 + 1) * 128], xnT[:, kc, tile_ * 128:(tile_ + 1) * 128], wv_[:, kc, :], kc == 0, kc == 7,
                               [Bwv, BxnT[tb]], [Bpv])
                    pv3 = pv[:].rearrange("p (a b) -> p a b", b=128)
                    act(va[:, tb * 4:(tb + 1) * 4, 0, 0:64], pv3[:, :, 0:64], AF.Copy, [Bpv], [Bva])
                    act(va[:, tb * 4:(tb + 1) * 4, 1, 64:128], pv3[:, :, 64:128], AF.Copy, [Bpv], [Bva])
                    yield

            heads = []
            for hh in range(2):
                lo = hh * 64
                heads.append(dict(k=(kT[:, hh, :], Bk), q=(qT[:, :], Bq),
                                  v=(lambda i, hh=hh: va[:, i, hh, :]), Bv=Bva,
                                  osl=slice(lo, lo + 64), lsl=slice(64 - lo, 128 - lo)))
            NCH = 8 if STOP_AT >= 3 else 0
            if NCH:
                for _ in proj_chunk_g(0):
                    pass
            for c in range(NCH):
                wo, Bwo = chunk_state[c]
                attention(heads, 0.125, mask_fn, gT, BgT, sz, Bsz, f32r, bfr, heat=HEAT)
                side = proj_chunk_g(c + 1) if c + 1 < NCH else None
                co_run(wout_g(wo, Bwo, gT, BgT, B_WO), side, 2)
            barrier()

    for L in layers:
        if L[0] == "A":
            layer_A(int(L[1]))
        elif L[0] == "C":
            layer_C()
        elif L[0] == "B":
            layer_B()
        else:
            raise NotImplementedError(L)

    Bout = Buf("out")
    for kc in range(8):
        S.dma("sp", yT_d[kc * 128:(kc + 1) * 128, :], xT[:, kc, :], reads=BxT[kc], own=Bout)
    S.finish([Bout])
    ges.close()
    return nc


def _pack_cols(inp):
    c = np.zeros((128, NCOLS), np.float32)

    def colmajor(v):
        return np.ascontiguousarray(v.reshape(-1, 128).T)
    for j in range(2):
        b = CA[j]
        c[:, b:b + 8] = colmajor(inp["a_norm"][j])
        for k in range(3):
            c[:, b + 8 + 8 * k:b + 16 + 8 * k] = colmajor(inp["a_conv_w"][j, k])
        c[:, b + 32:b + 40] = colmajor(inp["a_conv_b"][j])
    c[:, CB:CB + 8] = colmajor(inp["b_norm"][0])
    c[:, CB + 8] = np.tile(inp["b_q_norm"][0], 2)
    c[:, CB + 9] = np.tile(inp["b_k_norm"][0], 2)
    c[:, CC:CC + 8] = colmajor(inp["c_norm"][0])
    c[:, CC + 8:CC + 11] = colmajor(inp["c_q_lat_norm"][0])
    c[:, CC + 11:CC + 13] = colmajor(inp["c_kv_lat_norm"][0])
    c[:96, CC + 13] = inp["c_q_norm"][0]
    c[:96, CC + 14] = inp["c_k_norm"][0]
    p = np.arange(128)
    theta = 10000.0
    c[:, CK + 0] = (theta ** (-(np.arange(32, dtype=np.float32)) / 32.0)).astype(np.float32)[p % 32]
    inv16 = (theta ** (-(np.arange(16, dtype=np.float32)) / 16.0)).astype(np.float32)
    c[:, CK + 1] = np.where((p % 64) < 32, inv16[p % 16], 0.0)
    c[:, CK + 2] = np.where((p >= 64) & (p < 96), inv16[(p - 64) % 16], 0.0)
    for i in range(NBIS + 1):
        c[:, CK + 3 + i] = 2.0 ** -(i + 1)
    return c


def _const_mats():
    m = np.zeros((128, NMAT, 128), np.float32)
    m[:, M_ID, :] = np.eye(128)
    m[:, M_ONE, :] = 1.0
    m[0:64, M_B64, 0:64] = 1.0
    m[64:128, M_B64, 64:128] = 1.0
    for o in range(128):
        dd = o % 64
        if dd < 32:
            m[o + 32, M_R64, o] = -1.0
        else:
            m[o - 32, M_R64, o] = 1.0
        if dd < 16:
            m[o + 16, M_RIDX, o] = -1.0
        elif dd < 32:
            m[o - 16, M_RIDX, o] = 1.0
        if 64 <= o < 80:
            m[o + 16, M_RMLA, o] = -1.0
        elif 80 <= o < 96:
            m[o - 16, M_RMLA, o] = 1.0
    return m.reshape(128, NMAT * 128)


LAYERS = ["A0", "B0", "C0", "A1"]
_NC_CACHE = {}


def kernel(**inp):
    inp = {k: np.asarray(v) for k, v in inp.items()}
    key = tuple(LAYERS)
    if key not in _NC_CACHE:
        _NC_CACHE[key] = build(LAYERS)
    nc = _NC_CACHE[key]
    cols = _pack_cols(inp)
    cmat = _const_mats()
    x = inp["x"]
    in_maps = []
    for b in range(8):
        m = {
            "xT": np.ascontiguousarray(x[b].T),
            "pos": np.ascontiguousarray(inp["positions"][b:b + 1].astype(np.int32)),
            "cols": cols, "cmat": cmat,
        }
        for k in ("a_w_in", "a_w_out", "b_w_in", "b_w_out", "c_w_in", "c_w_uq", "c_w_ukv", "c_w_out"):
            m[k] = np.ascontiguousarray(inp[k], dtype=np.float32)
        in_maps.append(m)
    res = run_bass_kernel_spmd(nc, in_maps, core_ids=list(range(8)))
    out = np.stack([np.ascontiguousarray(res.results[b]["yT"].T) for b in range(8)], axis=0)
    return out.astype(np.float32)
```
